# Optimizing a Trainium2 kernel written in Bass

```python
import math
import functools
import jax
import jax.numpy as jnp
from jax import lax
import numpy as np

D_MODEL = 2048
BATCH = 16
SEQ = 256
DEPTH = 4
DEC_BATCH = 4
DEC_SEQ = 1024
PAST_LEN = 256

GRID_W = 64
BRANCH_WIDTH = 512
N_BRANCH = 3
NA_HEADS = 4
NA_HEAD_DIM = 128
NA_WIDTH = NA_HEADS * NA_HEAD_DIM
NA_WIN_ROWS = 8
NA_WIN_COLS = 16
NA_QCOL_BLOCK = 16
NA_KCOL_BLOCK = NA_QCOL_BLOCK + NA_WIN_COLS
NA_REL_ROWS = 2 * NA_WIN_ROWS - 1
NA_REL_COLS = 2 * NA_WIN_COLS - 1
DIFF_HEADS = 4
DIFF_QK_DIM = 64
DIFF_V_DIM = 128
DIFF_QK_WIDTH = DIFF_HEADS * 2 * DIFF_QK_DIM
DIFF_V_WIDTH = DIFF_HEADS * DIFF_V_DIM
RWKV_HEADS = 8
RWKV_HEAD_DIM = 64
RWKV_WIDTH = RWKV_HEADS * RWKV_HEAD_DIM
RWKV_DECAY_RANK = 64
RWKV_ICL_RANK = 64
RWKV_GATE_RANK = 128
RWKV_SIZES = (RWKV_WIDTH, RWKV_WIDTH, RWKV_WIDTH, RWKV_DECAY_RANK, RWKV_DECAY_RANK, RWKV_ICL_RANK, RWKV_ICL_RANK, RWKV_GATE_RANK)
RWKV_FEAT = 3 * RWKV_WIDTH + 2 * RWKV_DECAY_RANK + 2 * RWKV_ICL_RANK + RWKV_GATE_RANK
RWKV_GN_EPS = 64e-5
IN_SIZES = (NA_WIDTH, NA_WIDTH, NA_WIDTH, DIFF_QK_WIDTH, DIFF_QK_WIDTH, DIFF_V_WIDTH, RWKV_FEAT, N_BRANCH * D_MODEL)
N_IN = 3 * NA_WIDTH + 2 * DIFF_QK_WIDTH + DIFF_V_WIDTH + RWKV_FEAT + N_BRANCH * D_MODEL
FFN_HIDDEN = -(-8 * D_MODEL // (3 * 256)) * 256
ROPE_THETA = 10000.0
Q_BLOCK = 128
LN_EPS = 1e-5

kernel_name = 'hybrid_diffusion_na_diff_rwkv7_step'


def _split(x, sizes):
    return jnp.split(x, np.cumsum(sizes)[:-1].tolist(), axis=-1)


def _normalize(x):
    xf = x.astype(jnp.float32)
    mu = jnp.mean(xf, axis=-1, keepdims=True)
    var = jnp.mean(jnp.square(xf - mu), axis=-1, keepdims=True)
    return (xf - mu) * lax.rsqrt(var + LN_EPS)


def _layer_norm(x, g, b):
    return (_normalize(x) * g + b).astype(x.dtype)


def _modulate(x, shift, scale):
    return (_normalize(x) * (1.0 + scale) + shift).astype(x.dtype)


def _adaln(cond, w_ada_l, b_ada_l):
    m = jax.nn.silu(cond) @ w_ada_l + b_ada_l
    m = m.reshape(m.shape[0], 1, 6, D_MODEL)
    return tuple(m[:, :, i] for i in range(6))


def _swiglu(h, w_in, w_out):
    gate, up = jnp.split(h @ w_in, 2, axis=-1)
    return (jax.nn.silu(gate) * up) @ w_out


def _dense_attend(q, k, v):
    b, t, h, d = q.shape
    qb = q.reshape(b, t // Q_BLOCK, Q_BLOCK, h, d).swapaxes(0, 1)
    def block(qi):
        s = jnp.einsum('bqhd,bkhd->bhqk', qi, k, preferred_element_type=jnp.float32) * (d ** -0.5)
        p = jax.nn.softmax(s, axis=-1).astype(v.dtype)
        return jnp.einsum('bhqk,bkhe->bqhe', p, v)
    o = lax.map(block, qb)
    return o.swapaxes(0, 1).reshape(b, t, h, v.shape[-1])


def _diff_attend(q, k, v, lam):
    b, t, h, _, d = q.shape
    qb = q.reshape(b, t // Q_BLOCK, Q_BLOCK, h, 2, d).swapaxes(0, 1)
    def block(qi):
        s = jnp.einsum('bqhcd,bkhcd->bhcqk', qi, k, preferred_element_type=jnp.float32) * (d ** -0.5)
        p = jax.nn.softmax(s, axis=-1)
        w = (p[:, :, 0] - lam * p[:, :, 1]).astype(v.dtype)
        return jnp.einsum('bhqk,bkhe->bqhe', w, v)
    o = lax.map(block, qb)
    return o.swapaxes(0, 1).reshape(b, t, h, v.shape[-1])


def _diff_lambda(lam_p, lam_init):
    lp = lam_p.astype(jnp.float32)
    return jnp.exp(jnp.sum(lp[0] * lp[1])) - jnp.exp(jnp.sum(lp[2] * lp[3])) + lam_init


def _diff_post(o, g, lam_init):
    b, t, h, e = o.shape
    of = o.astype(jnp.float32)
    of = of * lax.rsqrt(jnp.mean(jnp.square(of), axis=-1, keepdims=True) + LN_EPS)
    return (of.reshape(b, t, h * e) * g * (1.0 - lam_init)).astype(o.dtype)


def _axial_rope_tables(n_tokens):
    t = jnp.arange(n_tokens)
    rows = (t // GRID_W).astype(jnp.float32)
    cols = (t % GRID_W).astype(jnp.float32)
    half = DIFF_QK_DIM // 2
    inv = ROPE_THETA ** (-jnp.arange(0, half, 2, dtype=jnp.float32) / half)
    ang_r = rows[:, None] * inv
    ang_c = cols[:, None] * inv
    ang = jnp.concatenate([ang_r, ang_r, ang_c, ang_c], axis=-1)
    return jnp.cos(ang), jnp.sin(ang)


def _rotate_half(x):
    x1, x2 = jnp.split(x, 2, axis=-1)
    return jnp.concatenate([-x2, x1], axis=-1)


def _apply_axial_rope(x, cos, sin):
    half = x.shape[-1] // 2
    xf = x.astype(jnp.float32)
    rot = jnp.concatenate([_rotate_half(xf[..., :half]), _rotate_half(xf[..., half:])], axis=-1)
    return (xf * cos[:, None, None, :] + rot * sin[:, None, None, :]).astype(x.dtype)


def _na_static_indices(rows):
    win_r = min(NA_WIN_ROWS, rows)
    r = np.arange(rows)
    row_start = np.clip(r - win_r // 2, 0, rows - win_r)
    row_idx = row_start[:, None] + np.arange(win_r)[None, :]
    rel_r = row_idx - r[:, None]
    n_cb = GRID_W // NA_QCOL_BLOCK
    j = np.arange(n_cb)
    kcol_start = np.clip(j * NA_QCOL_BLOCK - NA_WIN_COLS // 2, 0, GRID_W - NA_KCOL_BLOCK)
    col_idx = kcol_start[:, None] + np.arange(NA_KCOL_BLOCK)[None, :]
    qcol = j[:, None] * NA_QCOL_BLOCK + np.arange(NA_QCOL_BLOCK)[None, :]
    win_c0 = np.clip(qcol - NA_WIN_COLS // 2, 0, GRID_W - NA_WIN_COLS)
    kc = col_idx[:, None, :]
    in_win = (kc >= win_c0[:, :, None]) & (kc < win_c0[:, :, None] + NA_WIN_COLS)
    rel_c = np.clip(kc - qcol[:, :, None], -(NA_WIN_COLS - 1), NA_WIN_COLS - 1)
    return row_idx, rel_r, col_idx, rel_c, in_win


def _na_latent(q, k, v, k_ctx, v_ctx, rpb):
    b, s, h, d = q.shape
    rows = s // GRID_W
    row_idx, rel_r, col_idx, rel_c, in_win = _na_static_indices(rows)
    wr = row_idx.shape[1]
    n_cb = GRID_W // NA_QCOL_BLOCK
    ri = row_idx[:, None, :, None]
    ci = col_idx[None, :, None, :]
    kb = k.reshape(b, rows, GRID_W, h, d)[:, ri, ci]
    vb = v.reshape(b, rows, GRID_W, h, d)[:, ri, ci]
    qb = q.reshape(b, rows, n_cb, NA_QCOL_BLOCK, h, d)
    scale = d ** -0.5
    bias = rpb.astype(jnp.float32)[:, (rel_r + NA_WIN_ROWS - 1)[:, None, None, :, None], (rel_c + NA_WIN_COLS - 1)[None, :, :, None, :]]
    s_loc = jnp.einsum('brjqhd,brjikhd->bhrjqik', qb, kb, preferred_element_type=jnp.float32) * scale + bias[None]
    s_loc = jnp.where(in_win[None, None, None, :, :, None, :], s_loc, -jnp.inf)
    n_loc = wr * NA_KCOL_BLOCK
    s_loc = s_loc.reshape(b, h, rows, n_cb, NA_QCOL_BLOCK, n_loc)
    s_ctx = jnp.einsum('brjqhd,blhd->bhrjql', qb, k_ctx, preferred_element_type=jnp.float32) * scale
    p = jax.nn.softmax(jnp.concatenate([s_loc, s_ctx], axis=-1), axis=-1).astype(v.dtype)
    p_loc = p[..., :n_loc].reshape(b, h, rows, n_cb, NA_QCOL_BLOCK, wr, NA_KCOL_BLOCK)
    p_ctx = p[..., n_loc:]
    o = jnp.einsum('bhrjqik,brjikhe->brjqhe', p_loc, vb) + jnp.einsum('bhrjql,blhe->brjqhe', p_ctx, v_ctx)
    return o.reshape(b, s, h, d)


def _token_shift(f, mix):
    prev = jnp.pad(f[:, :-1], ((0, 0), (1, 0), (0, 0)))
    nxt = jnp.pad(f[:, 1:], ((0, 0), (0, 1), (0, 0)))
    return f + mix[0] * (prev - f) + mix[1] * (nxt - f)


def _wkv_scan(s0, r, w, k, v, a, b, reverse):
    def step(state, xs):
        r_t, w_t, k_t, v_t, a_t, b_t = xs
        sa = jnp.einsum('bhvk,bhk->bhv', state, a_t)
        state = state * w_t[:, :, None, :] + sa[..., None] * b_t[:, :, None, :] + v_t[..., None] * k_t[:, :, None, :]
        return state, jnp.einsum('bhvk,bhk->bhv', state, r_t)
    xs = tuple(jnp.moveaxis(z, 1, 0) for z in (r, w, k, v, a, b))
    s_fin, y = lax.scan(step, s0, xs, reverse=reverse)
    return s_fin, jnp.moveaxis(y, 0, 1)


def _rwkv_mixer(f, s_f0, s_b0, lp):
    bsz, t, _ = f.shape
    f = _token_shift(f, lp['rwkv_mix'])
    r, k, v, wd_f, wd_b, ad_f, ad_b, gd = _split(f, RWKV_SIZES)
    def hd(z):
        return z.astype(jnp.float32).reshape(bsz, t, RWKV_HEADS, RWKV_HEAD_DIM)
    rh, kh, vh = hd(r), hd(k), hd(v)
    kk = kh * lp['rwkv_kk'].astype(jnp.float32).reshape(RWKV_HEADS, RWKV_HEAD_DIM)
    kk = kk / jnp.maximum(jnp.linalg.norm(kk, axis=-1, keepdims=True), 1e-12)
    ka = lp['rwkv_ka'].astype(jnp.float32).reshape(RWKV_HEADS, RWKV_HEAD_DIM)
    rk = lp['rwkv_rk'].astype(jnp.float32)
    ys, bonus, finals = [], [], []
    for d, (wd, ad, s0) in enumerate(((wd_f, ad_f, s_f0), (wd_b, ad_b, s_b0))):
        w_log = -jax.nn.softplus(-(lp['rwkv_w0'][d] + jnp.tanh(wd) @ lp['rwkv_w2'][d])) - 0.5
        decay = hd(jnp.exp(-jnp.exp(w_log.astype(jnp.float32))))
        a = hd(jax.nn.sigmoid(lp['rwkv_a0'][d] + ad @ lp['rwkv_a2'][d]))
        k_d = kh * (1.0 + (a - 1.0) * ka)
        s_fin, y_d = _wkv_scan(s0.astype(jnp.float32), rh, decay, k_d, vh, -kk, kk * a, reverse=(d == 1))
        ys.append(y_d)
        bonus.append(jnp.sum(rh * k_d * rk, axis=-1, keepdims=True) * vh)
        finals.append(s_fin.astype(f.dtype))
    y = ys[0] + ys[1]
    mu = jnp.mean(y, axis=-1, keepdims=True)
    var = jnp.mean(jnp.square(y - mu), axis=-1, keepdims=True)
    y = ((y - mu) * lax.rsqrt(var + RWKV_GN_EPS)).reshape(bsz, t, RWKV_WIDTH) * lp['rwkv_lnx_g'] + lp['rwkv_lnx_b']
    y = y + (bonus[0] + bonus[1]).reshape(bsz, t, RWKV_WIDTH)
    g = jax.nn.sigmoid(gd) @ lp['rwkv_g2']
    return (y * g).astype(f.dtype), finals[0], finals[1]


def _merge(gate_pre, oa, ob, oc, lp):
    o = jnp.stack([oa, ob, oc], axis=2)
    branches = jnp.einsum('btie,ied->btid', o, lp['w_branch'])
    g = jax.nn.sigmoid(gate_pre.reshape(gate_pre.shape[0], gate_pre.shape[1], N_BRANCH, D_MODEL))
    return jnp.sum(g * branches, axis=2) @ lp['w_out']


def _mixer_context(h, lp, lam_init):
    b, t, _ = h.shape
    na_q, na_k, na_v, df_q, df_k, df_v, rw, gate_pre = _split(h @ lp['w_in'], IN_SIZES)
    qa = na_q.reshape(b, t, NA_HEADS, NA_HEAD_DIM)
    ka = na_k.reshape(b, t, NA_HEADS, NA_HEAD_DIM)
    va = na_v.reshape(b, t, NA_HEADS, NA_HEAD_DIM)
    oa = _dense_attend(qa, ka, va).reshape(b, t, NA_WIDTH)
    qd = df_q.reshape(b, t, DIFF_HEADS, 2, DIFF_QK_DIM)
    kd = df_k.reshape(b, t, DIFF_HEADS, 2, DIFF_QK_DIM)
    vd = df_v.reshape(b, t, DIFF_HEADS, DIFF_V_DIM)
    lam = _diff_lambda(lp['diff_lambda'], lam_init)
    ob = _diff_post(_diff_attend(qd, kd, vd, lam), lp['diff_subln'], lam_init)
    zeros = jnp.zeros((b, RWKV_HEADS, RWKV_HEAD_DIM, RWKV_HEAD_DIM), jnp.float32)
    oc, s_f, s_b = _rwkv_mixer(rw, zeros, zeros, lp)
    return _merge(gate_pre, oa, ob, oc, lp), (ka, va, kd, vd, s_f, s_b)


def _mixer_latent(h, cache_l, lp, lam_init):
    k_na_ctx, v_na_ctx, k_df_ctx, v_df_ctx, s_f0, s_b0 = cache_l
    b, t, _ = h.shape
    na_q, na_k, na_v, df_q, df_k, df_v, rw, gate_pre = _split(h @ lp['w_in'], IN_SIZES)
    qa = na_q.reshape(b, t, NA_HEADS, NA_HEAD_DIM)
    ka = na_k.reshape(b, t, NA_HEADS, NA_HEAD_DIM)
    va = na_v.reshape(b, t, NA_HEADS, NA_HEAD_DIM)
    oa = _na_latent(qa, ka, va, k_na_ctx, v_na_ctx, lp['na_rpb']).reshape(b, t, NA_WIDTH)
    cos, sin = _axial_rope_tables(t)
    qd = _apply_axial_rope(df_q.reshape(b, t, DIFF_HEADS, 2, DIFF_QK_DIM), cos, sin)
    kd = _apply_axial_rope(df_k.reshape(b, t, DIFF_HEADS, 2, DIFF_QK_DIM), cos, sin)
    vd = df_v.reshape(b, t, DIFF_HEADS, DIFF_V_DIM)
    k_all = jnp.concatenate([kd, k_df_ctx.astype(kd.dtype)], axis=1)
    v_all = jnp.concatenate([vd, v_df_ctx.astype(vd.dtype)], axis=1)
    lam = _diff_lambda(lp['diff_lambda'], lam_init)
    ob = _diff_post(_diff_attend(qd, k_all, v_all, lam), lp['diff_subln'], lam_init)
    oc, _, _ = _rwkv_mixer(rw, s_f0, s_b0, lp)
    return _merge(gate_pre, oa, ob, oc, lp), None


def _block(x, cond, mixer, lp):
    alpha = (2.0 * DEPTH) ** 0.25
    sh1, sc1, g1, sh2, sc2, g2 = _adaln(cond, lp['w_ada'], lp['b_ada'])
    mixed, ctx_tensors = mixer(_modulate(x, sh1, sc1))
    x = _layer_norm(alpha * x + g1 * mixed, lp['ln1_g'], lp['ln1_b'])
    ff = _swiglu(_modulate(x, sh2, sc2), lp['w_ffn_in'], lp['w_ffn_out'])
    x = _layer_norm(alpha * x + g2 * ff, lp['ln2_g'], lp['ln2_b'])
    return x, ctx_tensors


def setup_inputs(seed: int = 0) -> dict:
    key = jax.random.key(seed)
    ks = iter(jax.random.split(key, 40))
    def nrm(shape, scale):
        return scale * jax.random.normal(next(ks), shape, jnp.float32)
    def uni(shape, lo, hi):
        return jax.random.uniform(next(ks), shape, jnp.float32, lo, hi)
    beta = (8.0 * DEPTH) ** -0.25
    L, D = DEPTH, D_MODEL
    return {
        'x_prompt': nrm((BATCH, SEQ, D), 1.0),
        'x_sample': nrm((DEC_BATCH, DEC_SEQ, D), 1.0),
        'cache_na_k': nrm((DEC_BATCH, L, PAST_LEN, NA_HEADS, NA_HEAD_DIM), 1.0),
        'cache_na_v': nrm((DEC_BATCH, L, PAST_LEN, NA_HEADS, NA_HEAD_DIM), 1.0),
        'cache_diff_k': nrm((DEC_BATCH, L, PAST_LEN, DIFF_HEADS, 2, DIFF_QK_DIM), 1.0),
        'cache_diff_v': nrm((DEC_BATCH, L, PAST_LEN, DIFF_HEADS, DIFF_V_DIM), 1.0),
        'state_rwkv_fwd': nrm((DEC_BATCH, L, RWKV_HEADS, RWKV_HEAD_DIM, RWKV_HEAD_DIM), 1.0),
        'state_rwkv_bwd': nrm((DEC_BATCH, L, RWKV_HEADS, RWKV_HEAD_DIM, RWKV_HEAD_DIM), 1.0),
        'c': nrm((DEC_BATCH, D), 1.0),
        'c_ctx': nrm((D,), 1.0),
        'w_ada': nrm((L, D, 6 * D), 0.5 * D ** -0.5),
        'b_ada': nrm((L, 6 * D), 0.01),
        'w_in': nrm((L, D, N_IN), D ** -0.5),
        'na_rpb': nrm((L, NA_HEADS, NA_REL_ROWS, NA_REL_COLS), 0.1),
        'diff_lambda': nrm((L, 4, DIFF_QK_DIM), 0.1),
        'diff_subln': 1.0 + nrm((L, DIFF_V_WIDTH), 0.02),
        'rwkv_mix': uni((L, 2, RWKV_FEAT), 0.0, 0.5),
        'rwkv_w0': uni((L, 2, RWKV_WIDTH), -5.5, -0.5),
        'rwkv_w2': nrm((L, 2, RWKV_DECAY_RANK, RWKV_WIDTH), 0.5 * RWKV_DECAY_RANK ** -0.5),
        'rwkv_a0': nrm((L, 2, RWKV_WIDTH), 0.1),
        'rwkv_a2': nrm((L, 2, RWKV_ICL_RANK, RWKV_WIDTH), 0.5 * RWKV_ICL_RANK ** -0.5),
        'rwkv_g2': nrm((L, RWKV_GATE_RANK, RWKV_WIDTH), RWKV_GATE_RANK ** -0.5),
        'rwkv_kk': 0.85 + nrm((L, RWKV_WIDTH), 0.05),
        'rwkv_ka': 1.0 + nrm((L, RWKV_WIDTH), 0.05),
        'rwkv_rk': nrm((L, RWKV_HEADS, RWKV_HEAD_DIM), 0.1),
        'rwkv_lnx_g': 1.0 + nrm((L, RWKV_WIDTH), 0.02),
        'rwkv_lnx_b': nrm((L, RWKV_WIDTH), 0.02),
        'w_branch': nrm((L, N_BRANCH, BRANCH_WIDTH, D), BRANCH_WIDTH ** -0.5),
        'w_out': nrm((L, D, D), beta * D ** -0.5),
        'ln1_g': 1.0 + nrm((L, D), 0.02),
        'ln1_b': nrm((L, D), 0.02),
        'w_ffn_in': nrm((L, D, 2 * FFN_HIDDEN), D ** -0.5),
        'w_ffn_out': nrm((L, FFN_HIDDEN, D), beta * FFN_HIDDEN ** -0.5),
        'ln2_g': 1.0 + nrm((L, D), 0.02),
        'ln2_b': nrm((L, D), 0.02),
    }


def reference(x_prompt, x_sample, cache_na_k, cache_na_v, cache_diff_k, cache_diff_v,
              state_rwkv_fwd, state_rwkv_bwd, c, c_ctx, w_ada, b_ada, w_in, na_rpb,
              diff_lambda, diff_subln, rwkv_mix, rwkv_w0, rwkv_w2, rwkv_a0, rwkv_a2,
              rwkv_g2, rwkv_kk, rwkv_ka, rwkv_rk, rwkv_lnx_g, rwkv_lnx_b, w_branch,
              w_out, ln1_g, ln1_b, w_ffn_in, w_ffn_out, ln2_g, ln2_b):
    x_ctx, x_lat = x_prompt, x_sample
    cond_ctx = c_ctx[None, :]
    new_t = ([], [], [], [], [], [])
    for l in range(DEPTH):
        lp = {
            'w_ada': w_ada[l], 'b_ada': b_ada[l], 'w_in': w_in[l], 'na_rpb': na_rpb[l],
            'diff_lambda': diff_lambda[l], 'diff_subln': diff_subln[l],
            'rwkv_mix': rwkv_mix[l], 'rwkv_w0': rwkv_w0[l], 'rwkv_w2': rwkv_w2[l],
            'rwkv_a0': rwkv_a0[l], 'rwkv_a2': rwkv_a2[l], 'rwkv_g2': rwkv_g2[l],
            'rwkv_kk': rwkv_kk[l], 'rwkv_ka': rwkv_ka[l], 'rwkv_rk': rwkv_rk[l],
            'rwkv_lnx_g': rwkv_lnx_g[l], 'rwkv_lnx_b': rwkv_lnx_b[l],
            'w_branch': w_branch[l], 'w_out': w_out[l],
            'ln1_g': ln1_g[l], 'ln1_b': ln1_b[l],
            'w_ffn_in': w_ffn_in[l], 'w_ffn_out': w_ffn_out[l],
            'ln2_g': ln2_g[l], 'ln2_b': ln2_b[l],
        }
        lam_init = 0.8 - 0.6 * math.exp(-0.3 * l)
        x_ctx, ctx_t = _block(x_ctx, cond_ctx, functools.partial(_mixer_context, lp=lp, lam_init=lam_init), lp)
        for store, tensor in zip(new_t, ctx_t):
            store.append(tensor)
        cache_l = (cache_na_k[:, l], cache_na_v[:, l], cache_diff_k[:, l], cache_diff_v[:, l],
                   state_rwkv_fwd[:, l], state_rwkv_bwd[:, l])
        x_lat, _ = _block(x_lat, c, functools.partial(_mixer_latent, cache_l=cache_l, lp=lp, lam_init=lam_init), lp)
    new_na_k = jnp.stack(new_t[0], axis=1)
    new_na_v = jnp.stack(new_t[1], axis=1)
    new_diff_k = jnp.stack(new_t[2], axis=1)
    new_diff_v = jnp.stack(new_t[3], axis=1)
    new_rwkv_fwd = jnp.stack(new_t[4], axis=1)
    new_rwkv_bwd = jnp.stack(new_t[5], axis=1)
    return (x_ctx, x_lat, new_na_k, new_na_v, new_diff_k, new_diff_v, new_rwkv_fwd, new_rwkv_bwd)
```

```python
import os
import numpy as np
from contextlib import ExitStack
import concourse.bass as bass
import concourse.mybir as mybir
from concourse.bass_utils import run_bass_kernel_spmd

F32 = mybir.dt.float32
BF16 = mybir.dt.bfloat16
AF = mybir.ActivationFunctionType
ALU = mybir.AluOpType

D = 2048
L = 4
NL = int(os.environ.get('KL', '4'))
T = 1024
KT = 16
NIN = 11136
FH = 5632
ALPHA = float((2.0 * L) ** 0.25)
LN_EPS = 1e-5
GN_EPS = 64e-5
NEG = -30000.0
C_NAQ, C_NAK, C_NAV, C_DFQ, C_DFK, C_DFV, C_RW, C_GATE = 0, 512, 1024, 1536, 2048, 2560, 3072, 4992

PV = {}
_o = 0
for _n, _w in (("b_ada", 96), ("ln1_g", 16), ("ln1_b", 16), ("ln2_g", 16), ("ln2_b", 16),
               ("subln", 4), ("mix0", 15), ("mix1", 15), ("a0", 8), ("kk", 4), ("ka", 4),
               ("rk", 4), ("lnx_g", 4), ("lnx_b", 4), ("lam", 4), ("w0", 8)):
    PV[_n] = (_o, _w)
    _o += _w
NPV = _o


class _Stop(Exception):
    pass


def chk(n):
    if int(os.environ.get("KSTOP", "0")) == n:
        raise _Stop()


class Tok:
    __slots__ = ("w", "r", "excl")

    def __init__(self, init_r=None, excl=False):
        self.w = None
        self.r = dict(init_r) if init_r else {}
        self.excl = excl


class Eng:
    def __init__(self, nm, h, sem):
        self.nm, self.h, self.sem = nm, h, sem
        self.cnt = 0
        self.seen = {}


class Chan:
    def __init__(self, sem):
        self.sem = sem
        self.val = 0


class Sched:
    def __init__(self, nc, es):
        self.nc, self.es = nc, es
        self.E = {}
        for nm, h in (("pe", nc.tensor), ("act", nc.scalar), ("dve", nc.vector),
                      ("pool", nc.gpsimd), ("sp", nc.sync)):
            self.E[nm] = Eng(nm, h, es.enter_context(nc.semaphore("sem_" + nm)))
        self.grave = {}
        self.nbuf = 0
        self.chans = []
        self.chmap = {}

    def chan(self):
        c = Chan(self.es.enter_context(self.nc.semaphore("ch%d" % len(self.chans))))
        self.chans.append(c)
        return c

    def chan_for(self, key):
        if key not in self.chmap:
            self.chmap[key] = self.chan()
        return self.chmap[key]

    def tok(self):
        return Tok(self.grave)

    def _wait(self, e, ev):
        sem, val, src = ev
        if isinstance(src, Chan):
            val = src.val
        elif src is e and e.nm == "pe":
            return
        k = id(sem)
        if e.seen.get(k, 0) >= val:
            return
        e.h.wait_ge(sem, val)
        e.seen[k] = val

    def _deps(self, e, R, W):
        for t in R:
            if t.w is not None:
                self._wait(e, t.w)
            if t.excl:
                for ev in list(t.r.values()):
                    if ev[2] is not e:
                        self._wait(e, ev)
        for t in W:
            if t.w is not None:
                self._wait(e, t.w)
            for ev in list(t.r.values()):
                self._wait(e, ev)

    @staticmethod
    def _mark(ev, R, W):
        k = id(ev[0])
        for t in R:
            t.r[k] = ev
        for t in W:
            t.w = ev
            t.r = {}

    def op(self, en, fn, R=(), W=(), inc=True):
        e = self.E[en]
        self._deps(e, R, W)
        ins = fn(e.h)
        if inc:
            e.cnt += 1
            ins.then_inc(e.sem, 1)
            ev = (e.sem, e.cnt, e)
        else:
            ev = (e.sem, e.cnt + 1, e)
        self._mark(ev, R, W)

    def dma(self, qn, ch, out, in_, R=(), W=()):
        e = self.E[qn]
        self._deps(e, R, W)
        e.h.dma_start(out=out, in_=in_).then_inc(ch.sem, 16)
        ch.val += 16
        self._mark((ch.sem, ch.val, ch), R, W)

    def bury(self, toks):
        for t in toks:
            evs = list(t.r.values())
            if t.w is not None:
                evs.append(t.w)
            for ev in evs:
                k = id(ev[0])
                val = ev[2].val if isinstance(ev[2], Chan) else ev[1]
                old = self.grave.get(k)
                if old is None or old[1] < val:
                    self.grave[k] = (ev[0], val, ev[2])


class Buf:
    def __init__(self, S, scope, name, shape, dt, ntok=1):
        S.nbuf += 1
        self.name = name
        self.h = scope.enter_context(S.nc.sbuf_tensor("%s_%d" % (name, S.nbuf), list(shape), dt))
        self.t = [S.tok() for _ in range(ntok)]
        scope.callback(S.bury, self.t)

    def __getitem__(self, idx):
        return self.h[idx]


def _sp(x):
    if isinstance(x, tuple):
        a, t = x
        return a, (list(t) if isinstance(t, (list, tuple)) else [t])
    return x, []


class K:
    def __init__(self, nc, es, dbg):
        self.nc, self.es = nc, es
        self.S = Sched(nc, es)
        self.dbg = dbg

    def mm(self, out, lhsT, rhs, start=True, stop=True, inc=None):
        o, ot = _sp(out); l, lt = _sp(lhsT); r, rt = _sp(rhs)
        self.S.op("pe", lambda h: h.matmul(o, lhsT=l, rhs=r, start=start, stop=stop),
                  R=lt + rt, W=ot, inc=(stop if inc is None else inc))

    def tr(self, out, in_, ident):
        o, ot = _sp(out); i, it = _sp(in_); d, dt_ = _sp(ident)
        self.S.op("pe", lambda h: h.transpose(o, i, d), R=it + dt_, W=ot)

    def act(self, out, in_, func, bias=None, scale=1.0, eng="act"):
        o, ot = _sp(out); i, it = _sp(in_); b, bt = _sp(bias); s, st = _sp(scale)
        kw = {}
        if b is not None:
            kw["bias"] = b
        self.S.op(eng, lambda h: h.activation(out=o, in_=i, func=func, scale=s, **kw),
                  R=it + bt + st, W=ot)

    def ts(self, eng, out, in0, s1, s2, op0, op1=None):
        o, ot = _sp(out); i, it = _sp(in0); a, at = _sp(s1); b, bt = _sp(s2)
        if op1 is None:
            self.S.op(eng, lambda h: h.tensor_scalar(out=o, in0=i, scalar1=a, scalar2=None, op0=op0),
                      R=it + at, W=ot)
        else:
            self.S.op(eng, lambda h: h.tensor_scalar(out=o, in0=i, scalar1=a, scalar2=b, op0=op0, op1=op1),
                      R=it + at + bt, W=ot)

    def tt(self, eng, out, in0, in1, op):
        o, ot = _sp(out); i, it = _sp(in0); j, jt = _sp(in1)
        self.S.op(eng, lambda h: h.tensor_tensor(out=o, in0=i, in1=j, op=op), R=it + jt, W=ot)

    def stt(self, eng, out, in0, sc, in1, op0, op1):
        o, ot = _sp(out); i, it = _sp(in0); s, st = _sp(sc); j, jt = _sp(in1)
        self.S.op(eng, lambda h: h.scalar_tensor_tensor(out=o, in0=i, scalar=s, in1=j, op0=op0, op1=op1),
                  R=it + st + jt, W=ot)

    def copy(self, eng, out, in_):
        o, ot = _sp(out); i, it = _sp(in_)
        if eng == "act":
            self.S.op(eng, lambda h: h.activation(out=o, in_=i, func=AF.Copy), R=it, W=ot)
        else:
            self.S.op(eng, lambda h: h.tensor_copy(out=o, in_=i), R=it, W=ot)

    def memset(self, eng, out, val):
        o, ot = _sp(out)
        self.S.op(eng, lambda h: h.memset(o, val), W=ot)

    def dma(self, q, ch, out, in_):
        o, ot = _sp(out); i, it = _sp(in_)
        if isinstance(ch, str):
            ch = self.S.chan_for(ch)
        self.S.dma(q, ch, o, i, R=it, W=ot)

    def setup_psum(self):
        self.banks = []
        for i in range(8):
            h = self.es.enter_context(self.nc.psum_tensor("bank%d" % i, [128, 512], F32))
            self.banks.append((h[:, :], Tok(excl=True)))
        self._bk = {"a": [0, 0, 4], "b": [0, 4, 4]}

    def bank(self, pool="a"):
        st = self._bk[pool]
        i = st[1] + st[0] % st[2]
        st[0] += 1
        return self.banks[i]


def build(nc, dbg=None):
    stage = int(os.environ.get("KSTAGE", "99"))
    es = ExitStack()
    with es:
        k = K(nc, es, dbg)
        S = k.S
        k.setup_psum()

        def din(name, shape, dt=F32):
            return nc.dram_tensor(name, list(shape), dt, kind="ExternalInput").ap()

        def dout(name, shape, dt=F32):
            return nc.dram_tensor(name, list(shape), dt, kind="ExternalOutput").ap()

        x_in = din("x_in", [T, D])
        cond = din("cond", [128, KT])
        consts = din("consts", [128, 1024])
        pvec = din("pvec", [NL, 128, NPV])
        w_ada = din("w_ada", [NL, D, 6 * D])
        w_in = din("w_in", [NL, D, NIN])
        w_branch = din("w_branch", [NL, 3, 512, D])
        w_out = din("w_out", [NL, D, D])
        w_f1 = din("w_ffn_in", [NL, D, 2 * FH])
        w_f2 = din("w_ffn_out", [NL, FH, D])
        nab = din("nab", [NL, 4, 2, 10, 128, 512])
        dmask = din("dmask", [2, 10, 128, 512])
        rope = din("rope", [2, 128, T])
        ext_k = din("ext_k", [NL, 2, 256, 512])
        ext_v = din("ext_v", [NL, 2, 256, 512])
        rw_w2 = din("rw_w2", [NL, 128, 512])
        rw_a2 = din("rw_a2", [NL, 128, 512])
        rw_g2 = din("rw_g2", [NL, 128, 512])
        s0_in = din("s0", [NL, 2, 4, 128, 128])
        keepf = din("keepf", [128, 1])
        st_out = dout("st", [NL, 2, 4, 4, 128, 128])
        y_out = dout("y", [T, D])
        kv_out = dout("kv", [NL, 4, T, 512])

        top = es
        P = lambda name, shape, dt, ntok=1, scope=None: Buf(S, scope or top, name, shape, dt, ntok)

        xT = P("xT", [128, KT, T], F32, ntok=KT)
        hT = P("hT", [128, KT, T], BF16, ntok=KT)
        cst = P("cst", [128, 1024], F32)
        pv = P("pv", [128, NPV], F32)
        mod = P("mod", [128, 96], F32)
        mod1p = P("mod1p", [128, 32], F32)
        lnp = P("lnp", [128, 64], F32)
        scond = P("scond", [128, KT], BF16)
        condt = P("condt", [128, KT], F32)
        onesb = P("onesb", [128, 128], BF16)
        NWB = 3
        wring = [P("wr%d" % i, [128, 4096], BF16) for i in range(NWB)]
        wch = [S.chan() for _ in range(NWB)]
        wcur = [0]
        ch_misc = S.chan()
        ch_pool = S.chan()
        ch_out = S.chan()

        ident = (cst[:, 0:128], cst.t[0])

        k.dma("sp", "cst", (cst[:, :], cst.t[0]), consts)
        k.dma("sp", "condt", (condt[:, :], condt.t[0]), cond)
        k.act((scond[:, :], scond.t[0]), (condt[:, :], condt.t[0]), AF.Silu)
        k.memset("dve", (onesb[:, :], onesb.t[0]), 1.0)

        with ExitStack() as sc:
            xin = [P("xin%d" % i, [128, D], F32, scope=sc) for i in range(2)]
            chx = [S.chan(), S.chan()]
            for tt in range(8):
                b = xin[tt % 2]
                k.dma("sp", chx[tt % 2], (b[:, :], b.t[0]), x_in[tt * 128:(tt + 1) * 128, :])
                for kq in range(4):
                    bk, bt = k.bank("a")
                    for j in range(4):
                        kt = kq * 4 + j
                        k.tr((bk[:, j * 128:(j + 1) * 128], bt), (b[:, kt * 128:(kt + 1) * 128], b.t[0]), ident)
                    for j in range(4):
                        kt = kq * 4 + j
                        eng = "dve" if j % 2 == 0 else "act"
                        if eng == "dve":
                            k.ts("dve", (xT[:, kt, tt * 128:(tt + 1) * 128], xT.t[kt]),
                                 (bk[:, j * 128:(j + 1) * 128], bt), ALPHA, None, ALU.mult)
                        else:
                            k.act((xT[:, kt, tt * 128:(tt + 1) * 128], xT.t[kt]),
                                  (bk[:, j * 128:(j + 1) * 128], bt), AF.Copy, scale=ALPHA)

        def wload(srcs):
            i = wcur[0] % NWB
            wcur[0] += 1
            b = wring[i]
            for dst, src in srcs:
                k.dma("pool", wch[i], (dst(b), b.t[0]), src)
            return b

        def wview(b, kt_n, ncols):
            return b.h[:, 0:kt_n * ncols].rearrange("p (k n) -> p k n", k=kt_n)

        def stream(jobs, depth=NWB - 1):
            bufs = {}
            n = len(jobs)
            for i in range(min(depth, n)):
                bufs[i] = jobs[i][0]()
            for i in range(n):
                jobs[i][1](bufs.pop(i))
                if i + depth < n:
                    bufs[i + depth] = jobs[i + depth][0]()

        def wsrc(w2d, c0, nc_, kt0=0, ktn=KT):
            return w2d[kt0 * 128:(kt0 + ktn) * 128, c0:c0 + nc_].rearrange("(k p) n -> p k n", p=128)

        def stats_norm(eps, emit):
            with ExitStack() as sc:
                MU = P("MU", [128, T], F32, scope=sc)
                RS = P("RS", [128, T], F32, scope=sc)
                cb = [P("cb%d" % i, [128, 512], BF16, scope=sc) for i in range(2)]
                sq = [P("sq%d" % i, [128, 512], BF16, scope=sc) for i in range(2)]
                tmp = [P("nt%d" % i, [128, T], F32, scope=sc) for i in range(2)]
                s1 = [k.bank("b") for _ in range(2)]
                s2 = [k.bank("b") for _ in range(2)]
                i = 0
                for kt in range(KT):
                    for hf in range(2):
                        sl = slice(hf * 512, (hf + 1) * 512)
                        c, q = cb[i % 2], sq[i % 2]
                        i += 1
                        k.copy("dve", (c[:, :], c.t[0]), (xT[:, kt, sl], xT.t[kt]))
                        k.act((q[:, :], q.t[0]), (xT[:, kt, sl], xT.t[kt]), AF.Square)
                        k.mm(s1[hf], (onesb[:, :], onesb.t[0]), (c[:, :], c.t[0]), start=(kt == 0), stop=(kt == KT - 1), inc=True)
                        k.mm(s2[hf], (onesb[:, :], onesb.t[0]), (q[:, :], q.t[0]), start=(kt == 0), stop=(kt == KT - 1), inc=True)
                for hf in range(2):
                    sl = slice(hf * 512, (hf + 1) * 512)
                    k.ts("dve", (MU[:, sl], MU.t[0]), s1[hf], 1.0 / D, None, ALU.mult)
                    k.tt("dve", (RS[:, sl], RS.t[0]), (MU[:, sl], MU.t[0]), (MU[:, sl], MU.t[0]), ALU.mult)
                    k.stt("dve", (RS[:, sl], RS.t[0]), s2[hf], 1.0 / D, (RS[:, sl], RS.t[0]), ALU.mult, ALU.subtract)
                    k.act((RS[:, sl], RS.t[0]), (RS[:, sl], RS.t[0]), AF.Sqrt, bias=eps)
                    k.S.op("dve", lambda h, sl=sl: h.reciprocal(out=RS[:, sl], in_=RS[:, sl]), W=[RS.t[0]])
                for kt in range(KT):
                    tm = tmp[kt % 2]
                    k.tt("dve", (tm[:, :], tm.t[0]), (xT[:, kt, :], xT.t[kt]), (MU[:, :], MU.t[0]), ALU.subtract)
                    k.tt("dve", (tm[:, :], tm.t[0]), (tm[:, :], tm.t[0]), (RS[:, :], RS.t[0]), ALU.mult)
                    emit(kt, tm)

        def modulate(which):
            sh0 = 0 if which == 0 else 48
            def emit(kt, tm):
                k.act((hT[:, kt, :], hT.t[kt]), (tm[:, :], tm.t[0]), AF.Identity,
                      bias=(mod[:, sh0 + kt:sh0 + kt + 1], mod.t[0]),
                      scale=(mod1p[:, which * 16 + kt:which * 16 + kt + 1], mod1p.t[0]))
            stats_norm(ALPHA * ALPHA * LN_EPS, emit)

        def layernorm(which, out_alpha):
            o = which * 32 if out_alpha != 1.0 else None
            def emit(kt, tm):
                if out_alpha != 1.0:
                    gs = (lnp[:, which * 32 + kt:which * 32 + kt + 1], lnp.t[0])
                    bs = (lnp[:, which * 32 + 16 + kt:which * 32 + 17 + kt], lnp.t[0])
                else:
                    g0 = PV["ln2_g"][0]; b0 = PV["ln2_b"][0]
                    gs = (pv[:, g0 + kt:g0 + kt + 1], pv.t[0])
                    bs = (pv[:, b0 + kt:b0 + kt + 1], pv.t[0])
                k.act((xT[:, kt, :], xT.t[kt]), (tm[:, :], tm.t[0]), AF.Identity, bias=bs, scale=gs)
            stats_norm(LN_EPS, emit)

        def adaln(l):
            bk, bt = k.bank("a")
            jobs = []
            for cgi in range(48):
                def ld(cgi=cgi):
                    return wload([(lambda b: wview(b, KT, 256), wsrc(w_ada[l], cgi * 256, 256))])
                def use(b, cgi=cgi):
                    wv = wview(b, KT, 256)
                    for j in range(2):
                        col = cgi * 2 + j
                        for kt in range(KT):
                            k.mm((bk[:, col:col + 1], bt), (wv[:, kt, j * 128:(j + 1) * 128], b.t[0]),
                                 (scond[:, kt:kt + 1], scond.t[0]), start=(kt == 0), stop=(kt == KT - 1),
                                 inc=(kt == KT - 1))
                jobs.append((ld, use))
            stream(jobs)
            b0 = PV["b_ada"][0]
            k.tt("dve", (mod[:, :], mod.t[0]), (bk[:, 0:96], bt), (pv[:, b0:b0 + 96], pv.t[0]), ALU.add)
            k.ts("dve", (mod1p[:, 0:16], mod1p.t[0]), (mod[:, 16:32], mod.t[0]), 1.0, None, ALU.add)
            k.ts("dve", (mod1p[:, 16:32], mod1p.t[0]), (mod[:, 64:80], mod.t[0]), 1.0, None, ALU.add)
            for i, nm in enumerate(("ln1_g", "ln1_b", "ln2_g", "ln2_b")):
                o0 = PV[nm][0]
                k.ts("dve", (lnp[:, i * 16:(i + 1) * 16], lnp.t[0]), (pv[:, o0:o0 + 16], pv.t[0]), ALPHA, None, ALU.mult)

        def proj_fm(wmat, c0, ncols, consume, kt0=0, ktn=KT, src=None, src_tok=None):
            jobs = []
            nchunk = ncols // 256
            for ci in range(nchunk):
                def ld(ci=ci):
                    return wload([(lambda b: wview(b, ktn, 256), wsrc(wmat, c0 + ci * 256, 256, kt0, ktn))])
                def use(b, ci=ci):
                    wv = wview(b, ktn, 256)
                    for j in range(2):
                        for hf in range(2):
                            bk = k.bank("a")
                            for kt in range(ktn):
                                rhs = src(kt, hf)
                                k.mm(bk, (wv[:, kt, j * 128:(j + 1) * 128], b.t[0]), rhs,
                                     start=(kt == 0), stop=(kt == ktn - 1))
                            consume(ci * 2 + j, hf, bk)
                jobs.append((ld, use))
            return jobs

        hsrc = lambda kt, hf: (hT[:, kt, hf * 512:(hf + 1) * 512], hT.t[kt])

        def proj_tm(l, c0, slot, post=None):
            with ExitStack() as sc:
                stg = [P("stg%d" % i, [128, 512], F32, scope=sc) for i in range(2)]
                bufs = [wload([(lambda b: wview(b, KT, 256), wsrc(w_in[l], c0 + ci * 256, 256))]) for ci in range(2)]
                for tt in range(8):
                    bk, bt = k.bank("a")
                    for ci in range(2):
                        wv = wview(bufs[ci], KT, 256)
                        for kt in range(KT):
                            k.mm((bk[:, ci * 256:(ci + 1) * 256], bt), (hT[:, kt, tt * 128:(tt + 1) * 128], hT.t[kt]),
                                 (wv[:, kt, :], bufs[ci].t[0]), start=(kt == 0), stop=(kt == KT - 1),
                                 inc=(ci == 1 and kt == KT - 1))
                    st = stg[tt % 2]
                    k.copy("act", (st[:, :], st.t[0]), (bk[:, :], bt))
                    if post is not None:
                        post(tt, (bk, bt))
                    k.dma("sp", "o_stg%d" % (tt % 2), kv_out[l, slot, tt * 128:(tt + 1) * 128, :], (st[:, :], st.t[0]))

        KMIX = os.environ.get("KMIX", "abc")
        onesf = (cst[:, 768:896], cst.t[0])
        rotT = (cst[:, 896:1024], cst.t[0])
        bch = [S.chan() for _ in range(4)]
        bcur = [0]

        def attention(sc, qT, kT, V, ncomp, bias_src, out_cb):
            bring = [P("br%d" % i, [128, 512], F32, scope=sc) for i in range(4)]

            def bload(src):
                i = bcur[0] % 4
                bcur[0] += 1
                k.dma("sp", bch[i], (bring[i][:, :], bring[i].t[0]), src)
                return bring[i]
            tmp = [P("atm%d" % i, [128, 512], F32, scope=sc) for i in range(2)]
            pt = [P("apt%d" % i, [128, 512], BF16, scope=sc) for i in range(3)]
            kd = 128 // ncomp
            units = [(h, qh, c, j) for h in range(4) for qh in range(2) for c in range(ncomp) for j in range(10)]
            srcs = [bias_src(h, qh, j) for (h, qh, c, j) in units]
            bufs = {}
            for i in range(min(3, len(units))):
                bufs[i] = bload(srcs[i])
            n = 0
            for ui, (h, qh, c, j) in enumerate(units):
                if c == 0 and j == 0:
                    obanks = [k.bank("b") for _ in range(2 * ncomp)]
                qs = slice(qh * 512, (qh + 1) * 512)
                rows = slice(c * kd, (c + 1) * kd)
                sb = k.bank("a")
                k.mm(sb, (kT[rows, h, j * 128:(j + 1) * 128], kT.t[h]), (qT[rows, h, qs], qT.t[h]))
                bb = bufs.pop(ui)
                tm = tmp[n % 2]; p_ = pt[n % 3]; n += 1
                k.tt("dve", (tm[:, :], tm.t[0]), sb, (bb[:, :], bb.t[0]), ALU.add)
                if ui + 3 < len(units):
                    bufs[ui + 3] = bload(srcs[ui + 3])
                k.act((p_[:, :], p_.t[0]), (tm[:, :], tm.t[0]), AF.Exp)
                k.mm(obanks[2 * c], (V[:, j, h * 128:(h + 1) * 128], V.t[j]), (p_[:, :], p_.t[0]),
                     start=(j == 0), stop=(j == 9), inc=True)
                k.mm(obanks[2 * c + 1], (onesb[:, :], onesb.t[0]), (p_[:, :], p_.t[0]),
                     start=(j == 0), stop=(j == 9), inc=True)
                if c == ncomp - 1 and j == 9:
                    out_cb(h, qh, obanks)

        def load_ext(sc, l, which, kT, V):
            k.dma("pool", "pool_extv", (V[:, 8:10, :], [V.t[8], V.t[9]]), ext_v[l, which].rearrange("(t p) n -> p t n", p=128))
            sc = ExitStack()
            ek = P("ek", [128, 2, 512], F32, scope=sc)
            k.dma("sp", "ek", (ek[:, :, :], ek.t[0]), ext_k[l, which].rearrange("(t p) n -> p t n", p=128))
            for t in range(2):
                bk, bt = k.bank("a")
                for h in range(4):
                    k.tr((bk[:, h * 128:(h + 1) * 128], bt), (ek[:, t, h * 128:(h + 1) * 128], ek.t[0]), ident)
                for h in range(4):
                    k.copy("dve" if h % 2 == 0 else "act", (kT[:, h, 1024 + t * 128:1024 + (t + 1) * 128], kT.t[h]),
                           (bk[:, h * 128:(h + 1) * 128], bt))
            sc.close()

        def mixer_phase(l):
            with ExitStack() as msc:
                oc_ = P("oT2", [128, 4, T], BF16, ntok=4, scope=msc)
                if "c" in KMIX:
                    rwkv_phase(l, oc_)
                else:
                    k.memset("dve", (oc_[:, :, :], oc_.t), 0.0)
                oT = [P("oT%d" % i, [128, 4, T], BF16, ntok=4, scope=msc) for i in range(2)] + [oc_]
                if "a" in KMIX:
                    with ExitStack() as sc:
                        qT = P("qT", [128, 4, T], BF16, ntok=4, scope=sc)
                        kT = P("kT", [128, 4, T + 256], BF16, ntok=4, scope=sc)
                        V = P("V", [128, 10, 512], BF16, ntok=10, scope=sc)
                        load_ext(sc, l, 0, kT, V)
                        def cq(ft, hf, bk):
                            k.act((qT[:, ft, hf * 512:(hf + 1) * 512], qT.t[ft]), bk, AF.Copy, scale=128 ** -0.5)
                        def ck(ft, hf, bk):
                            k.copy("dve", (kT[:, ft, hf * 512:(hf + 1) * 512], kT.t[ft]), bk)
                        stream(proj_fm(w_in[l], C_NAQ, 512, cq, src=hsrc) + proj_fm(w_in[l], C_NAK, 512, ck, src=hsrc))
                        proj_tm(l, C_NAK, 0)
                        proj_tm(l, C_NAV, 1, post=lambda tt, bk: k.copy("dve", (V[:, tt, :], V.t[tt]), bk))
                        def cb_a(h, qh, ob):
                            with ExitStack() as s2:
                                rc = P("rc", [128, 512], F32, scope=s2)
                                k.S.op("dve", lambda e: e.reciprocal(out=rc[:, :], in_=ob[1][0]), R=[ob[1][1]], W=[rc.t[0]])
                                k.tt("dve", (oT[0][:, h, qh * 512:(qh + 1) * 512], oT[0].t[h]), ob[0], (rc[:, :], rc.t[0]), ALU.mult)
                        attention(sc, qT, kT, V, 1, lambda h, qh, j: nab[l, h, qh, j], cb_a)
                else:
                    k.memset("dve", (oT[0][:, :, :], oT[0].t), 0.0)
                    proj_tm(l, C_NAK, 0); proj_tm(l, C_NAV, 1)
                if "b" in KMIX:
                    with ExitStack() as sc:
                        qT = P("dqT", [128, 4, T], BF16, ntok=4, scope=sc)
                        kT = P("dkT", [128, 4, T + 256], BF16, ntok=4, scope=sc)
                        V = P("dV", [128, 10, 512], BF16, ntok=10, scope=sc)
                        lam = P("lamt", [128, 8], F32, scope=sc)
                        sg = P("sgn", [128, 4], F32, scope=sc)
                        rsc = ExitStack()
                        cs = P("cs", [128, 2, T], F32, scope=rsc)
                        xr = [P("xr%d" % i, [128, 512], F32, scope=rsc) for i in range(2)]
                        t1 = [P("t1%d" % i, [128, 512], F32, scope=rsc) for i in range(2)]
                        k.dma("sp", "cs", (cs[:, :, :], cs.t[0]), rope.rearrange("c p t -> p c t"))
                        load_ext(sc, l, 1, kT, V)
                        lo = PV["lam"][0]
                        k.tt("dve", (lam[:, 0:1], lam.t[0]), (pv[:, lo:lo + 1], pv.t[0]), (pv[:, lo + 1:lo + 2], pv.t[0]), ALU.mult)
                        k.tt("dve", (lam[:, 1:2], lam.t[0]), (pv[:, lo + 2:lo + 3], pv.t[0]), (pv[:, lo + 3:lo + 4], pv.t[0]), ALU.mult)
                        bk, bt = k.bank("a")
                        k.mm((bk[:, 0:2], bt), onesf, (lam[:, 0:2], lam.t[0]))
                        k.act((lam[:, 2:4], lam.t[0]), (bk[:, 0:2], bt), AF.Exp)
                        lam_init = 0.8 - 0.6 * float(np.exp(-0.3 * l))
                        k.tt("dve", (lam[:, 4:5], lam.t[0]), (lam[:, 3:4], lam.t[0]), (lam[:, 2:3], lam.t[0]), ALU.subtract)
                        k.ts("dve", (lam[:, 5:6], lam.t[0]), (lam[:, 4:5], lam.t[0]), -lam_init, None, ALU.add)
                        so = PV["subln"][0]
                        k.ts("dve", (lam[:, 6:7], lam.t[0]), (pv[:, so:so + 1], pv.t[0]), 0.0, None, ALU.add)
                        k.ts("dve", (sg[:, :], sg.t[0]), (pv[:, so:so + 4], pv.t[0]), 1.0 - lam_init, None, ALU.mult)
                        ri = [0]
                        def mk(dst, scale):
                            def c_(ft, hf, bk):
                                i = ri[0] % 2; ri[0] += 1
                                sl = slice(hf * 512, (hf + 1) * 512)
                                k.act((xr[i][:, :], xr[i].t[0]), bk, AF.Copy, scale=scale)
                                rb = k.bank("b")
                                k.mm(rb, rotT, (xr[i][:, :], xr[i].t[0]))
                                k.tt("dve", (t1[i][:, :], t1[i].t[0]), (xr[i][:, :], xr[i].t[0]), (cs[:, 0, sl], cs.t[0]), ALU.mult)
                                k.tt("dve", (xr[i][:, :], xr[i].t[0]), rb, (cs[:, 1, sl], cs.t[0]), ALU.mult)
                                k.tt("dve", (dst[:, ft, sl], dst.t[ft]), (t1[i][:, :], t1[i].t[0]), (xr[i][:, :], xr[i].t[0]), ALU.add)
                            return c_
                        stream(proj_fm(w_in[l], C_DFQ, 512, mk(qT, 64 ** -0.5), src=hsrc) + proj_fm(w_in[l], C_DFK, 512, mk(kT, 1.0), src=hsrc))
                        rsc.close()
                        proj_tm(l, C_DFK, 2)
                        proj_tm(l, C_DFV, 3, post=lambda tt, bk: k.copy("dve", (V[:, tt, :], V.t[tt]), bk))
                        def cb_b(h, qh, ob):
                            with ExitStack() as s2:
                                r0 = P("r0", [128, 512], F32, scope=s2)
                                r1 = P("r1", [128, 512], F32, scope=s2)
                                sq = P("sqb", [128, 512], BF16, scope=s2)
                                k.S.op("dve", lambda e: e.reciprocal(out=r0[:, :], in_=ob[1][0]), R=[ob[1][1]], W=[r0.t[0]])
                                k.S.op("dve", lambda e: e.reciprocal(out=r1[:, :], in_=ob[3][0]), R=[ob[3][1]], W=[r1.t[0]])
                                k.tt("dve", (r0[:, :], r0.t[0]), ob[0], (r0[:, :], r0.t[0]), ALU.mult)
                                k.tt("dve", (r1[:, :], r1.t[0]), ob[2], (r1[:, :], r1.t[0]), ALU.mult)
                                k.stt("dve", (r0[:, :], r0.t[0]), (r1[:, :], r1.t[0]), (lam[:, 5:6], lam.t[0]), (r0[:, :], r0.t[0]), ALU.mult, ALU.add)
                                k.act((sq[:, :], sq.t[0]), (r0[:, :], r0.t[0]), AF.Square)
                                mb = k.bank("a")
                                k.mm(mb, (onesb[:, :], onesb.t[0]), (sq[:, :], sq.t[0]))
                                k.act((r1[:, :], r1.t[0]), mb, AF.Sqrt, bias=LN_EPS, scale=1.0 / 128)
                                k.S.op("dve", lambda e: e.reciprocal(out=r1[:, :], in_=r1[:, :]), W=[r1.t[0]])
                                k.tt("dve", (r0[:, :], r0.t[0]), (r0[:, :], r0.t[0]), (r1[:, :], r1.t[0]), ALU.mult)
                                k.ts("dve", (oT[1][:, h, qh * 512:(qh + 1) * 512], oT[1].t[h]), (r0[:, :], r0.t[0]),
                                     (sg[:, h:h + 1], sg.t[0]), None, ALU.mult)
                        attention(sc, qT, kT, V, 2, lambda h, qh, j: dmask[qh, j], cb_b)
                else:
                    k.memset("dve", (oT[1][:, :, :], oT[1].t), 0.0)
                    proj_tm(l, C_DFK, 2); proj_tm(l, C_DFV, 3)
                with ExitStack() as sc:
                    mT = P("mT", [128, KT, T], BF16, ntok=KT, scope=sc)
                    acc = P("acc", [128, 2, T], F32, ntok=2, scope=sc)
                    sg4 = [P("sg4%d" % i, [128, 512], F32, scope=sc) for i in range(4)]
                    jobs = []
                    for jj in range(8):
                        for i in range(3):
                            def ld(jj=jj, i=i):
                                return wload([(lambda b: wview(b, KT, 256), wsrc(w_in[l], C_GATE + i * D + jj * 256, 256))])
                            def ldb(jj=jj, i=i):
                                return wload([(lambda b: b.h[:, 0:1024].rearrange("p (k n) -> p k n", k=4),
                                               w_branch[l, i, :, jj * 256:(jj + 1) * 256].rearrange("(k p) n -> p k n", p=128))])
                            def useg(b, jj=jj, i=i):
                                wv = wview(b, KT, 256)
                                for u in range(4):
                                    j2, hf = u // 2, u % 2
                                    bk = k.bank("a")
                                    for kt in range(KT):
                                        k.mm(bk, (wv[:, kt, j2 * 128:(j2 + 1) * 128], b.t[0]), hsrc(kt, hf), start=(kt == 0), stop=(kt == KT - 1))
                                    k.act((sg4[u][:, :], sg4[u].t[0]), bk, AF.Sigmoid)
                            def useb(b, jj=jj, i=i):
                                wv = b.h[:, 0:1024].rearrange("p (k n) -> p k n", k=4)
                                for u in range(4):
                                    j2, hf = u // 2, u % 2
                                    sl = slice(hf * 512, (hf + 1) * 512)
                                    bk = k.bank("b")
                                    for k4 in range(4):
                                        k.mm(bk, (wv[:, k4, j2 * 128:(j2 + 1) * 128], b.t[0]), (oT[i][:, k4, sl], oT[i].t[k4]), start=(k4 == 0), stop=(k4 == 3))
                                    if i == 0:
                                        k.tt("dve", (acc[:, j2, sl], acc.t[j2]), (sg4[u][:, :], sg4[u].t[0]), bk, ALU.mult)
                                    else:
                                        t_ = sg4[u]
                                        k.tt("dve", (t_[:, :], t_.t[0]), (sg4[u][:, :], sg4[u].t[0]), bk, ALU.mult)
                                        if i == 1:
                                            k.tt("dve", (acc[:, j2, sl], acc.t[j2]), (acc[:, j2, sl], acc.t[j2]), (t_[:, :], t_.t[0]), ALU.add)
                                        else:
                                            k.tt("dve", (mT[:, jj * 2 + j2, sl], mT.t[jj * 2 + j2]), (acc[:, j2, sl], acc.t[j2]), (t_[:, :], t_.t[0]), ALU.add)
                            jobs.append((ld, useg))
                            jobs.append((ldb, useb))
                    stream(jobs)
                    def cres(ft, hf, bk):
                        sl = slice(hf * 512, (hf + 1) * 512)
                        k.stt("dve", (xT[:, ft, sl], xT.t[ft]), bk, (mod[:, g1c + ft:g1c + ft + 1], mod.t[0]),
                              (xT[:, ft, sl], xT.t[ft]), ALU.mult, ALU.add)
                    stream(proj_fm(w_out[l], 0, D, cres, src=lambda kt, hf: (mT[:, kt, hf * 512:(hf + 1) * 512], mT.t[kt])))

        CW = 0.6065306597126334
        bd = (cst[:, 128:256], cst.t[0])
        m_us = (cst[:, 256:384], cst.t[0]); m_ui = (cst[:, 384:512], cst.t[0])
        m_ls = (cst[:, 512:640], cst.t[0]); m_li = (cst[:, 640:768], cst.t[0])
        ch_st = S.chan()

        def rwkv_phase(l, ocT):
            pvc = lambda nm, j, n=1: (pv[:, PV[nm][0] + j:PV[nm][0] + j + n], pv.t[0])
            with ExitStack() as sc:
                tw = P("tw", [128, T], F32, scope=sc)
                adT = P("adT", [128, T], F32, scope=sc)
                sgd = P("sgd", [128, T], F32, scope=sc)
                w2t = P("w2t", [128, 512], F32, scope=sc)
                a2t = P("a2t", [128, 512], F32, scope=sc)
                g2t = P("g2t", [128, 512], F32, scope=sc)
                c0 = P("c0", [128, 16], F32, scope=sc)
                kf = P("kf", [128, 1], F32, scope=sc)
                Fr = P("Fr", [128, 4, 258], F32, scope=sc)
                k.dma("sp", "w2t", (w2t[:, :], w2t.t[0]), rw_w2[l])
                k.dma("sp", "a2t", (a2t[:, :], a2t.t[0]), rw_a2[l])
                k.dma("sp", "g2t", (g2t[:, :], g2t.t[0]), rw_g2[l])
                k.dma("sp", "kf", (kf[:, :], kf.t[0]), keepf)
                k.memset("dve", (Fr[:, :, :], Fr.t[0]), 0.0)
                m0, m1 = PV["mix0"][0], PV["mix1"][0]
                k.tt("dve", (c0[:, 0:15], c0.t[0]), (pv[:, m0:m0 + 15], pv.t[0]), (pv[:, m1:m1 + 15], pv.t[0]), ALU.add)
                k.ts("dve", (c0[:, 0:15], c0.t[0]), (c0[:, 0:15], c0.t[0]), -1.0, 1.0, ALU.mult, ALU.add)

                def shifted(tiles_dst):
                    for g0 in range(0, len(tiles_dst), 2):
                        grp = tiles_dst[g0:g0 + 2]
                        b = wload([(lambda b_, i=i: wview(b_, KT, 256)[:, :, i * 128:(i + 1) * 128],
                                    wsrc(w_in[l], C_RW + ft * 128, 128)) for i, (ft, _) in enumerate(grp)])
                        wv = wview(b, KT, 256)
                        for i, (ft, dst) in enumerate(grp):
                            for hf in range(2):
                                bk = k.bank("a")
                                for kt in range(KT):
                                    k.mm(bk, (wv[:, kt, i * 128:(i + 1) * 128], b.t[0]), hsrc(kt, hf), start=(kt == 0), stop=(kt == KT - 1))
                                k.S.op("act", lambda h, hf=hf, bk=bk: h.activation(
                                    out=Fr[:, 2 * hf:2 * hf + 2, 1:257], in_=bk[0].rearrange("p (s t) -> p s t", s=2), func=AF.Copy),
                                    R=[bk[1]], W=[Fr.t[0]])
                            k.ts("dve", (Fr[:, 1:4, 0:1], Fr.t[0]), (Fr[:, 0:3, 256:257], Fr.t[0]), (kf[:, 0:1], kf.t[0]), None, ALU.mult)
                            k.ts("dve", (Fr[:, 0:3, 257:258], Fr.t[0]), (Fr[:, 1:4, 1:2], Fr.t[0]), (kf[:, 0:1], kf.t[0]), None, ALU.mult)
                            d3 = dst[:, :].rearrange("p (s t) -> p s t", s=4)
                            k.ts("dve", (d3, dst.t[0]), (Fr[:, :, 1:257], Fr.t[0]), (c0[:, ft:ft + 1], c0.t[0]), None, ALU.mult)
                            k.stt("dve", (d3, dst.t[0]), (Fr[:, :, 0:256], Fr.t[0]), (pv[:, m0 + ft:m0 + ft + 1], pv.t[0]), (d3, dst.t[0]), ALU.mult, ALU.add)
                            k.stt("dve", (d3, dst.t[0]), (Fr[:, :, 2:258], Fr.t[0]), (pv[:, m1 + ft:m1 + ft + 1], pv.t[0]), (d3, dst.t[0]), ALU.mult, ALU.add)

                shifted([(12, tw), (13, adT)])
                shifted([(14, sgd)])
                k.act((tw[:, :], tw.t[0]), (tw[:, :], tw.t[0]), AF.Tanh)
                k.act((sgd[:, :], sgd.t[0]), (sgd[:, :], sgd.t[0]), AF.Sigmoid)

                for p in range(4):
                    with ExitStack() as ps_:
                        rT = P("rT", [128, T], F32, scope=ps_)
                        kT_ = P("kTr", [128, T], F32, scope=ps_)
                        vT = P("vT", [128, T], F32, scope=ps_)
                        kkT = P("kkT", [128, T], F32, scope=ps_)
                        yacc = P("yacc", [128, T], F32, scope=ps_)
                        Vtok = P("Vtok", [128, 8, 128], F32, ntok=8, scope=ps_)
                        shifted([(p, rT), (4 + p, kT_)])
                        shifted([(8 + p, vT)])
                        k.ts("dve", (kkT[:, :], kkT.t[0]), (kT_[:, :], kT_.t[0]), pvc("kk", p), None, ALU.mult)
                        k.act((yacc[:, :], yacc.t[0]), (kkT[:, :], kkT.t[0]), AF.Square)
                        for hf in range(2):
                            sl = slice(hf * 512, (hf + 1) * 512)
                            bk = k.bank("a")
                            k.mm(bk, bd, (yacc[:, sl], yacc.t[0]))
                            k.ts("dve", (yacc[:, sl], yacc.t[0]), bk, 1e-24, None, ALU.max)
                        k.act((yacc[:, :], yacc.t[0]), (yacc[:, :], yacc.t[0]), AF.Sqrt)
                        k.S.op("dve", lambda h: h.reciprocal(out=yacc[:, :], in_=yacc[:, :]), W=[yacc.t[0]])
                        k.tt("dve", (kkT[:, :], kkT.t[0]), (kkT[:, :], kkT.t[0]), (yacc[:, :], yacc.t[0]), ALU.mult)
                        for g in range(2):
                            bk, bt = k.bank("a")
                            for j in range(4):
                                c = g * 4 + j
                                k.tr((bk[:, j * 128:(j + 1) * 128], bt), (vT[:, c * 128:(c + 1) * 128], vT.t[0]), ident)
                            k.copy("act", (Vtok[:, g * 4:(g + 1) * 4, :], Vtok.t[g * 4:(g + 1) * 4]), (bk.rearrange("p (c f) -> p c f", c=4), bt))

                        def asig_half(d, hf, dst):
                            rows = slice(64 * d, 64 * d + 64)
                            bk = k.bank("a")
                            k.mm(bk, (a2t[rows, p * 128:(p + 1) * 128], a2t.t[0]), (adT[rows, hf * 512:(hf + 1) * 512], adT.t[0]))
                            k.act(dst, bk, AF.Sigmoid, bias=pvc("a0", d * 4 + p))

                        for d in range(2):
                            with ExitStack() as ds_:
                                S0 = [P("S0%d" % i, [128, 128], F32, scope=ds_) for i in range(2)]
                                Up = [P("Up%d" % i, [128, 2, 128], F32, scope=ds_) for i in range(2)]
                                Vp = [P("Vp%d" % i, [128, 2, 128], F32, scope=ds_) for i in range(2)]
                                wful = P("wful", [128, 8], F32, scope=ds_)
                                tot = P("tot", [128, 4], F32, scope=ds_)
                                for i in range(2):
                                    k.memset("dve", (Up[i][:, :, :], Up[i].t[0]), 0.0)
                                    k.memset("dve", (Vp[i][:, :, :], Vp[i].t[0]), 0.0)
                                k.dma("sp", "S0", (S0[0][:, :], S0[0].t[0]), s0_in[l, d, p])
                                scur = 0
                                rows_d = slice(64 * d, 64 * d + 64)
                                for hf in ((0, 1) if d == 0 else (1, 0)):
                                    sl = slice(hf * 512, (hf + 1) * 512)
                                    with ExitStack() as hs_:
                                        HB = lambda nm: P(nm, [128, 512], F32, scope=hs_)
                                        btT, ktT, atT, rtT = HB("btT"), HB("ktT"), HB("atT"), HB("rtT")
                                        Tk = P("Tk", [128, 4, 3, 128], F32, ntok=4, scope=hs_)
                                        with ExitStack() as ts_:
                                            x1, x2 = P("x1", [128, 512], F32, scope=ts_), P("x2", [128, 512], F32, scope=ts_)
                                            A = lambda b_: (b_[:, :], b_.t[0])
                                            asig_half(d, hf, A(x1))
                                            k.tt("dve", A(btT), (kkT[:, sl], kkT.t[0]), A(x1), ALU.mult)
                                            k.ts("dve", A(x1), A(x1), pvc("ka", p), None, ALU.mult)
                                            k.S.op("dve", lambda h: h.tensor_scalar(out=x1[:, :], in0=x1[:, :], scalar1=pv[:, PV["ka"][0] + p:PV["ka"][0] + p + 1],
                                                                                   scalar2=1.0, op0=ALU.subtract, op1=ALU.add), R=[pv.t[0]], W=[x1.t[0]])
                                            k.tt("dve", A(ktT), (kT_[:, sl], kT_.t[0]), A(x1), ALU.mult)
                                            bk = k.bank("a")
                                            k.mm(bk, (w2t[rows_d, p * 128:(p + 1) * 128], w2t.t[0]), (tw[rows_d, sl], tw.t[0]))
                                            k.act(A(x1), bk, AF.Sigmoid, bias=pvc("w0", d * 4 + p))
                                            for c in range(4):
                                                cs_ = slice(c * 128, (c + 1) * 128)
                                                k.S.op("dve", lambda h, cs_=cs_: h.tensor_tensor_scan(out=x2[:, cs_], data0=onesf[0][:, 0:128], data1=x1[:, cs_],
                                                                                                   initial=0.0, op0=ALU.mult, op1=ALU.add),
                                                       R=[x1.t[0], cst.t[0]], W=[x2.t[0]])
                                            if d == 0:
                                                k.tt("dve", A(x1), A(x2), A(x1), ALU.subtract)
                                            else:
                                                k.copy("dve", (tot[:, 0:4], tot.t[0]), (x2[:, :].rearrange("p (c t) -> p c t", c=4)[:, :, 127], x2.t[0]))
                                                for c in range(4):
                                                    cs_ = slice(c * 128, (c + 1) * 128)
                                                    k.S.op("dve", lambda h, cs_=cs_, c=c: h.tensor_scalar(out=x2[:, cs_], in0=x2[:, cs_], scalar1=tot[:, c:c + 1],
                                                                                                         scalar2=-1.0, op0=ALU.subtract, op1=ALU.mult), R=[tot.t[0]], W=[x2.t[0]])
                                                k.tt("dve", A(atT), A(x2), A(x1), ALU.add)
                                                k.copy("dve", A(x1), A(x2))
                                                k.copy("dve", A(x2), A(atT))
                                            k.act(A(atT), A(x1), AF.Exp, scale=-CW)
                                            k.stt("dve", A(atT), A(atT), -1.0, (kkT[:, sl], kkT.t[0]), ALU.mult, ALU.mult)
                                            k.act(A(rtT), A(x2), AF.Exp, scale=-CW)
                                            wcol = 127 if d == 0 else 0
                                            k.copy("dve", (wful[:, hf * 4:(hf + 1) * 4], wful.t[0]),
                                                   (rtT[:, :].rearrange("p (c t) -> p c t", c=4)[:, :, wcol], rtT.t[0]))
                                            k.tt("dve", A(rtT), A(rtT), (rT[:, sl], rT.t[0]), ALU.mult)
                                            k.act(A(x1), A(x2), AF.Exp, scale=CW)
                                            k.tt("dve", A(btT), A(btT), A(x1), ALU.mult)
                                            k.tt("dve", A(ktT), A(ktT), A(x1), ALU.mult)
                                        for c in range(4):
                                            cs_ = slice(c * 128, (c + 1) * 128)
                                            bk, bt = k.bank("a")
                                            for i, src_ in enumerate((btT, ktT, atT)):
                                                k.tr((bk[:, i * 128:(i + 1) * 128], bt), (src_[:, cs_], src_.t[0]), ident)
                                            k.copy("act", (Tk[:, c, :, :], Tk.t[c]), (bk[:, 0:384].rearrange("p (i f) -> p i f", i=3), bt))
                                        with ExitStack() as cs2:
                                            Mh = [P("Mh%d" % i, [128, 512], F32, scope=cs2) for i in range(2)]
                                            ZB = [P("ZB%d" % i, [128, 384], F32, scope=cs2) for i in range(2)]
                                            AtT = P("AtT", [128, 128], F32, scope=cs2)
                                            tS = P("tS", [128, 128], F32, scope=cs2)
                                            mS, mI = (m_us, m_ui) if d == 0 else (m_ls, m_li)
                                            mN = m_ls if d == 0 else m_us
                                            for c in ((0, 1, 2, 3) if d == 0 else (3, 2, 1, 0)):
                                                gc = hf * 4 + c
                                                cs_ = slice(c * 128, (c + 1) * 128)
                                                vp = Vp[gc % 2]; up = Up[gc % 2]
                                                k.copy("dve", (vp[:, 0, 0:64], vp.t[0]), (Vtok[:, gc, 0:64], Vtok.t[gc]))
                                                k.copy("dve", (vp[:, 1, 64:128], vp.t[0]), (Vtok[:, gc, 64:128], Vtok.t[gc]))
                                                for e in range(2):
                                                    rw_ = slice(64 * e, 64 * e + 64)
                                                    M_, Z_ = Mh[e], ZB[e]
                                                    b1 = k.bank("a")
                                                    for i, (lh, rh) in enumerate(((btT, atT), (ktT, atT), (btT, rtT), (ktT, rtT))):
                                                        k.mm((b1[0][:, i * 128:(i + 1) * 128], b1[1]), (lh[rw_, cs_], lh.t[0]), (rh[rw_, cs_], rh.t[0]),
                                                             inc=(i == 3))
                                                    k.tt("dve", (M_[:, 0:128], M_.t[0]), (b1[0][:, 0:128], b1[1]), mS, ALU.mult)
                                                    k.tt("dve", (M_[:, 128:256], M_.t[0]), (b1[0][:, 128:256], b1[1]), mS, ALU.mult)
                                                    k.tt("dve", (M_[:, 256:384], M_.t[0]), (b1[0][:, 256:384], b1[1]), mI, ALU.mult)
                                                    k.tt("dve", (M_[:, 384:512], M_.t[0]), (b1[0][:, 384:512], b1[1]), mI, ALU.mult)
                                                    b2 = k.bank("b")
                                                    k.mm((b2[0][:, 0:128], b2[1]), (atT[rw_, cs_], atT.t[0]), (btT[rw_, cs_], btT.t[0]), inc=False)
                                                    k.mm((b2[0][:, 128:192], b2[1]), (M_[:, 128:256], M_.t[0]), (Vtok[:, gc, 64 * e:64 * e + 64], Vtok.t[gc]))
                                                    k.tt("dve", (Z_[:, 128:256], Z_.t[0]), (b2[0][:, 0:128], b2[1]), mN, ALU.mult)
                                                    k.copy("act", (Z_[:, 64:128], Z_.t[0]), (b2[0][:, 128:192], b2[1]))
                                                    k.copy("act", (Z_[:, 0:64], Z_.t[0]), (Tk[:, c, 2, 64 * e:64 * e + 64], Tk.t[c]))
                                                    k.copy("act", (Z_[:, 256:384], Z_.t[0]), (M_[:, 0:128], M_.t[0]))
                                                    for lv in range(7):
                                                        b3 = k.bank("b")
                                                        if lv < 6:
                                                            k.mm((b3[0][:, 0:256], b3[1]), (Z_[:, 256:384], Z_.t[0]), (Z_[:, 0:256], Z_.t[0]), inc=False)
                                                            k.mm((b3[0][:, 256:384], b3[1]), (Z_[:, 128:256], Z_.t[0]), (Z_[:, 256:384], Z_.t[0]))
                                                            k.tt("dve", (Z_[:, 0:128], Z_.t[0]), (Z_[:, 0:128], Z_.t[0]), (b3[0][:, 0:128], b3[1]), ALU.add)
                                                            k.copy("act", (Z_[:, 128:384], Z_.t[0]), (b3[0][:, 128:384], b3[1]))
                                                        else:
                                                            k.mm((b3[0][:, 0:128], b3[1]), (Z_[:, 256:384], Z_.t[0]), (Z_[:, 0:128], Z_.t[0]))
                                                            k.tt("dve", (Z_[:, 0:128], Z_.t[0]), (Z_[:, 0:128], Z_.t[0]), (b3[0][:, 0:128], b3[1]), ALU.add)
                                                for e in range(2):
                                                    k.copy("act", (tS[:, 64 * e:64 * e + 64], tS.t[0]), (ZB[e][:, 0:64], ZB[e].t[0]))
                                                b4 = k.bank("a")
                                                k.tr((b4[0][:, 0:128], b4[1]), (tS[:, :], tS.t[0]), ident)
                                                k.copy("act", (AtT[:, :], AtT.t[0]), (b4[0][:, 0:128], b4[1]))
                                                s0t = S0[scur % 2]; s1t = S0[(scur + 1) % 2]
                                                b5 = k.bank("a")
                                                k.mm((b5[0][:, 0:128], b5[1]), (AtT[:, :], AtT.t[0]), (s0t[:, :], s0t.t[0]))
                                                for e in range(2):
                                                    k.tt("dve", (up[:, e, 64 * e:64 * e + 64], up.t[0]), (b5[0][:, 64 * e:64 * e + 64], b5[1]),
                                                         (ZB[e][:, 64:128], ZB[e].t[0]), ALU.add)
                                                b6 = k.bank("b")
                                                k.mm((b6[0][:, 0:128], b6[1]), (s0t[:, :], s0t.t[0]), (rtT[:, cs_], rtT.t[0]), start=True, stop=False, inc=False)
                                                for e in range(2):
                                                    k.mm((b6[0][:, 0:128], b6[1]), (up[:, e, :], up.t[0]), (Mh[e][:, 256:384], Mh[e].t[0]), start=False, stop=False, inc=False)
                                                    k.mm((b6[0][:, 0:128], b6[1]), (vp[:, e, :], vp.t[0]), (Mh[e][:, 384:512], Mh[e].t[0]), start=False, stop=(e == 1), inc=(e == 1))
                                                ys = (yacc[:, gc * 128:(gc + 1) * 128], yacc.t[0])
                                                if d == 0:
                                                    k.copy("act", ys, (b6[0][:, 0:128], b6[1]))
                                                else:
                                                    k.tt("dve", ys, ys, (b6[0][:, 0:128], b6[1]), ALU.add)
                                                b7 = k.bank("a")
                                                k.mm((b7[0][:, 0:128], b7[1]), (Tk[:, c, 0, :], Tk.t[c]), (up[:, 0, :], up.t[0]), start=True, stop=False, inc=False)
                                                k.mm((b7[0][:, 0:128], b7[1]), (Tk[:, c, 0, :], Tk.t[c]), (up[:, 1, :], up.t[0]), start=False, stop=False, inc=False)
                                                k.mm((b7[0][:, 0:128], b7[1]), (Tk[:, c, 1, :], Tk.t[c]), (Vtok[:, gc, :], Vtok.t[gc]), start=False, stop=True)
                                                k.tt("dve", (tS[:, :], tS.t[0]), (b7[0][:, 0:128], b7[1]), (s0t[:, :], s0t.t[0]), ALU.add)
                                                k.stt("dve", (s1t[:, :], s1t.t[0]), (tS[:, :], tS.t[0]), (wful[:, gc:gc + 1], wful.t[0]), bd, ALU.mult, ALU.mult)
                                                scur += 1
                                                seq_end = (gc % 2 == 1) if d == 0 else (gc % 2 == 0)
                                                if seq_end:
                                                    k.dma("sp", "o_S0_%d" % ((scur) % 2), st_out[l, d, gc // 2, p], (s1t[:, :], s1t.t[0]))
                                                    k.ts("dve", (s1t[:, :], s1t.t[0]), (s1t[:, :], s1t.t[0]), (kf[:, 0:1], kf.t[0]), None, ALU.mult)
                        with ExitStack() as gs_:
                            HB = lambda nm: P(nm, [128, 512], F32, scope=gs_)
                            mu_, rs_, u_, ts_ = HB("gmu"), HB("grs"), HB("gu"), HB("gts")
                            A = lambda b_: (b_[:, :], b_.t[0])
                            for hf in range(2):
                                sl = slice(hf * 512, (hf + 1) * 512)
                                ya = (yacc[:, sl], yacc.t[0])
                                b1 = k.bank("a")
                                k.mm(b1, bd, ya)
                                k.act(A(u_), ya, AF.Square)
                                b2 = k.bank("a")
                                k.mm(b2, bd, A(u_))
                                k.ts("dve", A(mu_), b1, 1.0 / 64, None, ALU.mult)
                                k.tt("dve", A(rs_), A(mu_), A(mu_), ALU.mult)
                                k.stt("dve", A(rs_), b2, 1.0 / 64, A(rs_), ALU.mult, ALU.subtract)
                                k.act(A(rs_), A(rs_), AF.Sqrt, bias=GN_EPS)
                                k.S.op("dve", lambda h: h.reciprocal(out=rs_[:, :], in_=rs_[:, :]), W=[rs_.t[0]])
                                k.tt("dve", ya, ya, A(mu_), ALU.subtract)
                                k.tt("dve", ya, ya, A(rs_), ALU.mult)
                                k.S.op("dve", lambda h, sl=sl: h.tensor_scalar(out=yacc[:, sl], in0=yacc[:, sl], scalar1=pvc("lnx_g", p)[0], scalar2=pvc("lnx_b", p)[0],
                                                                            op0=ALU.mult, op1=ALU.add), R=[pv.t[0]], W=[yacc.t[0]])
                                asig_half(0, hf, A(ts_))
                                asig_half(1, hf, A(u_))
                                k.tt("dve", A(ts_), A(ts_), A(u_), ALU.add)
                                k.ts("dve", A(ts_), A(ts_), pvc("ka", p), None, ALU.mult)
                                k.S.op("dve", lambda h: h.tensor_scalar(out=ts_[:, :], in0=ts_[:, :], scalar1=pvc("ka", p)[0], scalar2=1.0, op0=ALU.subtract, op1=ALU.add),
                                       R=[pv.t[0]], W=[ts_.t[0]])
                                k.S.op("dve", lambda h: h.tensor_scalar(out=ts_[:, :], in0=ts_[:, :], scalar1=pvc("ka", p)[0], scalar2=1.0, op0=ALU.subtract, op1=ALU.add),
                                       R=[pv.t[0]], W=[ts_.t[0]])
                                k.tt("dve", A(u_), (rT[:, sl], rT.t[0]), (kT_[:, sl], kT_.t[0]), ALU.mult)
                                k.stt("dve", A(u_), A(u_), pvc("rk", p), A(ts_), ALU.mult, ALU.mult)
                                b3 = k.bank("a")
                                k.mm(b3, bd, A(u_))
                                k.tt("dve", A(u_), b3, (vT[:, sl], vT.t[0]), ALU.mult)
                                k.tt("dve", ya, ya, A(u_), ALU.add)
                                b4 = k.bank("a")
                                k.mm(b4, (g2t[:, p * 128:(p + 1) * 128], g2t.t[0]), (sgd[:, sl], sgd.t[0]))
                                k.tt("dve", (ocT[:, p, sl], ocT.t[p]), ya, b4, ALU.mult)

        g1c, g2c = 32, 80
        try:
          chk(1)
          for l in range(NL):
            k.dma("sp", "pv", (pv[:, :], pv.t[0]), pvec[l])
            adaln(l)
            chk(2)
            modulate(0)
            chk(3)
            mixer_phase(l)
            chk(4)
            layernorm(0, ALPHA)
            modulate(1)
            chk(5)
            if stage >= 2:
                with ExitStack() as sc:
                    hid = [P("hid%d" % i, [128, 11, T], BF16, ntok=11, scope=sc) for i in range(2)]
                    sg = [P("sg%d" % i, [128, 512], F32, scope=sc) for i in range(2)]
                    sgi = [0]
                    for ps in range(4):
                        hb = hid[ps % 2]
                        jobs = []
                        for j in range(11):
                            ht = ps * 11 + j
                            def ld(ht=ht):
                                return wload([(lambda b: wview(b, KT, 256)[:, :, 0:128], wsrc(w_f1[l], ht * 128, 128)),
                                              (lambda b: wview(b, KT, 256)[:, :, 128:256], wsrc(w_f1[l], FH + ht * 128, 128))])
                            def use(b, j=j, hb=hb):
                                wv = wview(b, KT, 256)
                                for hf in range(2):
                                    bg = k.bank("a"); bu = k.bank("a")
                                    for kt in range(KT):
                                        k.mm(bg, (wv[:, kt, 0:128], b.t[0]), hsrc(kt, hf), start=(kt == 0), stop=(kt == KT - 1))
                                    for kt in range(KT):
                                        k.mm(bu, (wv[:, kt, 128:256], b.t[0]), hsrc(kt, hf), start=(kt == 0), stop=(kt == KT - 1))
                                    s_ = sg[sgi[0] % 2]; sgi[0] += 1
                                    k.act((s_[:, :], s_.t[0]), bg, AF.Silu)
                                    k.tt("dve", (hb[:, j, hf * 512:(hf + 1) * 512], hb.t[j]), (s_[:, :], s_.t[0]), bu, ALU.mult)
                            jobs.append((ld, use))
                        for cg in range(8):
                            def ld2(cg=cg, ps=ps):
                                return wload([(lambda b: wview(b, 11, 256), wsrc(w_f2[l], cg * 256, 256, ps * 11, 11))])
                            def use2(b, cg=cg, hb=hb):
                                wv = wview(b, 11, 256)
                                for jj in range(2):
                                    kt_o = cg * 2 + jj
                                    for hf in range(2):
                                        bk = k.bank("a")
                                        for j in range(11):
                                            k.mm(bk, (wv[:, j, jj * 128:(jj + 1) * 128], b.t[0]),
                                                 (hb[:, j, hf * 512:(hf + 1) * 512], hb.t[j]), start=(j == 0), stop=(j == 10))
                                        sl = slice(hf * 512, (hf + 1) * 512)
                                        k.stt("dve", (xT[:, kt_o, sl], xT.t[kt_o]), bk, (mod[:, g2c + kt_o:g2c + kt_o + 1], mod.t[0]),
                                              (xT[:, kt_o, sl], xT.t[kt_o]), ALU.mult, ALU.add)
                            jobs.append((ld2, use2))
                        stream(jobs)
            layernorm(1, ALPHA if l < NL - 1 else 1.0)
        except _Stop:
            pass

        with ExitStack() as sc:
            yst = [P("yst%d" % i, [128, D], F32, scope=sc) for i in range(2)]
            for tt in range(8):
                st = yst[tt % 2]
                for kq in range(4):
                    bk, bt = k.bank("a")
                    for j in range(4):
                        kt = kq * 4 + j
                        k.tr((bk[:, j * 128:(j + 1) * 128], bt), (xT[:, kt, tt * 128:(tt + 1) * 128], xT.t[kt]), ident)
                    k.copy("dve" if kq % 2 == 0 else "act", (st[:, kq * 512:(kq + 1) * 512], st.t[0]), (bk[:, :], bt))
                k.dma("sp", "o_yst%d" % (tt % 2), y_out[tt * 128:(tt + 1) * 128, :], (st[:, :], st.t[0]))
        for c_ in S.chans:
            if c_.val > 0:
                S.E["sp"].h.wait_ge(c_.sem, c_.val)
    return nc


def _fm(v):
    v = np.asarray(v, np.float32)
    n = v.shape[-1] // 128
    return np.ascontiguousarray(np.swapaxes(v.reshape(v.shape[:-1] + (n, 128)), -1, -2))


def _consts():
    c = np.zeros((128, 1024), np.float32)
    p = np.arange(128)[:, None]
    f = np.arange(128)[None, :]
    c[:, 0:128] = (p == f)
    c[:, 128:256] = ((p // 64) == (f // 64))
    c[:, 256:384] = (f > p)
    c[:, 384:512] = (f >= p)
    c[:, 512:640] = (f < p)
    c[:, 640:768] = (f <= p)
    c[:, 768:896] = 1.0
    for m in range(128):
        if m % 32 < 16:
            c[m + 16, 896 + m] = -1.0
        else:
            c[m - 16, 896 + m] = 1.0
    return c


def _pvec(inp):
    out = np.zeros((L, 128, NPV), np.float32)
    def put(nm, arr):
        o, w = PV[nm]
        out[:, :, o:o + w] = arr.reshape(L, 128, w)
    put("b_ada", _fm(inp["b_ada"].reshape(L, 6, D)).transpose(0, 2, 1, 3))
    for nm in ("ln1_g", "ln1_b", "ln2_g", "ln2_b"):
        put(nm, _fm(inp[nm]))
    put("subln", _fm(inp["diff_subln"]))
    put("mix0", _fm(inp["rwkv_mix"][:, 0]))
    put("mix1", _fm(inp["rwkv_mix"][:, 1]))
    put("a0", _fm(inp["rwkv_a0"]).transpose(0, 2, 1, 3))
    put("w0", _fm(inp["rwkv_w0"]).transpose(0, 2, 1, 3))
    put("kk", _fm(inp["rwkv_kk"]))
    put("ka", _fm(inp["rwkv_ka"]))
    put("rk", _fm(inp["rwkv_rk"].reshape(L, 512)))
    put("lnx_g", _fm(inp["rwkv_lnx_g"]))
    put("lnx_b", _fm(inp["rwkv_lnx_b"]))
    lam = np.zeros((L, 128, 4), np.float32)
    lam[:, 0:64, :] = np.swapaxes(np.asarray(inp["diff_lambda"], np.float32), 1, 2)
    put("lam", lam)
    return out


def _na_index():
    q = np.arange(T); rq, cq = q // 64, q % 64
    kk = np.arange(T); rk, ck = kk // 64, kk % 64
    row_start = np.clip(rq - 4, 0, 16 - 8)
    win_c0 = np.clip(cq - 8, 0, 64 - 16)
    inw = ((rk[None, :] >= row_start[:, None]) & (rk[None, :] < row_start[:, None] + 8) &
           (ck[None, :] >= win_c0[:, None]) & (ck[None, :] < win_c0[:, None] + 16))
    rr = np.clip(rk[None, :] - rq[:, None] + 7, 0, 14)
    rc = np.clip(ck[None, :] - cq[:, None], -15, 15) + 15
    return inw, rr, rc


def _tiles(bias_qk):
    b = np.swapaxes(bias_qk, -1, -2)
    sh = b.shape[:-2]
    b = b.reshape(sh + (10, 128, 2, 512))
    return np.ascontiguousarray(np.moveaxis(b, -2, -4))


def _ctx_mask():
    m = np.full((T, T + 256), NEG, np.float32)
    for s_ in range(4):
        m[s_ * 256:(s_ + 1) * 256, s_ * 256:(s_ + 1) * 256] = 0.0
    return m


def _rope_tables():
    t = np.arange(T)
    rows = (t // 64).astype(np.float32); cols = (t % 64).astype(np.float32)
    half = 32
    inv = (10000.0 ** (-np.arange(0, half, 2, dtype=np.float32) / half)).astype(np.float32)
    ang_r = rows[:, None] * inv; ang_c = cols[:, None] * inv
    ang = np.concatenate([ang_r, ang_r, ang_c, ang_c], -1)
    ang = np.concatenate([ang, ang], -1).T
    return np.stack([np.cos(ang), np.sin(ang)], 0).astype(np.float32)


_CACHE = {}


def kernel(**inp):
    inp = {k_: np.asarray(v) for k_, v in inp.items()}
    if "nc" not in _CACHE:
        nc = bass.Bass("TRN2", target_bir_lowering=False)
        build(nc)
        _CACHE["nc"] = nc
    nc = _CACHE["nc"]
    consts = _consts()
    pvec = _pvec(inp)
    shared = {"consts": consts, "pvec": pvec[:NL]}
    for nm in ("w_ada", "w_in", "w_branch", "w_out", "w_ffn_in", "w_ffn_out"):
        shared[nm] = np.ascontiguousarray(inp[nm][:NL], dtype=np.float32)
    inw, rr, rc = _na_index()
    rpb = np.asarray(inp["na_rpb"], np.float32)[:NL]
    lat_bias = np.zeros((NL, 4, T, T + 256), np.float32)
    lat_bias[:, :, :, :T] = np.where(inw[None, None], rpb[:, :, rr, rc], np.float32(NEG))
    nab_lat = _tiles(lat_bias)
    cm = _ctx_mask()
    nab_ctx = np.ascontiguousarray(np.broadcast_to(_tiles(cm)[None, None], (NL, 4, 2, 10, 128, 512)))
    dm_lat = np.zeros((2, 10, 128, 512), np.float32)
    dm_ctx = _tiles(cm)
    rope_lat = _rope_tables()
    rope_ctx = np.stack([np.ones((128, T), np.float32), np.zeros((128, T), np.float32)], 0)
    zkv = np.zeros((NL, 2, 256, 512), np.float32)
    shared["rw_w2"] = np.ascontiguousarray(inp["rwkv_w2"][:NL].reshape(NL, 128, 512), dtype=np.float32)
    shared["rw_a2"] = np.ascontiguousarray(inp["rwkv_a2"][:NL].reshape(NL, 128, 512), dtype=np.float32)
    shared["rw_g2"] = np.ascontiguousarray(inp["rwkv_g2"][:NL], dtype=np.float32)
    zs0 = np.zeros((NL, 2, 4, 128, 128), np.float32)
    in_maps = []
    for c in range(8):
        m = dict(shared)
        if c < 4:
            m["x_in"] = np.ascontiguousarray(inp["x_sample"][c])
            m["cond"] = _fm(inp["c"][c])
            m["nab"] = nab_lat; m["dmask"] = dm_lat; m["rope"] = rope_lat
            m["ext_k"] = np.ascontiguousarray(np.stack([inp["cache_na_k"][c, :NL].reshape(NL, 256, 512),
                                                         inp["cache_diff_k"][c, :NL].reshape(NL, 256, 512)], 1))
            m["ext_v"] = np.ascontiguousarray(np.stack([inp["cache_na_v"][c, :NL].reshape(NL, 256, 512),
                                                         inp["cache_diff_v"][c, :NL].reshape(NL, 256, 512)], 1))
            s0 = np.zeros((NL, 2, 4, 128, 128), np.float32)
            for d_, nm in enumerate(("state_rwkv_fwd", "state_rwkv_bwd")):
                st = np.asarray(inp[nm][c, :NL], np.float32)
                for h in range(8):
                    e = h % 2
                    s0[:, d_, h // 2, 64 * e:64 * e + 64, 64 * e:64 * e + 64] = np.swapaxes(st[:, h], -1, -2)
            m["s0"] = s0
            m["keepf"] = np.ones((128, 1), np.float32)
        else:
            j = c - 4
            m["x_in"] = np.ascontiguousarray(inp["x_prompt"][4 * j:4 * j + 4].reshape(T, D))
            m["cond"] = _fm(inp["c_ctx"])
            m["nab"] = nab_ctx; m["dmask"] = dm_ctx; m["rope"] = rope_ctx
            m["ext_k"] = zkv; m["ext_v"] = zkv
            m["s0"] = zs0
            m["keepf"] = np.zeros((128, 1), np.float32)
        in_maps.append(m)
    if os.environ.get("KSIM"):
        return nc, in_maps
    res = run_bass_kernel_spmd(nc, in_maps, core_ids=list(range(8)))
    R = res.results
    y_sample = np.stack([R[c]["y"] for c in range(4)], 0)
    y_prompt = np.concatenate([R[c]["y"].reshape(4, 256, D) for c in range(4, 8)], 0)
    kv = np.concatenate([R[c]["kv"].reshape(NL, 4, 4, 256, 512) for c in range(4, 8)], 2)
    kv = kv.transpose(1, 2, 0, 3, 4)
    new_na_k = np.ascontiguousarray(kv[0]).reshape(16, NL, 256, 4, 128)
    new_na_v = np.ascontiguousarray(kv[1]).reshape(16, NL, 256, 4, 128)
    new_diff_k = np.ascontiguousarray(kv[2]).reshape(16, NL, 256, 4, 2, 64)
    new_diff_v = np.ascontiguousarray(kv[3]).reshape(16, NL, 256, 4, 128)
    st = np.concatenate([R[c]["st"] for c in range(4, 8)], 2)
    fin = np.zeros((2, 16, NL, 8, 64, 64), np.float32)
    for h in range(8):
        e = h % 2
        blk = st[:, :, :, h // 2, 64 * e:64 * e + 64, 64 * e:64 * e + 64]
        fin[:, :, :, h] = np.transpose(blk, (1, 2, 0, 4, 3))
    return (y_prompt, y_sample, new_na_k, new_na_v, new_diff_k, new_diff_v, fin[0], fin[1])
```

```python
import os
import numpy as np
from contextlib import ExitStack
import concourse.bass as bass
import concourse.mybir as mybir
from concourse.bass_utils import run_bass_kernel_spmd

F32 = mybir.dt.float32
BF16 = mybir.dt.bfloat16
AF = mybir.ActivationFunctionType
ALU = mybir.AluOpType

D = 2048
L = 4
NL = int(os.environ.get('KL', '4'))
T = 1024
KT = 16
NIN = 11136
FH = 5632
ALPHA = float((2.0 * L) ** 0.25)
LN_EPS = 1e-5
GN_EPS = 64e-5
NEG = -30000.0
C_NAQ, C_NAK, C_NAV, C_DFQ, C_DFK, C_DFV, C_RW, C_GATE = 0, 512, 1024, 1536, 2048, 2560, 3072, 4992

PV = {}
_o = 0
for _n, _w in (("b_ada", 96), ("ln1_g", 16), ("ln1_b", 16), ("ln2_g", 16), ("ln2_b", 16),
               ("subln", 4), ("mix0", 15), ("mix1", 15), ("a0", 8), ("kk", 4), ("ka", 4),
               ("rk", 4), ("lnx_g", 4), ("lnx_b", 4), ("lam", 4), ("w0", 8)):
    PV[_n] = (_o, _w)
    _o += _w
NPV = _o


class _Stop(Exception):
    pass


def chk(n):
    if int(os.environ.get("KSTOP", "0")) == n:
        raise _Stop()


class Tok:
    __slots__ = ("w", "r", "excl")

    def __init__(self, init_r=None, excl=False):
        self.w = None
        self.r = dict(init_r) if init_r else {}
        self.excl = excl


class Eng:
    def __init__(self, nm, h, sem):
        self.nm, self.h, self.sem = nm, h, sem
        self.cnt = 0
        self.seen = {}


class Chan:
    def __init__(self, sem):
        self.sem = sem
        self.val = 0


class Sched:
    def __init__(self, nc, es):
        self.nc, self.es = nc, es
        self.E = {}
        for nm, h in (("pe", nc.tensor), ("act", nc.scalar), ("dve", nc.vector),
                      ("pool", nc.gpsimd), ("sp", nc.sync)):
            self.E[nm] = Eng(nm, h, es.enter_context(nc.semaphore("sem_" + nm)))
        self.grave = {}
        self.nbuf = 0
        self.chans = []
        self.chmap = {}

    def chan(self):
        c = Chan(self.es.enter_context(self.nc.semaphore("ch%d" % len(self.chans))))
        self.chans.append(c)
        return c

    def chan_for(self, key):
        if key not in self.chmap:
            self.chmap[key] = self.chan()
        return self.chmap[key]

    def tok(self):
        return Tok(self.grave)

    def _wait(self, e, ev):
        sem, val, src = ev
        if isinstance(src, Chan):
            val = src.val
        elif src is e and e.nm == "pe":
            return
        k = id(sem)
        if e.seen.get(k, 0) >= val:
            return
        e.h.wait_ge(sem, val)
        e.seen[k] = val

    def _deps(self, e, R, W):
        for t in R:
            if t.w is not None:
                self._wait(e, t.w)
            if t.excl:
                for ev in list(t.r.values()):
                    if ev[2] is not e:
                        self._wait(e, ev)
        for t in W:
            if t.w is not None:
                self._wait(e, t.w)
            for ev in list(t.r.values()):
                self._wait(e, ev)

    @staticmethod
    def _mark(ev, R, W):
        k = id(ev[0])
        for t in R:
            t.r[k] = ev
        for t in W:
            t.w = ev
            t.r = {}

    def op(self, en, fn, R=(), W=(), inc=True):
        e = self.E[en]
        self._deps(e, R, W)
        ins = fn(e.h)
        if inc:
            e.cnt += 1
            ins.then_inc(e.sem, 1)
            ev = (e.sem, e.cnt, e)
        else:
            ev = (e.sem, e.cnt + 1, e)
        self._mark(ev, R, W)

    def dma(self, qn, ch, out, in_, R=(), W=()):
        e = self.E[qn]
        self._deps(e, R, W)
        e.h.dma_start(out=out, in_=in_).then_inc(ch.sem, 16)
        ch.val += 16
        self._mark((ch.sem, ch.val, ch), R, W)

    def bury(self, toks):
        for t in toks:
            evs = list(t.r.values())
            if t.w is not None:
                evs.append(t.w)
            for ev in evs:
                k = id(ev[0])
                val = ev[2].val if isinstance(ev[2], Chan) else ev[1]
                old = self.grave.get(k)
                if old is None or old[1] < val:
                    self.grave[k] = (ev[0], val, ev[2])


class Buf:
    def __init__(self, S, scope, name, shape, dt, ntok=1):
        S.nbuf += 1
        self.name = name
        self.h = scope.enter_context(S.nc.sbuf_tensor("%s_%d" % (name, S.nbuf), list(shape), dt))
        self.t = [S.tok() for _ in range(ntok)]
        scope.callback(S.bury, self.t)

    def __getitem__(self, idx):
        return self.h[idx]


def _sp(x):
    if isinstance(x, tuple):
        a, t = x
        return a, (list(t) if isinstance(t, (list, tuple)) else [t])
    return x, []


class K:
    def __init__(self, nc, es, dbg):
        self.nc, self.es = nc, es
        self.S = Sched(nc, es)
        self.dbg = dbg

    def mm(self, out, lhsT, rhs, start=True, stop=True, inc=None):
        o, ot = _sp(out); l, lt = _sp(lhsT); r, rt = _sp(rhs)
        self.S.op("pe", lambda h: h.matmul(o, lhsT=l, rhs=r, start=start, stop=stop),
                  R=lt + rt, W=ot, inc=(stop if inc is None else inc))

    def tr(self, out, in_, ident):
        o, ot = _sp(out); i, it = _sp(in_); d, dt_ = _sp(ident)
        self.S.op("pe", lambda h: h.transpose(o, i, d), R=it + dt_, W=ot)

    def act(self, out, in_, func, bias=None, scale=1.0, eng="act"):
        o, ot = _sp(out); i, it = _sp(in_); b, bt = _sp(bias); s, st = _sp(scale)
        kw = {}
        if b is not None:
            kw["bias"] = b
        self.S.op(eng, lambda h: h.activation(out=o, in_=i, func=func, scale=s, **kw),
                  R=it + bt + st, W=ot)

    def ts(self, eng, out, in0, s1, s2, op0, op1=None):
        o, ot = _sp(out); i, it = _sp(in0); a, at = _sp(s1); b, bt = _sp(s2)
        if op1 is None:
            self.S.op(eng, lambda h: h.tensor_scalar(out=o, in0=i, scalar1=a, scalar2=None, op0=op0),
                      R=it + at, W=ot)
        else:
            self.S.op(eng, lambda h: h.tensor_scalar(out=o, in0=i, scalar1=a, scalar2=b, op0=op0, op1=op1),
                      R=it + at + bt, W=ot)

    def tt(self, eng, out, in0, in1, op):
        o, ot = _sp(out); i, it = _sp(in0); j, jt = _sp(in1)
        self.S.op(eng, lambda h: h.tensor_tensor(out=o, in0=i, in1=j, op=op), R=it + jt, W=ot)

    def stt(self, eng, out, in0, sc, in1, op0, op1):
        o, ot = _sp(out); i, it = _sp(in0); s, st = _sp(sc); j, jt = _sp(in1)
        self.S.op(eng, lambda h: h.scalar_tensor_tensor(out=o, in0=i, scalar=s, in1=j, op0=op0, op1=op1),
                  R=it + st + jt, W=ot)

    def copy(self, eng, out, in_):
        o, ot = _sp(out); i, it = _sp(in_)
        if eng == "act":
            self.S.op(eng, lambda h: h.activation(out=o, in_=i, func=AF.Copy), R=it, W=ot)
        else:
            self.S.op(eng, lambda h: h.tensor_copy(out=o, in_=i), R=it, W=ot)

    def memset(self, eng, out, val):
        o, ot = _sp(out)
        self.S.op(eng, lambda h: h.memset(o, val), W=ot)

    def dma(self, q, ch, out, in_):
        o, ot = _sp(out); i, it = _sp(in_)
        if isinstance(ch, str):
            ch = self.S.chan_for(ch)
        self.S.dma(q, ch, o, i, R=it, W=ot)

    def setup_psum(self):
        self.banks = []
        for i in range(8):
            h = self.es.enter_context(self.nc.psum_tensor("bank%d" % i, [128, 512], F32))
            self.banks.append((h[:, :], Tok(excl=True)))
        self._bk = {"a": [0, 0, 4], "b": [0, 4, 4]}

    def bank(self, pool="a"):
        st = self._bk[pool]
        i = st[1] + st[0] % st[2]
        st[0] += 1
        return self.banks[i]


def build(nc, dbg=None):
    stage = int(os.environ.get("KSTAGE", "99"))
    es = ExitStack()
    with es:
        k = K(nc, es, dbg)
        S = k.S
        k.setup_psum()

        def din(name, shape, dt=F32):
            return nc.dram_tensor(name, list(shape), dt, kind="ExternalInput").ap()

        def dout(name, shape, dt=F32):
            return nc.dram_tensor(name, list(shape), dt, kind="ExternalOutput").ap()

        x_in = din("x_in", [T, D])
        cond = din("cond", [128, KT])
        consts = din("consts", [128, 1024])
        pvec = din("pvec", [NL, 128, NPV])
        w_ada = din("w_ada", [NL, D, 6 * D])
        w_in = din("w_in", [NL, D, NIN])
        w_branch = din("w_branch", [NL, 3, 512, D])
        w_out = din("w_out", [NL, D, D])
        w_f1 = din("w_ffn_in", [NL, D, 2 * FH])
        w_f2 = din("w_ffn_out", [NL, FH, D])
        nab = din("nab", [NL, 4, 2, 10, 128, 512])
        dmask = din("dmask", [2, 10, 128, 512])
        rope = din("rope", [2, 128, T])
        ext_k = din("ext_k", [NL, 2, 256, 512])
        ext_v = din("ext_v", [NL, 2, 256, 512])
        rw_w2 = din("rw_w2", [NL, 128, 512])
        rw_a2 = din("rw_a2", [NL, 128, 512])
        rw_g2 = din("rw_g2", [NL, 128, 512])
        s0_in = din("s0", [NL, 2, 4, 128, 128])
        keepf = din("keepf", [128, 1])
        st_out = dout("st", [NL, 2, 4, 4, 128, 128])
        y_out = dout("y", [T, D])
        kv_out = dout("kv", [NL, 4, T, 512])

        top = es
        P = lambda name, shape, dt, ntok=1, scope=None: Buf(S, scope or top, name, shape, dt, ntok)

        xT = P("xT", [128, KT, T], F32, ntok=KT)
        hT = P("hT", [128, KT, T], BF16, ntok=KT)
        cst = P("cst", [128, 1024], F32)
        pv = P("pv", [128, NPV], F32)
        mod = P("mod", [128, 96], F32)
        mod1p = P("mod1p", [128, 32], F32)
        lnp = P("lnp", [128, 64], F32)
        scond = P("scond", [128, KT], BF16)
        condt = P("condt", [128, KT], F32)
        onesb = P("onesb", [128, 128], BF16)
        NWB = 3
        wring = [P("wr%d" % i, [128, 4096], BF16) for i in range(NWB)]
        wch = [S.chan() for _ in range(NWB)]
        wcur = [0]
        ch_misc = S.chan()
        ch_pool = S.chan()
        ch_out = S.chan()

        ident = (cst[:, 0:128], cst.t[0])

        k.dma("sp", "cst", (cst[:, :], cst.t[0]), consts)
        k.dma("sp", "condt", (condt[:, :], condt.t[0]), cond)
        k.act((scond[:, :], scond.t[0]), (condt[:, :], condt.t[0]), AF.Silu)
        k.memset("dve", (onesb[:, :], onesb.t[0]), 1.0)

        with ExitStack() as sc:
            xin = [P("xin%d" % i, [128, D], F32, scope=sc) for i in range(2)]
            chx = [S.chan(), S.chan()]
            for tt in range(8):
                b = xin[tt % 2]
                k.dma("sp", chx[tt % 2], (b[:, :], b.t[0]), x_in[tt * 128:(tt + 1) * 128, :])
                for kq in range(4):
                    bk, bt = k.bank("a")
                    for j in range(4):
                        kt = kq * 4 + j
                        k.tr((bk[:, j * 128:(j + 1) * 128], bt), (b[:, kt * 128:(kt + 1) * 128], b.t[0]), ident)
                    for j in range(4):
                        kt = kq * 4 + j
                        eng = "dve" if j % 2 == 0 else "act"
                        if eng == "dve":
                            k.ts("dve", (xT[:, kt, tt * 128:(tt + 1) * 128], xT.t[kt]),
                                 (bk[:, j * 128:(j + 1) * 128], bt), ALPHA, None, ALU.mult)
                        else:
                            k.act((xT[:, kt, tt * 128:(tt + 1) * 128], xT.t[kt]),
                                  (bk[:, j * 128:(j + 1) * 128], bt), AF.Copy, scale=ALPHA)

        def wload(srcs):
            i = wcur[0] % NWB
            wcur[0] += 1
            b = wring[i]
            for dst, src in srcs:
                k.dma("pool", wch[i], (dst(b), b.t[0]), src)
            return b

        def wview(b, kt_n, ncols):
            return b.h[:, 0:kt_n * ncols].rearrange("p (k n) -> p k n", k=kt_n)

        def stream(jobs, depth=NWB - 1):
            bufs = {}
            n = len(jobs)
            for i in range(min(depth, n)):
                bufs[i] = jobs[i][0]()
            for i in range(n):
                jobs[i][1](bufs.pop(i))
                if i + depth < n:
                    bufs[i + depth] = jobs[i + depth][0]()

        def wsrc(w2d, c0, nc_, kt0=0, ktn=KT):
            return w2d[kt0 * 128:(kt0 + ktn) * 128, c0:c0 + nc_].rearrange("(k p) n -> p k n", p=128)

        def stats_norm(eps, emit):
            with ExitStack() as sc:
                MU = P("MU", [128, T], F32, scope=sc)
                RS = P("RS", [128, T], F32, scope=sc)
                cb = [P("cb%d" % i, [128, 512], BF16, scope=sc) for i in range(2)]
                sq = [P("sq%d" % i, [128, 512], BF16, scope=sc) for i in range(2)]
                tmp = [P("nt%d" % i, [128, T], F32, scope=sc) for i in range(2)]
                s1 = [k.bank("b") for _ in range(2)]
                s2 = [k.bank("b") for _ in range(2)]
                i = 0
                for kt in range(KT):
                    for hf in range(2):
                        sl = slice(hf * 512, (hf + 1) * 512)
                        c, q = cb[i % 2], sq[i % 2]
                        i += 1
                        k.copy("dve", (c[:, :], c.t[0]), (xT[:, kt, sl], xT.t[kt]))
                        k.act((q[:, :], q.t[0]), (xT[:, kt, sl], xT.t[kt]), AF.Square)
                        k.mm(s1[hf], (onesb[:, :], onesb.t[0]), (c[:, :], c.t[0]), start=(kt == 0), stop=(kt == KT - 1), inc=True)
                        k.mm(s2[hf], (onesb[:, :], onesb.t[0]), (q[:, :], q.t[0]), start=(kt == 0), stop=(kt == KT - 1), inc=True)
                for hf in range(2):
                    sl = slice(hf * 512, (hf + 1) * 512)
                    k.ts("dve", (MU[:, sl], MU.t[0]), s1[hf], 1.0 / D, None, ALU.mult)
                    k.tt("dve", (RS[:, sl], RS.t[0]), (MU[:, sl], MU.t[0]), (MU[:, sl], MU.t[0]), ALU.mult)
                    k.stt("dve", (RS[:, sl], RS.t[0]), s2[hf], 1.0 / D, (RS[:, sl], RS.t[0]), ALU.mult, ALU.subtract)
                    k.act((RS[:, sl], RS.t[0]), (RS[:, sl], RS.t[0]), AF.Sqrt, bias=eps)
                    k.S.op("dve", lambda h, sl=sl: h.reciprocal(out=RS[:, sl], in_=RS[:, sl]), W=[RS.t[0]])
                for kt in range(KT):
                    tm = tmp[kt % 2]
                    k.tt("dve", (tm[:, :], tm.t[0]), (xT[:, kt, :], xT.t[kt]), (MU[:, :], MU.t[0]), ALU.subtract)
                    k.tt("dve", (tm[:, :], tm.t[0]), (tm[:, :], tm.t[0]), (RS[:, :], RS.t[0]), ALU.mult)
                    emit(kt, tm)

        def modulate(which):
            sh0 = 0 if which == 0 else 48
            def emit(kt, tm):
                k.act((hT[:, kt, :], hT.t[kt]), (tm[:, :], tm.t[0]), AF.Identity,
                      bias=(mod[:, sh0 + kt:sh0 + kt + 1], mod.t[0]),
                      scale=(mod1p[:, which * 16 + kt:which * 16 + kt + 1], mod1p.t[0]))
            stats_norm(ALPHA * ALPHA * LN_EPS, emit)

        def layernorm(which, out_alpha):
            o = which * 32 if out_alpha != 1.0 else None
            def emit(kt, tm):
                if out_alpha != 1.0:
                    gs = (lnp[:, which * 32 + kt:which * 32 + kt + 1], lnp.t[0])
                    bs = (lnp[:, which * 32 + 16 + kt:which * 32 + 17 + kt], lnp.t[0])
                else:
                    g0 = PV["ln2_g"][0]; b0 = PV["ln2_b"][0]
                    gs = (pv[:, g0 + kt:g0 + kt + 1], pv.t[0])
                    bs = (pv[:, b0 + kt:b0 + kt + 1], pv.t[0])
                k.act((xT[:, kt, :], xT.t[kt]), (tm[:, :], tm.t[0]), AF.Identity, bias=bs, scale=gs)
            stats_norm(LN_EPS, emit)

        def adaln(l):
            bk, bt = k.bank("a")
            jobs = []
            for cgi in range(48):
                def ld(cgi=cgi):
                    return wload([(lambda b: wview(b, KT, 256), wsrc(w_ada[l], cgi * 256, 256))])
                def use(b, cgi=cgi):
                    wv = wview(b, KT, 256)
                    for j in range(2):
                        col = cgi * 2 + j
                        for kt in range(KT):
                            k.mm((bk[:, col:col + 1], bt), (wv[:, kt, j * 128:(j + 1) * 128], b.t[0]),
                                 (scond[:, kt:kt + 1], scond.t[0]), start=(kt == 0), stop=(kt == KT - 1),
                                 inc=(kt == KT - 1))
                jobs.append((ld, use))
            stream(jobs)
            b0 = PV["b_ada"][0]
            k.tt("dve", (mod[:, :], mod.t[0]), (bk[:, 0:96], bt), (pv[:, b0:b0 + 96], pv.t[0]), ALU.add)
            k.ts("dve", (mod1p[:, 0:16], mod1p.t[0]), (mod[:, 16:32], mod.t[0]), 1.0, None, ALU.add)
            k.ts("dve", (mod1p[:, 16:32], mod1p.t[0]), (mod[:, 64:80], mod.t[0]), 1.0, None, ALU.add)
            for i, nm in enumerate(("ln1_g", "ln1_b", "ln2_g", "ln2_b")):
                o0 = PV[nm][0]
                k.ts("dve", (lnp[:, i * 16:(i + 1) * 16], lnp.t[0]), (pv[:, o0:o0 + 16], pv.t[0]), ALPHA, None, ALU.mult)

        def proj_fm(wmat, c0, ncols, consume, kt0=0, ktn=KT, src=None, src_tok=None):
            jobs = []
            nchunk = ncols // 256
            for ci in range(nchunk):
                def ld(ci=ci):
                    return wload([(lambda b: wview(b, ktn, 256), wsrc(wmat, c0 + ci * 256, 256, kt0, ktn))])
                def use(b, ci=ci):
                    wv = wview(b, ktn, 256)
                    for j in range(2):
                        for hf in range(2):
                            bk = k.bank("a")
                            for kt in range(ktn):
                                rhs = src(kt, hf)
                                k.mm(bk, (wv[:, kt, j * 128:(j + 1) * 128], b.t[0]), rhs,
                                     start=(kt == 0), stop=(kt == ktn - 1))
                            consume(ci * 2 + j, hf, bk)
                jobs.append((ld, use))
            return jobs

        hsrc = lambda kt, hf: (hT[:, kt, hf * 512:(hf + 1) * 512], hT.t[kt])

        def proj_tm(l, c0, slot, post=None):
            with ExitStack() as sc:
                stg = [P("stg%d" % i, [128, 512], F32, scope=sc) for i in range(2)]
                bufs = [wload([(lambda b: wview(b, KT, 256), wsrc(w_in[l], c0 + ci * 256, 256))]) for ci in range(2)]
                for tt in range(8):
                    bk, bt = k.bank("a")
                    for ci in range(2):
                        wv = wview(bufs[ci], KT, 256)
                        for kt in range(KT):
                            k.mm((bk[:, ci * 256:(ci + 1) * 256], bt), (hT[:, kt, tt * 128:(tt + 1) * 128], hT.t[kt]),
                                 (wv[:, kt, :], bufs[ci].t[0]), start=(kt == 0), stop=(kt == KT - 1),
                                 inc=(ci == 1 and kt == KT - 1))
                    st = stg[tt % 2]
                    k.copy("act", (st[:, :], st.t[0]), (bk[:, :], bt))
                    if post is not None:
                        post(tt, (bk, bt))
                    k.dma("sp", "o_stg%d" % (tt % 2), kv_out[l, slot, tt * 128:(tt + 1) * 128, :], (st[:, :], st.t[0]))

        KMIX = os.environ.get("KMIX", "abc")
        onesf = (cst[:, 768:896], cst.t[0])
        rotT = (cst[:, 896:1024], cst.t[0])
        bch = [S.chan() for _ in range(4)]
        bcur = [0]

        def attention(sc, qT, kT, V, ncomp, bias_src, out_cb):
            bring = [P("br%d" % i, [128, 512], F32, scope=sc) for i in range(4)]

            def bload(src):
                i = bcur[0] % 4
                bcur[0] += 1
                k.dma("sp", bch[i], (bring[i][:, :], bring[i].t[0]), src)
                return bring[i]
            tmp = [P("atm%d" % i, [128, 512], F32, scope=sc) for i in range(2)]
            pt = [P("apt%d" % i, [128, 512], BF16, scope=sc) for i in range(3)]
            kd = 128 // ncomp
            units = [(h, qh, c, j) for h in range(4) for qh in range(2) for c in range(ncomp) for j in range(10)]
            srcs = [bias_src(h, qh, j) for (h, qh, c, j) in units]
            bufs = {}
            for i in range(min(3, len(units))):
                bufs[i] = bload(srcs[i])
            n = 0

            def issue_s(u_):
                h_, qh_, c_, j_ = units[u_]
                rows_ = slice(c_ * kd, (c_ + 1) * kd)
                sb_ = k.bank("a")
                k.mm(sb_, (kT[rows_, h_, j_ * 128:(j_ + 1) * 128], kT.t[h_]), (qT[rows_, h_, qh_ * 512:(qh_ + 1) * 512], qT.t[h_]))
                return sb_
            SPRE = 2
            sbs = {}
            for i in range(min(SPRE, len(units))):
                sbs[i] = issue_s(i)
            for ui, (h, qh, c, j) in enumerate(units):
                if c == 0 and j == 0:
                    obanks = [k.bank("b") for _ in range(2 * ncomp)]
                sb = sbs.pop(ui)
                bb = bufs.pop(ui)
                tm = tmp[n % 2]; p_ = pt[n % 3]; n += 1
                k.tt("dve", (tm[:, :], tm.t[0]), sb, (bb[:, :], bb.t[0]), ALU.add)
                if ui + 3 < len(units):
                    bufs[ui + 3] = bload(srcs[ui + 3])
                k.act((p_[:, :], p_.t[0]), (tm[:, :], tm.t[0]), AF.Exp)
                if ui + SPRE < len(units):
                    sbs[ui + SPRE] = issue_s(ui + SPRE)
                k.mm(obanks[2 * c], (V[:, j, h * 128:(h + 1) * 128], V.t[j]), (p_[:, :], p_.t[0]),
                     start=(j == 0), stop=(j == 9), inc=True)
                k.mm(obanks[2 * c + 1], (onesb[:, :], onesb.t[0]), (p_[:, :], p_.t[0]),
                     start=(j == 0), stop=(j == 9), inc=True)
                if c == ncomp - 1 and j == 9:
                    out_cb(h, qh, obanks)

        def load_ext(sc, l, which, kT, V):
            k.dma("pool", "pool_extv", (V[:, 8:10, :], [V.t[8], V.t[9]]), ext_v[l, which].rearrange("(t p) n -> p t n", p=128))
            sc = ExitStack()
            ek = P("ek", [128, 2, 512], F32, scope=sc)
            k.dma("sp", "ek", (ek[:, :, :], ek.t[0]), ext_k[l, which].rearrange("(t p) n -> p t n", p=128))
            for t in range(2):
                bk, bt = k.bank("a")
                for h in range(4):
                    k.tr((bk[:, h * 128:(h + 1) * 128], bt), (ek[:, t, h * 128:(h + 1) * 128], ek.t[0]), ident)
                for h in range(4):
                    k.copy("dve" if h % 2 == 0 else "act", (kT[:, h, 1024 + t * 128:1024 + (t + 1) * 128], kT.t[h]),
                           (bk[:, h * 128:(h + 1) * 128], bt))
            sc.close()

        def mixer_phase(l):
            with ExitStack() as msc:
                oc_ = P("oT2", [128, 4, T], BF16, ntok=4, scope=msc)
                if "c" in KMIX:
                    rwkv_phase(l, oc_)
                else:
                    k.memset("dve", (oc_[:, :, :], oc_.t), 0.0)
                oT = [P("oT%d" % i, [128, 4, T], BF16, ntok=4, scope=msc) for i in range(2)] + [oc_]
                if "a" in KMIX:
                    with ExitStack() as sc:
                        qT = P("qT", [128, 4, T], BF16, ntok=4, scope=sc)
                        kT = P("kT", [128, 4, T + 256], BF16, ntok=4, scope=sc)
                        V = P("V", [128, 10, 512], BF16, ntok=10, scope=sc)
                        load_ext(sc, l, 0, kT, V)
                        def cq(ft, hf, bk):
                            k.act((qT[:, ft, hf * 512:(hf + 1) * 512], qT.t[ft]), bk, AF.Copy, scale=128 ** -0.5)
                        def ck(ft, hf, bk):
                            k.copy("dve", (kT[:, ft, hf * 512:(hf + 1) * 512], kT.t[ft]), bk)
                        stream(proj_fm(w_in[l], C_NAQ, 512, cq, src=hsrc) + proj_fm(w_in[l], C_NAK, 512, ck, src=hsrc))
                        proj_tm(l, C_NAK, 0)
                        proj_tm(l, C_NAV, 1, post=lambda tt, bk: k.copy("dve", (V[:, tt, :], V.t[tt]), bk))
                        def cb_a(h, qh, ob):
                            with ExitStack() as s2:
                                rc = P("rc", [128, 512], F32, scope=s2)
                                k.S.op("dve", lambda e: e.reciprocal(out=rc[:, :], in_=ob[1][0]), R=[ob[1][1]], W=[rc.t[0]])
                                k.tt("dve", (oT[0][:, h, qh * 512:(qh + 1) * 512], oT[0].t[h]), ob[0], (rc[:, :], rc.t[0]), ALU.mult)
                        attention(sc, qT, kT, V, 1, lambda h, qh, j: nab[l, h, qh, j], cb_a)
                else:
                    k.memset("dve", (oT[0][:, :, :], oT[0].t), 0.0)
                    proj_tm(l, C_NAK, 0); proj_tm(l, C_NAV, 1)
                if "b" in KMIX:
                    with ExitStack() as sc:
                        qT = P("dqT", [128, 4, T], BF16, ntok=4, scope=sc)
                        kT = P("dkT", [128, 4, T + 256], BF16, ntok=4, scope=sc)
                        V = P("dV", [128, 10, 512], BF16, ntok=10, scope=sc)
                        lam = P("lamt", [128, 8], F32, scope=sc)
                        sg = P("sgn", [128, 4], F32, scope=sc)
                        rsc = ExitStack()
                        cs = P("cs", [128, 2, T], F32, scope=rsc)
                        xr = [P("xr%d" % i, [128, 512], F32, scope=rsc) for i in range(2)]
                        t1 = [P("t1%d" % i, [128, 512], F32, scope=rsc) for i in range(2)]
                        k.dma("sp", "cs", (cs[:, :, :], cs.t[0]), rope.rearrange("c p t -> p c t"))
                        load_ext(sc, l, 1, kT, V)
                        lo = PV["lam"][0]
                        k.tt("dve", (lam[:, 0:1], lam.t[0]), (pv[:, lo:lo + 1], pv.t[0]), (pv[:, lo + 1:lo + 2], pv.t[0]), ALU.mult)
                        k.tt("dve", (lam[:, 1:2], lam.t[0]), (pv[:, lo + 2:lo + 3], pv.t[0]), (pv[:, lo + 3:lo + 4], pv.t[0]), ALU.mult)
                        bk, bt = k.bank("a")
                        k.mm((bk[:, 0:2], bt), onesf, (lam[:, 0:2], lam.t[0]))
                        k.act((lam[:, 2:4], lam.t[0]), (bk[:, 0:2], bt), AF.Exp)
                        lam_init = 0.8 - 0.6 * float(np.exp(-0.3 * l))
                        k.tt("dve", (lam[:, 4:5], lam.t[0]), (lam[:, 3:4], lam.t[0]), (lam[:, 2:3], lam.t[0]), ALU.subtract)
                        k.ts("dve", (lam[:, 5:6], lam.t[0]), (lam[:, 4:5], lam.t[0]), -lam_init, None, ALU.add)
                        so = PV["subln"][0]
                        k.ts("dve", (lam[:, 6:7], lam.t[0]), (pv[:, so:so + 1], pv.t[0]), 0.0, None, ALU.add)
                        k.ts("dve", (sg[:, :], sg.t[0]), (pv[:, so:so + 4], pv.t[0]), 1.0 - lam_init, None, ALU.mult)
                        ri = [0]
                        def mk(dst, scale):
                            def c_(ft, hf, bk):
                                i = ri[0] % 2; ri[0] += 1
                                sl = slice(hf * 512, (hf + 1) * 512)
                                k.act((xr[i][:, :], xr[i].t[0]), bk, AF.Copy, scale=scale)
                                rb = k.bank("b")
                                k.mm(rb, rotT, (xr[i][:, :], xr[i].t[0]))
                                k.tt("dve", (t1[i][:, :], t1[i].t[0]), (xr[i][:, :], xr[i].t[0]), (cs[:, 0, sl], cs.t[0]), ALU.mult)
                                k.tt("dve", (xr[i][:, :], xr[i].t[0]), rb, (cs[:, 1, sl], cs.t[0]), ALU.mult)
                                k.tt("dve", (dst[:, ft, sl], dst.t[ft]), (t1[i][:, :], t1[i].t[0]), (xr[i][:, :], xr[i].t[0]), ALU.add)
                            return c_
                        stream(proj_fm(w_in[l], C_DFQ, 512, mk(qT, 64 ** -0.5), src=hsrc) + proj_fm(w_in[l], C_DFK, 512, mk(kT, 1.0), src=hsrc))
                        rsc.close()
                        proj_tm(l, C_DFK, 2)
                        proj_tm(l, C_DFV, 3, post=lambda tt, bk: k.copy("dve", (V[:, tt, :], V.t[tt]), bk))
                        def cb_b(h, qh, ob):
                            with ExitStack() as s2:
                                r0 = P("r0", [128, 512], F32, scope=s2)
                                r1 = P("r1", [128, 512], F32, scope=s2)
                                sq = P("sqb", [128, 512], BF16, scope=s2)
                                k.S.op("dve", lambda e: e.reciprocal(out=r0[:, :], in_=ob[1][0]), R=[ob[1][1]], W=[r0.t[0]])
                                k.S.op("dve", lambda e: e.reciprocal(out=r1[:, :], in_=ob[3][0]), R=[ob[3][1]], W=[r1.t[0]])
                                k.tt("dve", (r0[:, :], r0.t[0]), ob[0], (r0[:, :], r0.t[0]), ALU.mult)
                                k.tt("dve", (r1[:, :], r1.t[0]), ob[2], (r1[:, :], r1.t[0]), ALU.mult)
                                k.stt("dve", (r0[:, :], r0.t[0]), (r1[:, :], r1.t[0]), (lam[:, 5:6], lam.t[0]), (r0[:, :], r0.t[0]), ALU.mult, ALU.add)
                                k.act((sq[:, :], sq.t[0]), (r0[:, :], r0.t[0]), AF.Square)
                                mb = k.bank("a")
                                k.mm(mb, (onesb[:, :], onesb.t[0]), (sq[:, :], sq.t[0]))
                                k.act((r1[:, :], r1.t[0]), mb, AF.Sqrt, bias=LN_EPS, scale=1.0 / 128)
                                k.S.op("dve", lambda e: e.reciprocal(out=r1[:, :], in_=r1[:, :]), W=[r1.t[0]])
                                k.tt("dve", (r0[:, :], r0.t[0]), (r0[:, :], r0.t[0]), (r1[:, :], r1.t[0]), ALU.mult)
                                k.ts("dve", (oT[1][:, h, qh * 512:(qh + 1) * 512], oT[1].t[h]), (r0[:, :], r0.t[0]),
                                     (sg[:, h:h + 1], sg.t[0]), None, ALU.mult)
                        attention(sc, qT, kT, V, 2, lambda h, qh, j: dmask[qh, j], cb_b)
                else:
                    k.memset("dve", (oT[1][:, :, :], oT[1].t), 0.0)
                    proj_tm(l, C_DFK, 2); proj_tm(l, C_DFV, 3)
                with ExitStack() as sc:
                    mT = P("mT", [128, KT, T], BF16, ntok=KT, scope=sc)
                    acc = P("acc", [128, 2, T], F32, ntok=2, scope=sc)
                    sg4 = [P("sg4%d" % i, [128, 512], F32, scope=sc) for i in range(4)]
                    jobs = []
                    for jj in range(8):
                        for i in range(3):
                            def ld(jj=jj, i=i):
                                return wload([(lambda b: wview(b, KT, 256), wsrc(w_in[l], C_GATE + i * D + jj * 256, 256))])
                            def ldb(jj=jj, i=i):
                                return wload([(lambda b: b.h[:, 0:1024].rearrange("p (k n) -> p k n", k=4),
                                               w_branch[l, i, :, jj * 256:(jj + 1) * 256].rearrange("(k p) n -> p k n", p=128))])
                            def useg(b, jj=jj, i=i):
                                wv = wview(b, KT, 256)
                                for u in range(4):
                                    j2, hf = u // 2, u % 2
                                    bk = k.bank("a")
                                    for kt in range(KT):
                                        k.mm(bk, (wv[:, kt, j2 * 128:(j2 + 1) * 128], b.t[0]), hsrc(kt, hf), start=(kt == 0), stop=(kt == KT - 1))
                                    k.act((sg4[u][:, :], sg4[u].t[0]), bk, AF.Sigmoid)
                            def useb(b, jj=jj, i=i):
                                wv = b.h[:, 0:1024].rearrange("p (k n) -> p k n", k=4)
                                for u in range(4):
                                    j2, hf = u // 2, u % 2
                                    sl = slice(hf * 512, (hf + 1) * 512)
                                    bk = k.bank("b")
                                    for k4 in range(4):
                                        k.mm(bk, (wv[:, k4, j2 * 128:(j2 + 1) * 128], b.t[0]), (oT[i][:, k4, sl], oT[i].t[k4]), start=(k4 == 0), stop=(k4 == 3))
                                    if i == 0:
                                        k.tt("dve", (acc[:, j2, sl], acc.t[j2]), (sg4[u][:, :], sg4[u].t[0]), bk, ALU.mult)
                                    else:
                                        t_ = sg4[u]
                                        k.tt("dve", (t_[:, :], t_.t[0]), (sg4[u][:, :], sg4[u].t[0]), bk, ALU.mult)
                                        if i == 1:
                                            k.tt("dve", (acc[:, j2, sl], acc.t[j2]), (acc[:, j2, sl], acc.t[j2]), (t_[:, :], t_.t[0]), ALU.add)
                                        else:
                                            k.tt("dve", (mT[:, jj * 2 + j2, sl], mT.t[jj * 2 + j2]), (acc[:, j2, sl], acc.t[j2]), (t_[:, :], t_.t[0]), ALU.add)
                            jobs.append((ld, useg))
                            jobs.append((ldb, useb))
                    stream(jobs)
                    def cres(ft, hf, bk):
                        sl = slice(hf * 512, (hf + 1) * 512)
                        k.stt("dve", (xT[:, ft, sl], xT.t[ft]), bk, (mod[:, g1c + ft:g1c + ft + 1], mod.t[0]),
                              (xT[:, ft, sl], xT.t[ft]), ALU.mult, ALU.add)
                    stream(proj_fm(w_out[l], 0, D, cres, src=lambda kt, hf: (mT[:, kt, hf * 512:(hf + 1) * 512], mT.t[kt])))

        CW = 0.6065306597126334
        bd = (cst[:, 128:256], cst.t[0])
        m_us = (cst[:, 256:384], cst.t[0]); m_ui = (cst[:, 384:512], cst.t[0])
        m_ls = (cst[:, 512:640], cst.t[0]); m_li = (cst[:, 640:768], cst.t[0])
        ch_st = S.chan()

        def rwkv_phase(l, ocT):
            pvc = lambda nm, j, n=1: (pv[:, PV[nm][0] + j:PV[nm][0] + j + n], pv.t[0])
            with ExitStack() as sc:
                tw = P("tw", [128, T], F32, scope=sc)
                adT = P("adT", [128, T], F32, scope=sc)
                sgd = P("sgd", [128, T], F32, scope=sc)
                c0 = P("c0", [128, 16], F32, scope=sc)
                kf = P("kf", [128, 1], F32, scope=sc)
                k.dma("sp", "kf", (kf[:, :], kf.t[0]), keepf)
                m0, m1 = PV["mix0"][0], PV["mix1"][0]
                k.tt("dve", (c0[:, 0:15], c0.t[0]), (pv[:, m0:m0 + 15], pv.t[0]), (pv[:, m1:m1 + 15], pv.t[0]), ALU.add)
                k.ts("dve", (c0[:, 0:15], c0.t[0]), (c0[:, 0:15], c0.t[0]), -1.0, 1.0, ALU.mult, ALU.add)

                def shifted(tiles_dst):
                    with ExitStack() as fs_:
                        Fr = P("Fr", [128, 4, 258], F32, scope=fs_)
                        k.memset("dve", (Fr[:, :, :], Fr.t[0]), 0.0)
                        _shifted(tiles_dst, Fr)

                def _shifted(tiles_dst, Fr):
                    for g0 in range(0, len(tiles_dst), 2):
                        grp = tiles_dst[g0:g0 + 2]
                        b = wload([(lambda b_, i=i: wview(b_, KT, 256)[:, :, i * 128:(i + 1) * 128],
                                    wsrc(w_in[l], C_RW + ft * 128, 128)) for i, (ft, _) in enumerate(grp)])
                        wv = wview(b, KT, 256)
                        for i, (ft, dst) in enumerate(grp):
                            for hf in range(2):
                                bk = k.bank("a")
                                for kt in range(KT):
                                    k.mm(bk, (wv[:, kt, i * 128:(i + 1) * 128], b.t[0]), hsrc(kt, hf), start=(kt == 0), stop=(kt == KT - 1))
                                k.S.op("act", lambda h, hf=hf, bk=bk: h.activation(
                                    out=Fr[:, 2 * hf:2 * hf + 2, 1:257], in_=bk[0].rearrange("p (s t) -> p s t", s=2), func=AF.Copy),
                                    R=[bk[1]], W=[Fr.t[0]])
                            k.ts("dve", (Fr[:, 1:4, 0:1], Fr.t[0]), (Fr[:, 0:3, 256:257], Fr.t[0]), (kf[:, 0:1], kf.t[0]), None, ALU.mult)
                            k.ts("dve", (Fr[:, 0:3, 257:258], Fr.t[0]), (Fr[:, 1:4, 1:2], Fr.t[0]), (kf[:, 0:1], kf.t[0]), None, ALU.mult)
                            d3 = dst[:, :].rearrange("p (s t) -> p s t", s=4)
                            k.ts("dve", (d3, dst.t[0]), (Fr[:, :, 1:257], Fr.t[0]), (c0[:, ft:ft + 1], c0.t[0]), None, ALU.mult)
                            k.stt("dve", (d3, dst.t[0]), (Fr[:, :, 0:256], Fr.t[0]), (pv[:, m0 + ft:m0 + ft + 1], pv.t[0]), (d3, dst.t[0]), ALU.mult, ALU.add)
                            k.stt("dve", (d3, dst.t[0]), (Fr[:, :, 2:258], Fr.t[0]), (pv[:, m1 + ft:m1 + ft + 1], pv.t[0]), (d3, dst.t[0]), ALU.mult, ALU.add)

                shifted([(12, tw), (13, adT)])
                shifted([(14, sgd)])
                k.act((tw[:, :], tw.t[0]), (tw[:, :], tw.t[0]), AF.Tanh)
                k.act((sgd[:, :], sgd.t[0]), (sgd[:, :], sgd.t[0]), AF.Sigmoid)

                for p in range(4):
                    with ExitStack() as ps_:
                        rT = P("rT", [128, T], F32, scope=ps_)
                        kT_ = P("kTr", [128, T], F32, scope=ps_)
                        vT = P("vT", [128, T], F32, scope=ps_)
                        kkT = P("kkT", [128, T], F32, scope=ps_)
                        yacc = P("yacc", [128, T], F32, scope=ps_)
                        Vtok = P("Vtok", [128, 8, 128], F32, ntok=8, scope=ps_)
                        w2t = P("w2t", [128, 128], F32, scope=ps_)
                        a2t = P("a2t", [128, 128], F32, scope=ps_)
                        g2t = P("g2t", [128, 128], F32, scope=ps_)
                        k.dma("sp", "w2t", (w2t[:, :], w2t.t[0]), rw_w2[l][:, p * 128:(p + 1) * 128])
                        k.dma("sp", "a2t", (a2t[:, :], a2t.t[0]), rw_a2[l][:, p * 128:(p + 1) * 128])
                        k.dma("sp", "g2t", (g2t[:, :], g2t.t[0]), rw_g2[l][:, p * 128:(p + 1) * 128])
                        shifted([(p, rT), (4 + p, kT_)])
                        shifted([(8 + p, vT)])
                        k.ts("dve", (kkT[:, :], kkT.t[0]), (kT_[:, :], kT_.t[0]), pvc("kk", p), None, ALU.mult)
                        k.act((yacc[:, :], yacc.t[0]), (kkT[:, :], kkT.t[0]), AF.Square)
                        for hf in range(2):
                            sl = slice(hf * 512, (hf + 1) * 512)
                            bk = k.bank("a")
                            k.mm(bk, bd, (yacc[:, sl], yacc.t[0]))
                            k.ts("dve", (yacc[:, sl], yacc.t[0]), bk, 1e-24, None, ALU.max)
                        k.act((yacc[:, :], yacc.t[0]), (yacc[:, :], yacc.t[0]), AF.Sqrt)
                        k.S.op("dve", lambda h: h.reciprocal(out=yacc[:, :], in_=yacc[:, :]), W=[yacc.t[0]])
                        k.tt("dve", (kkT[:, :], kkT.t[0]), (kkT[:, :], kkT.t[0]), (yacc[:, :], yacc.t[0]), ALU.mult)
                        for g in range(2):
                            bk, bt = k.bank("a")
                            for j in range(4):
                                c = g * 4 + j
                                k.tr((bk[:, j * 128:(j + 1) * 128], bt), (vT[:, c * 128:(c + 1) * 128], vT.t[0]), ident)
                            k.copy("act", (Vtok[:, g * 4:(g + 1) * 4, :], Vtok.t[g * 4:(g + 1) * 4]), (bk.rearrange("p (c f) -> p c f", c=4), bt))

                        def asig_half(d, hf, dst):
                            rows = slice(64 * d, 64 * d + 64)
                            bk = k.bank("a")
                            k.mm(bk, (a2t[rows, :], a2t.t[0]), (adT[rows, hf * 512:(hf + 1) * 512], adT.t[0]))
                            k.act(dst, bk, AF.Sigmoid, bias=pvc("a0", d * 4 + p))

                        for d in range(2):
                            with ExitStack() as ds_:
                                S0 = [P("S0%d" % i, [128, 128], F32, scope=ds_) for i in range(2)]
                                Up = [P("Up%d" % i, [128, 2, 128], F32, scope=ds_) for i in range(2)]
                                Vp = [P("Vp%d" % i, [128, 2, 128], F32, scope=ds_) for i in range(2)]
                                wful = P("wful", [128, 8], F32, scope=ds_)
                                tot = P("tot", [128, 4], F32, scope=ds_)
                                for i in range(2):
                                    k.memset("dve", (Up[i][:, :, :], Up[i].t[0]), 0.0)
                                    k.memset("dve", (Vp[i][:, :, :], Vp[i].t[0]), 0.0)
                                k.dma("sp", "S0", (S0[0][:, :], S0[0].t[0]), s0_in[l, d, p])
                                scur = 0
                                rows_d = slice(64 * d, 64 * d + 64)
                                for hf in ((0, 1) if d == 0 else (1, 0)):
                                    sl = slice(hf * 512, (hf + 1) * 512)
                                    with ExitStack() as hs_:
                                        HB = lambda nm: P(nm, [128, 512], F32, scope=hs_)
                                        btT, ktT, atT, rtT = HB("btT"), HB("ktT"), HB("atT"), HB("rtT")
                                        Tk = P("Tk", [128, 4, 3, 128], F32, ntok=4, scope=hs_)
                                        with ExitStack() as ts_:
                                            x1, x2 = P("x1", [128, 512], F32, scope=ts_), P("x2", [128, 512], F32, scope=ts_)
                                            A = lambda b_: (b_[:, :], b_.t[0])
                                            asig_half(d, hf, A(x1))
                                            k.tt("dve", A(btT), (kkT[:, sl], kkT.t[0]), A(x1), ALU.mult)
                                            k.ts("dve", A(x1), A(x1), pvc("ka", p), None, ALU.mult)
                                            k.S.op("dve", lambda h: h.tensor_scalar(out=x1[:, :], in0=x1[:, :], scalar1=pv[:, PV["ka"][0] + p:PV["ka"][0] + p + 1],
                                                                                   scalar2=1.0, op0=ALU.subtract, op1=ALU.add), R=[pv.t[0]], W=[x1.t[0]])
                                            k.tt("dve", A(ktT), (kT_[:, sl], kT_.t[0]), A(x1), ALU.mult)
                                            bk = k.bank("a")
                                            k.mm(bk, (w2t[rows_d, :], w2t.t[0]), (tw[rows_d, sl], tw.t[0]))
                                            k.act(A(x1), bk, AF.Sigmoid, bias=pvc("w0", d * 4 + p))
                                            for c in range(4):
                                                cs_ = slice(c * 128, (c + 1) * 128)
                                                k.S.op("dve", lambda h, cs_=cs_: h.tensor_tensor_scan(out=x2[:, cs_], data0=onesf[0][:, 0:128], data1=x1[:, cs_],
                                                                                                   initial=0.0, op0=ALU.mult, op1=ALU.add),
                                                       R=[x1.t[0], cst.t[0]], W=[x2.t[0]])
                                            if d == 0:
                                                k.tt("dve", A(x1), A(x2), A(x1), ALU.subtract)
                                            else:
                                                k.copy("dve", (tot[:, 0:4], tot.t[0]), (x2[:, :].rearrange("p (c t) -> p c t", c=4)[:, :, 127], x2.t[0]))
                                                for c in range(4):
                                                    cs_ = slice(c * 128, (c + 1) * 128)
                                                    k.S.op("dve", lambda h, cs_=cs_, c=c: h.tensor_scalar(out=x2[:, cs_], in0=x2[:, cs_], scalar1=tot[:, c:c + 1],
                                                                                                         scalar2=-1.0, op0=ALU.subtract, op1=ALU.mult), R=[tot.t[0]], W=[x2.t[0]])
                                                k.tt("dve", A(atT), A(x2), A(x1), ALU.add)
                                                k.copy("dve", A(x1), A(x2))
                                                k.copy("dve", A(x2), A(atT))
                                            k.act(A(atT), A(x1), AF.Exp, scale=-CW)
                                            k.stt("dve", A(atT), A(atT), -1.0, (kkT[:, sl], kkT.t[0]), ALU.mult, ALU.mult)
                                            k.act(A(rtT), A(x2), AF.Exp, scale=-CW)
                                            wcol = 127 if d == 0 else 0
                                            k.copy("dve", (wful[:, hf * 4:(hf + 1) * 4], wful.t[0]),
                                                   (rtT[:, :].rearrange("p (c t) -> p c t", c=4)[:, :, wcol], rtT.t[0]))
                                            k.tt("dve", A(rtT), A(rtT), (rT[:, sl], rT.t[0]), ALU.mult)
                                            k.act(A(x1), A(x2), AF.Exp, scale=CW)
                                            k.tt("dve", A(btT), A(btT), A(x1), ALU.mult)
                                            k.tt("dve", A(ktT), A(ktT), A(x1), ALU.mult)
                                        for c in range(4):
                                            cs_ = slice(c * 128, (c + 1) * 128)
                                            bk, bt = k.bank("a")
                                            for i, src_ in enumerate((btT, ktT, atT)):
                                                k.tr((bk[:, i * 128:(i + 1) * 128], bt), (src_[:, cs_], src_.t[0]), ident)
                                            k.copy("act", (Tk[:, c, :, :], Tk.t[c]), (bk[:, 0:384].rearrange("p (i f) -> p i f", i=3), bt))
                                        with ExitStack() as cs2:
                                            MhA = [[P("Mh%d%d" % (i, e), [128, 512], F32, scope=cs2) for e in range(2)] for i in range(2)]
                                            ZBA = [[P("ZB%d%d" % (i, e), [128, 384], F32, scope=cs2) for e in range(2)] for i in range(2)]
                                            AtT = P("AtT", [128, 128], F32, scope=cs2)
                                            tS = P("tS", [128, 128], F32, scope=cs2)
                                            mS, mI = (m_us, m_ui) if d == 0 else (m_ls, m_li)
                                            mN = m_ls if d == 0 else m_us
                                            corder = (0, 1, 2, 3) if d == 0 else (3, 2, 1, 0)
                                            for g in range(2):
                                                cpair = corder[2 * g:2 * g + 2]
                                                chains = [(ci, c, e) for ci, c in enumerate(cpair) for e in range(2)]
                                                for ci, c in enumerate(cpair):
                                                    gc = hf * 4 + c
                                                    vp = Vp[gc % 2]
                                                    k.copy("dve", (vp[:, 0, 0:64], vp.t[0]), (Vtok[:, gc, 0:64], Vtok.t[gc]))
                                                    k.copy("dve", (vp[:, 1, 64:128], vp.t[0]), (Vtok[:, gc, 64:128], Vtok.t[gc]))
                                                b1s = {}
                                                for (ci, c, e) in chains:
                                                    rw_ = slice(64 * e, 64 * e + 64)
                                                    cs_ = slice(c * 128, (c + 1) * 128)
                                                    b1 = k.bank("a")
                                                    b1s[(ci, e)] = b1
                                                    for i, (lh, rh) in enumerate(((btT, atT), (ktT, atT), (btT, rtT), (ktT, rtT))):
                                                        k.mm((b1[0][:, i * 128:(i + 1) * 128], b1[1]), (lh[rw_, cs_], lh.t[0]), (rh[rw_, cs_], rh.t[0]),
                                                             inc=(i == 3))
                                                for (ci, c, e) in chains:
                                                    M_ = MhA[ci][e]
                                                    b1 = b1s[(ci, e)]
                                                    k.tt("dve", (M_[:, 0:128], M_.t[0]), (b1[0][:, 0:128], b1[1]), mS, ALU.mult)
                                                    k.tt("dve", (M_[:, 128:256], M_.t[0]), (b1[0][:, 128:256], b1[1]), mS, ALU.mult)
                                                    k.tt("dve", (M_[:, 256:384], M_.t[0]), (b1[0][:, 256:384], b1[1]), mI, ALU.mult)
                                                    k.tt("dve", (M_[:, 384:512], M_.t[0]), (b1[0][:, 384:512], b1[1]), mI, ALU.mult)
                                                b2s = {}
                                                for (ci, c, e) in chains:
                                                    rw_ = slice(64 * e, 64 * e + 64)
                                                    cs_ = slice(c * 128, (c + 1) * 128)
                                                    gc = hf * 4 + c
                                                    M_ = MhA[ci][e]
                                                    b2 = k.bank("b")
                                                    b2s[(ci, e)] = b2
                                                    k.mm((b2[0][:, 0:128], b2[1]), (atT[rw_, cs_], atT.t[0]), (btT[rw_, cs_], btT.t[0]), inc=False)
                                                    k.mm((b2[0][:, 128:192], b2[1]), (M_[:, 128:256], M_.t[0]), (Vtok[:, gc, 64 * e:64 * e + 64], Vtok.t[gc]))
                                                for (ci, c, e) in chains:
                                                    M_, Z_ = MhA[ci][e], ZBA[ci][e]
                                                    b2 = b2s[(ci, e)]
                                                    k.tt("dve", (Z_[:, 128:256], Z_.t[0]), (b2[0][:, 0:128], b2[1]), mN, ALU.mult)
                                                    k.copy("act", (Z_[:, 64:128], Z_.t[0]), (b2[0][:, 128:192], b2[1]))
                                                    k.copy("act", (Z_[:, 0:64], Z_.t[0]), (Tk[:, c, 2, 64 * e:64 * e + 64], Tk.t[c]))
                                                    k.copy("act", (Z_[:, 256:384], Z_.t[0]), (M_[:, 0:128], M_.t[0]))
                                                for lv in range(7):
                                                    b3s = {}
                                                    for (ci, c, e) in chains:
                                                        Z_ = ZBA[ci][e]
                                                        b3 = k.bank("b")
                                                        b3s[(ci, e)] = b3
                                                        if lv < 6:
                                                            k.mm((b3[0][:, 0:256], b3[1]), (Z_[:, 256:384], Z_.t[0]), (Z_[:, 0:256], Z_.t[0]), inc=False)
                                                            k.mm((b3[0][:, 256:384], b3[1]), (Z_[:, 128:256], Z_.t[0]), (Z_[:, 256:384], Z_.t[0]))
                                                        else:
                                                            k.mm((b3[0][:, 0:128], b3[1]), (Z_[:, 256:384], Z_.t[0]), (Z_[:, 0:128], Z_.t[0]))
                                                    for (ci, c, e) in chains:
                                                        Z_ = ZBA[ci][e]
                                                        b3 = b3s[(ci, e)]
                                                        k.tt("dve", (Z_[:, 0:128], Z_.t[0]), (Z_[:, 0:128], Z_.t[0]), (b3[0][:, 0:128], b3[1]), ALU.add)
                                                        if lv < 6:
                                                            k.copy("act", (Z_[:, 128:384], Z_.t[0]), (b3[0][:, 128:384], b3[1]))
                                                for ci, c in enumerate(cpair):
                                                    gc = hf * 4 + c
                                                    cs_ = slice(c * 128, (c + 1) * 128)
                                                    vp = Vp[gc % 2]; up = Up[gc % 2]
                                                    Mh, ZB = MhA[ci], ZBA[ci]
                                                    for e in range(2):
                                                        k.copy("act", (tS[:, 64 * e:64 * e + 64], tS.t[0]), (ZB[e][:, 0:64], ZB[e].t[0]))
                                                    b4 = k.bank("a")
                                                    k.tr((b4[0][:, 0:128], b4[1]), (tS[:, :], tS.t[0]), ident)
                                                    k.copy("act", (AtT[:, :], AtT.t[0]), (b4[0][:, 0:128], b4[1]))
                                                    s0t = S0[scur % 2]; s1t = S0[(scur + 1) % 2]
                                                    b5 = k.bank("a")
                                                    k.mm((b5[0][:, 0:128], b5[1]), (AtT[:, :], AtT.t[0]), (s0t[:, :], s0t.t[0]))
                                                    for e in range(2):
                                                        k.tt("dve", (up[:, e, 64 * e:64 * e + 64], up.t[0]), (b5[0][:, 64 * e:64 * e + 64], b5[1]),
                                                             (ZB[e][:, 64:128], ZB[e].t[0]), ALU.add)
                                                    b6 = k.bank("b")
                                                    k.mm((b6[0][:, 0:128], b6[1]), (s0t[:, :], s0t.t[0]), (rtT[:, cs_], rtT.t[0]), start=True, stop=False, inc=False)
                                                    for e in range(2):
                                                        k.mm((b6[0][:, 0:128], b6[1]), (up[:, e, :], up.t[0]), (Mh[e][:, 256:384], Mh[e].t[0]), start=False, stop=False, inc=False)
                                                        k.mm((b6[0][:, 0:128], b6[1]), (vp[:, e, :], vp.t[0]), (Mh[e][:, 384:512], Mh[e].t[0]), start=False, stop=(e == 1), inc=(e == 1))
                                                    ys = (yacc[:, gc * 128:(gc + 1) * 128], yacc.t[0])
                                                    if d == 0:
                                                        k.copy("act", ys, (b6[0][:, 0:128], b6[1]))
                                                    else:
                                                        k.tt("dve", ys, ys, (b6[0][:, 0:128], b6[1]), ALU.add)
                                                    b7 = k.bank("a")
                                                    k.mm((b7[0][:, 0:128], b7[1]), (Tk[:, c, 0, :], Tk.t[c]), (up[:, 0, :], up.t[0]), start=True, stop=False, inc=False)
                                                    k.mm((b7[0][:, 0:128], b7[1]), (Tk[:, c, 0, :], Tk.t[c]), (up[:, 1, :], up.t[0]), start=False, stop=False, inc=False)
                                                    k.mm((b7[0][:, 0:128], b7[1]), (Tk[:, c, 1, :], Tk.t[c]), (Vtok[:, gc, :], Vtok.t[gc]), start=False, stop=True)
                                                    k.tt("dve", (tS[:, :], tS.t[0]), (b7[0][:, 0:128], b7[1]), (s0t[:, :], s0t.t[0]), ALU.add)
                                                    k.stt("dve", (s1t[:, :], s1t.t[0]), (tS[:, :], tS.t[0]), (wful[:, gc:gc + 1], wful.t[0]), bd, ALU.mult, ALU.mult)
                                                    scur += 1
                                                    seq_end = (gc % 2 == 1) if d == 0 else (gc % 2 == 0)
                                                    if seq_end:
                                                        k.dma("sp", "o_S0_%d" % ((scur) % 2), st_out[l, d, gc // 2, p], (s1t[:, :], s1t.t[0]))
                                                        k.ts("dve", (s1t[:, :], s1t.t[0]), (s1t[:, :], s1t.t[0]), (kf[:, 0:1], kf.t[0]), None, ALU.mult)
                        with ExitStack() as gs_:
                            HB = lambda nm: P(nm, [128, 512], F32, scope=gs_)
                            mu_, rs_, u_, ts_ = HB("gmu"), HB("grs"), HB("gu"), HB("gts")
                            A = lambda b_: (b_[:, :], b_.t[0])
                            for hf in range(2):
                                sl = slice(hf * 512, (hf + 1) * 512)
                                ya = (yacc[:, sl], yacc.t[0])
                                b1 = k.bank("a")
                                k.mm(b1, bd, ya)
                                k.act(A(u_), ya, AF.Square)
                                b2 = k.bank("a")
                                k.mm(b2, bd, A(u_))
                                k.ts("dve", A(mu_), b1, 1.0 / 64, None, ALU.mult)
                                k.tt("dve", A(rs_), A(mu_), A(mu_), ALU.mult)
                                k.stt("dve", A(rs_), b2, 1.0 / 64, A(rs_), ALU.mult, ALU.subtract)
                                k.act(A(rs_), A(rs_), AF.Sqrt, bias=GN_EPS)
                                k.S.op("dve", lambda h: h.reciprocal(out=rs_[:, :], in_=rs_[:, :]), W=[rs_.t[0]])
                                k.tt("dve", ya, ya, A(mu_), ALU.subtract)
                                k.tt("dve", ya, ya, A(rs_), ALU.mult)
                                k.S.op("dve", lambda h, sl=sl: h.tensor_scalar(out=yacc[:, sl], in0=yacc[:, sl], scalar1=pvc("lnx_g", p)[0], scalar2=pvc("lnx_b", p)[0],
                                                                            op0=ALU.mult, op1=ALU.add), R=[pv.t[0]], W=[yacc.t[0]])
                                asig_half(0, hf, A(ts_))
                                asig_half(1, hf, A(u_))
                                k.tt("dve", A(ts_), A(ts_), A(u_), ALU.add)
                                k.ts("dve", A(ts_), A(ts_), pvc("ka", p), None, ALU.mult)
                                k.S.op("dve", lambda h: h.tensor_scalar(out=ts_[:, :], in0=ts_[:, :], scalar1=pvc("ka", p)[0], scalar2=1.0, op0=ALU.subtract, op1=ALU.add),
                                       R=[pv.t[0]], W=[ts_.t[0]])
                                k.S.op("dve", lambda h: h.tensor_scalar(out=ts_[:, :], in0=ts_[:, :], scalar1=pvc("ka", p)[0], scalar2=1.0, op0=ALU.subtract, op1=ALU.add),
                                       R=[pv.t[0]], W=[ts_.t[0]])
                                k.tt("dve", A(u_), (rT[:, sl], rT.t[0]), (kT_[:, sl], kT_.t[0]), ALU.mult)
                                k.stt("dve", A(u_), A(u_), pvc("rk", p), A(ts_), ALU.mult, ALU.mult)
                                b3 = k.bank("a")
                                k.mm(b3, bd, A(u_))
                                k.tt("dve", A(u_), b3, (vT[:, sl], vT.t[0]), ALU.mult)
                                k.tt("dve", ya, ya, A(u_), ALU.add)
                                b4 = k.bank("a")
                                k.mm(b4, (g2t[:, :], g2t.t[0]), (sgd[:, sl], sgd.t[0]))
                                k.tt("dve", (ocT[:, p, sl], ocT.t[p]), ya, b4, ALU.mult)

        g1c, g2c = 32, 80
        try:
          chk(1)
          for l in range(NL):
            k.dma("sp", "pv", (pv[:, :], pv.t[0]), pvec[l])
            adaln(l)
            chk(2)
            modulate(0)
            chk(3)
            mixer_phase(l)
            chk(4)
            layernorm(0, ALPHA)
            modulate(1)
            chk(5)
            if stage >= 2:
                with ExitStack() as sc:
                    hid = [P("hid%d" % i, [128, 11, T], BF16, ntok=11, scope=sc) for i in range(2)]
                    sg = [P("sg%d" % i, [128, 512], F32, scope=sc) for i in range(2)]
                    sgi = [0]
                    for ps in range(4):
                        hb = hid[ps % 2]
                        jobs = []
                        for j in range(11):
                            ht = ps * 11 + j
                            def ld(ht=ht):
                                return wload([(lambda b: wview(b, KT, 256)[:, :, 0:128], wsrc(w_f1[l], ht * 128, 128)),
                                              (lambda b: wview(b, KT, 256)[:, :, 128:256], wsrc(w_f1[l], FH + ht * 128, 128))])
                            def use(b, j=j, hb=hb):
                                wv = wview(b, KT, 256)
                                for hf in range(2):
                                    bg = k.bank("a"); bu = k.bank("a")
                                    for kt in range(KT):
                                        k.mm(bg, (wv[:, kt, 0:128], b.t[0]), hsrc(kt, hf), start=(kt == 0), stop=(kt == KT - 1))
                                    for kt in range(KT):
                                        k.mm(bu, (wv[:, kt, 128:256], b.t[0]), hsrc(kt, hf), start=(kt == 0), stop=(kt == KT - 1))
                                    s_ = sg[sgi[0] % 2]; sgi[0] += 1
                                    k.act((s_[:, :], s_.t[0]), bg, AF.Silu)
                                    k.tt("dve", (hb[:, j, hf * 512:(hf + 1) * 512], hb.t[j]), (s_[:, :], s_.t[0]), bu, ALU.mult)
                            jobs.append((ld, use))
                        for cg in range(8):
                            def ld2(cg=cg, ps=ps):
                                return wload([(lambda b: wview(b, 11, 256), wsrc(w_f2[l], cg * 256, 256, ps * 11, 11))])
                            def use2(b, cg=cg, hb=hb):
                                wv = wview(b, 11, 256)
                                for jj in range(2):
                                    kt_o = cg * 2 + jj
                                    for hf in range(2):
                                        bk = k.bank("a")
                                        for j in range(11):
                                            k.mm(bk, (wv[:, j, jj * 128:(jj + 1) * 128], b.t[0]),
                                                 (hb[:, j, hf * 512:(hf + 1) * 512], hb.t[j]), start=(j == 0), stop=(j == 10))
                                        sl = slice(hf * 512, (hf + 1) * 512)
                                        k.stt("dve", (xT[:, kt_o, sl], xT.t[kt_o]), bk, (mod[:, g2c + kt_o:g2c + kt_o + 1], mod.t[0]),
                                              (xT[:, kt_o, sl], xT.t[kt_o]), ALU.mult, ALU.add)
                            jobs.append((ld2, use2))
                        stream(jobs)
            layernorm(1, ALPHA if l < NL - 1 else 1.0)
        except _Stop:
            pass

        with ExitStack() as sc:
            yst = [P("yst%d" % i, [128, D], F32, scope=sc) for i in range(2)]
            for tt in range(8):
                st = yst[tt % 2]
                for kq in range(4):
                    bk, bt = k.bank("a")
                    for j in range(4):
                        kt = kq * 4 + j
                        k.tr((bk[:, j * 128:(j + 1) * 128], bt), (xT[:, kt, tt * 128:(tt + 1) * 128], xT.t[kt]), ident)
                    k.copy("dve" if kq % 2 == 0 else "act", (st[:, kq * 512:(kq + 1) * 512], st.t[0]), (bk[:, :], bt))
                k.dma("sp", "o_yst%d" % (tt % 2), y_out[tt * 128:(tt + 1) * 128, :], (st[:, :], st.t[0]))
        for c_ in S.chans:
            if c_.val > 0:
                S.E["sp"].h.wait_ge(c_.sem, c_.val)
    return nc


def _fm(v):
    v = np.asarray(v, np.float32)
    n = v.shape[-1] // 128
    return np.ascontiguousarray(np.swapaxes(v.reshape(v.shape[:-1] + (n, 128)), -1, -2))


def _consts():
    c = np.zeros((128, 1024), np.float32)
    p = np.arange(128)[:, None]
    f = np.arange(128)[None, :]
    c[:, 0:128] = (p == f)
    c[:, 128:256] = ((p // 64) == (f // 64))
    c[:, 256:384] = (f > p)
    c[:, 384:512] = (f >= p)
    c[:, 512:640] = (f < p)
    c[:, 640:768] = (f <= p)
    c[:, 768:896] = 1.0
    for m in range(128):
        if m % 32 < 16:
            c[m + 16, 896 + m] = -1.0
        else:
            c[m - 16, 896 + m] = 1.0
    return c


def _pvec(inp):
    out = np.zeros((L, 128, NPV), np.float32)
    def put(nm, arr):
        o, w = PV[nm]
        out[:, :, o:o + w] = arr.reshape(L, 128, w)
    put("b_ada", _fm(inp["b_ada"].reshape(L, 6, D)).transpose(0, 2, 1, 3))
    for nm in ("ln1_g", "ln1_b", "ln2_g", "ln2_b"):
        put(nm, _fm(inp[nm]))
    put("subln", _fm(inp["diff_subln"]))
    put("mix0", _fm(inp["rwkv_mix"][:, 0]))
    put("mix1", _fm(inp["rwkv_mix"][:, 1]))
    put("a0", _fm(inp["rwkv_a0"]).transpose(0, 2, 1, 3))
    put("w0", _fm(inp["rwkv_w0"]).transpose(0, 2, 1, 3))
    put("kk", _fm(inp["rwkv_kk"]))
    put("ka", _fm(inp["rwkv_ka"]))
    put("rk", _fm(inp["rwkv_rk"].reshape(L, 512)))
    put("lnx_g", _fm(inp["rwkv_lnx_g"]))
    put("lnx_b", _fm(inp["rwkv_lnx_b"]))
    lam = np.zeros((L, 128, 4), np.float32)
    lam[:, 0:64, :] = np.swapaxes(np.asarray(inp["diff_lambda"], np.float32), 1, 2)
    put("lam", lam)
    return out


def _na_index():
    q = np.arange(T); rq, cq = q // 64, q % 64
    kk = np.arange(T); rk, ck = kk // 64, kk % 64
    row_start = np.clip(rq - 4, 0, 16 - 8)
    win_c0 = np.clip(cq - 8, 0, 64 - 16)
    inw = ((rk[None, :] >= row_start[:, None]) & (rk[None, :] < row_start[:, None] + 8) &
           (ck[None, :] >= win_c0[:, None]) & (ck[None, :] < win_c0[:, None] + 16))
    rr = np.clip(rk[None, :] - rq[:, None] + 7, 0, 14)
    rc = np.clip(ck[None, :] - cq[:, None], -15, 15) + 15
    return inw, rr, rc


def _tiles(bias_qk):
    b = np.swapaxes(bias_qk, -1, -2)
    sh = b.shape[:-2]
    b = b.reshape(sh + (10, 128, 2, 512))
    return np.ascontiguousarray(np.moveaxis(b, -2, -4))


def _ctx_mask():
    m = np.full((T, T + 256), NEG, np.float32)
    for s_ in range(4):
        m[s_ * 256:(s_ + 1) * 256, s_ * 256:(s_ + 1) * 256] = 0.0
    return m


def _rope_tables():
    t = np.arange(T)
    rows = (t // 64).astype(np.float32); cols = (t % 64).astype(np.float32)
    half = 32
    inv = (10000.0 ** (-np.arange(0, half, 2, dtype=np.float32) / half)).astype(np.float32)
    ang_r = rows[:, None] * inv; ang_c = cols[:, None] * inv
    ang = np.concatenate([ang_r, ang_r, ang_c, ang_c], -1)
    ang = np.concatenate([ang, ang], -1).T
    return np.stack([np.cos(ang), np.sin(ang)], 0).astype(np.float32)


_CACHE = {}


def kernel(**inp):
    inp = {k_: np.asarray(v) for k_, v in inp.items()}
    if "nc" not in _CACHE:
        nc = bass.Bass("TRN2", target_bir_lowering=False)
        build(nc)
        _CACHE["nc"] = nc
    nc = _CACHE["nc"]
    consts = _consts()
    pvec = _pvec(inp)
    shared = {"consts": consts, "pvec": pvec[:NL]}
    for nm in ("w_ada", "w_in", "w_branch", "w_out", "w_ffn_in", "w_ffn_out"):
        shared[nm] = np.ascontiguousarray(inp[nm][:NL], dtype=np.float32)
    inw, rr, rc = _na_index()
    rpb = np.asarray(inp["na_rpb"], np.float32)[:NL]
    lat_bias = np.zeros((NL, 4, T, T + 256), np.float32)
    lat_bias[:, :, :, :T] = np.where(inw[None, None], rpb[:, :, rr, rc], np.float32(NEG))
    nab_lat = _tiles(lat_bias)
    cm = _ctx_mask()
    nab_ctx = np.ascontiguousarray(np.broadcast_to(_tiles(cm)[None, None], (NL, 4, 2, 10, 128, 512)))
    dm_lat = np.zeros((2, 10, 128, 512), np.float32)
    dm_ctx = _tiles(cm)
    rope_lat = _rope_tables()
    rope_ctx = np.stack([np.ones((128, T), np.float32), np.zeros((128, T), np.float32)], 0)
    zkv = np.zeros((NL, 2, 256, 512), np.float32)
    shared["rw_w2"] = np.ascontiguousarray(inp["rwkv_w2"][:NL].reshape(NL, 128, 512), dtype=np.float32)
    shared["rw_a2"] = np.ascontiguousarray(inp["rwkv_a2"][:NL].reshape(NL, 128, 512), dtype=np.float32)
    shared["rw_g2"] = np.ascontiguousarray(inp["rwkv_g2"][:NL], dtype=np.float32)
    zs0 = np.zeros((NL, 2, 4, 128, 128), np.float32)
    in_maps = []
    for c in range(8):
        m = dict(shared)
        if c < 4:
            m["x_in"] = np.ascontiguousarray(inp["x_sample"][c])
            m["cond"] = _fm(inp["c"][c])
            m["nab"] = nab_lat; m["dmask"] = dm_lat; m["rope"] = rope_lat
            m["ext_k"] = np.ascontiguousarray(np.stack([inp["cache_na_k"][c, :NL].reshape(NL, 256, 512),
                                                         inp["cache_diff_k"][c, :NL].reshape(NL, 256, 512)], 1))
            m["ext_v"] = np.ascontiguousarray(np.stack([inp["cache_na_v"][c, :NL].reshape(NL, 256, 512),
                                                         inp["cache_diff_v"][c, :NL].reshape(NL, 256, 512)], 1))
            s0 = np.zeros((NL, 2, 4, 128, 128), np.float32)
            for d_, nm in enumerate(("state_rwkv_fwd", "state_rwkv_bwd")):
                st = np.asarray(inp[nm][c, :NL], np.float32)
                for h in range(8):
                    e = h % 2
                    s0[:, d_, h // 2, 64 * e:64 * e + 64, 64 * e:64 * e + 64] = np.swapaxes(st[:, h], -1, -2)
            m["s0"] = s0
            m["keepf"] = np.ones((128, 1), np.float32)
        else:
            j = c - 4
            m["x_in"] = np.ascontiguousarray(inp["x_prompt"][4 * j:4 * j + 4].reshape(T, D))
            m["cond"] = _fm(inp["c_ctx"])
            m["nab"] = nab_ctx; m["dmask"] = dm_ctx; m["rope"] = rope_ctx
            m["ext_k"] = zkv; m["ext_v"] = zkv
            m["s0"] = zs0
            m["keepf"] = np.zeros((128, 1), np.float32)
        in_maps.append(m)
    if os.environ.get("KSIM"):
        return nc, in_maps
    res = run_bass_kernel_spmd(nc, in_maps, core_ids=list(range(8)))
    R = res.results
    y_sample = np.stack([R[c]["y"] for c in range(4)], 0)
    y_prompt = np.concatenate([R[c]["y"].reshape(4, 256, D) for c in range(4, 8)], 0)
    kv = np.concatenate([R[c]["kv"].reshape(NL, 4, 4, 256, 512) for c in range(4, 8)], 2)
    kv = kv.transpose(1, 2, 0, 3, 4)
    new_na_k = np.ascontiguousarray(kv[0]).reshape(16, NL, 256, 4, 128)
    new_na_v = np.ascontiguousarray(kv[1]).reshape(16, NL, 256, 4, 128)
    new_diff_k = np.ascontiguousarray(kv[2]).reshape(16, NL, 256, 4, 2, 64)
    new_diff_v = np.ascontiguousarray(kv[3]).reshape(16, NL, 256, 4, 128)
    st = np.concatenate([R[c]["st"] for c in range(4, 8)], 2)
    fin = np.zeros((2, 16, NL, 8, 64, 64), np.float32)
    for h in range(8):
        e = h % 2
        blk = st[:, :, :, h // 2, 64 * e:64 * e + 64, 64 * e:64 * e + 64]
        fin[:, :, :, h] = np.transpose(blk, (1, 2, 0, 4, 3))
    return (y_prompt, y_sample, new_na_k, new_na_v, new_diff_k, new_diff_v, fin[0], fin[1])
```

```python
import os
import numpy as np
from contextlib import ExitStack
import concourse.bass as bass
import concourse.mybir as mybir
from concourse.bass_utils import run_bass_kernel_spmd

F32 = mybir.dt.float32
BF16 = mybir.dt.bfloat16
AF = mybir.ActivationFunctionType
ALU = mybir.AluOpType

D = 2048
L = 4
NL = int(os.environ.get('KL', '4'))
T = 1024
KT = 16
NIN = 11136
FH = 5632
ALPHA = float((2.0 * L) ** 0.25)
LN_EPS = 1e-5
GN_EPS = 64e-5
NEG = -30000.0
C_NAQ, C_NAK, C_NAV, C_DFQ, C_DFK, C_DFV, C_RW, C_GATE = 0, 512, 1024, 1536, 2048, 2560, 3072, 4992

PV = {}
_o = 0
for _n, _w in (("b_ada", 96), ("ln1_g", 16), ("ln1_b", 16), ("ln2_g", 16), ("ln2_b", 16),
               ("subln", 4), ("mix0", 15), ("mix1", 15), ("a0", 8), ("kk", 4), ("ka", 4),
               ("rk", 4), ("lnx_g", 4), ("lnx_b", 4), ("lam", 4), ("w0", 8)):
    PV[_n] = (_o, _w)
    _o += _w
NPV = _o


class _Stop(Exception):
    pass


def chk(n):
    if int(os.environ.get("KSTOP", "0")) == n:
        raise _Stop()


class Tok:
    __slots__ = ("w", "r", "excl")

    def __init__(self, init_r=None, excl=False):
        self.w = None
        self.r = dict(init_r) if init_r else {}
        self.excl = excl


class Eng:
    def __init__(self, nm, h, sem):
        self.nm, self.h, self.sem = nm, h, sem
        self.cnt = 0
        self.seen = {}


class Chan:
    def __init__(self, sem):
        self.sem = sem
        self.val = 0


class Sched:
    def __init__(self, nc, es):
        self.nc, self.es = nc, es
        self.E = {}
        for nm, h in (("pe", nc.tensor), ("act", nc.scalar), ("dve", nc.vector),
                      ("pool", nc.gpsimd), ("sp", nc.sync)):
            self.E[nm] = Eng(nm, h, es.enter_context(nc.semaphore("sem_" + nm)))
        self.grave = {}
        self.nbuf = 0
        self.chans = []
        self.chmap = {}

    def chan(self):
        c = Chan(self.es.enter_context(self.nc.semaphore("ch%d" % len(self.chans))))
        self.chans.append(c)
        return c

    def chan_for(self, key):
        if key not in self.chmap:
            self.chmap[key] = self.chan()
        return self.chmap[key]

    def tok(self):
        return Tok(self.grave)

    def _wait(self, e, ev):
        sem, val, src = ev
        if isinstance(src, Chan):
            val = src.val
        elif src is e and e.nm == "pe":
            return
        k = id(sem)
        if e.seen.get(k, 0) >= val:
            return
        e.h.wait_ge(sem, val)
        e.seen[k] = val

    def _deps(self, e, R, W):
        for t in R:
            if t.w is not None:
                self._wait(e, t.w)
            if t.excl:
                for ev in list(t.r.values()):
                    if ev[2] is not e:
                        self._wait(e, ev)
        for t in W:
            if t.w is not None:
                self._wait(e, t.w)
            for ev in list(t.r.values()):
                self._wait(e, ev)

    @staticmethod
    def _mark(ev, R, W):
        k = id(ev[0])
        for t in R:
            t.r[k] = ev
        for t in W:
            t.w = ev
            t.r = {}

    def op(self, en, fn, R=(), W=(), inc=True):
        e = self.E[en]
        self._deps(e, R, W)
        ins = fn(e.h)
        if inc:
            e.cnt += 1
            ins.then_inc(e.sem, 1)
            ev = (e.sem, e.cnt, e)
        else:
            ev = (e.sem, e.cnt + 1, e)
        self._mark(ev, R, W)

    def dma(self, qn, ch, out, in_, R=(), W=()):
        e = self.E[qn]
        self._deps(e, R, W)
        e.h.dma_start(out=out, in_=in_).then_inc(ch.sem, 16)
        ch.val += 16
        self._mark((ch.sem, ch.val, ch), R, W)

    def bury(self, toks):
        for t in toks:
            evs = list(t.r.values())
            if t.w is not None:
                evs.append(t.w)
            for ev in evs:
                k = id(ev[0])
                val = ev[2].val if isinstance(ev[2], Chan) else ev[1]
                old = self.grave.get(k)
                if old is None or old[1] < val:
                    self.grave[k] = (ev[0], val, ev[2])


class Buf:
    def __init__(self, S, scope, name, shape, dt, ntok=1):
        S.nbuf += 1
        self.name = name
        self.h = scope.enter_context(S.nc.sbuf_tensor("%s_%d" % (name, S.nbuf), list(shape), dt))
        self.t = [S.tok() for _ in range(ntok)]
        scope.callback(S.bury, self.t)

    def __getitem__(self, idx):
        return self.h[idx]


def _sp(x):
    if isinstance(x, tuple):
        a, t = x
        return a, (list(t) if isinstance(t, (list, tuple)) else [t])
    return x, []


class K:
    def __init__(self, nc, es, dbg):
        self.nc, self.es = nc, es
        self.S = Sched(nc, es)
        self.dbg = dbg

    def mm(self, out, lhsT, rhs, start=True, stop=True, inc=None):
        o, ot = _sp(out); l, lt = _sp(lhsT); r, rt = _sp(rhs)
        self.S.op("pe", lambda h: h.matmul(o, lhsT=l, rhs=r, start=start, stop=stop),
                  R=lt + rt, W=ot, inc=(stop if inc is None else inc))

    def tr(self, out, in_, ident):
        o, ot = _sp(out); i, it = _sp(in_); d, dt_ = _sp(ident)
        self.S.op("pe", lambda h: h.transpose(o, i, d), R=it + dt_, W=ot)

    def act(self, out, in_, func, bias=None, scale=1.0, eng="act"):
        o, ot = _sp(out); i, it = _sp(in_); b, bt = _sp(bias); s, st = _sp(scale)
        kw = {}
        if b is not None:
            kw["bias"] = b
        self.S.op(eng, lambda h: h.activation(out=o, in_=i, func=func, scale=s, **kw),
                  R=it + bt + st, W=ot)

    def ts(self, eng, out, in0, s1, s2, op0, op1=None):
        o, ot = _sp(out); i, it = _sp(in0); a, at = _sp(s1); b, bt = _sp(s2)
        if op1 is None:
            self.S.op(eng, lambda h: h.tensor_scalar(out=o, in0=i, scalar1=a, scalar2=None, op0=op0),
                      R=it + at, W=ot)
        else:
            self.S.op(eng, lambda h: h.tensor_scalar(out=o, in0=i, scalar1=a, scalar2=b, op0=op0, op1=op1),
                      R=it + at + bt, W=ot)

    def tt(self, eng, out, in0, in1, op):
        o, ot = _sp(out); i, it = _sp(in0); j, jt = _sp(in1)
        self.S.op(eng, lambda h: h.tensor_tensor(out=o, in0=i, in1=j, op=op), R=it + jt, W=ot)

    def stt(self, eng, out, in0, sc, in1, op0, op1):
        o, ot = _sp(out); i, it = _sp(in0); s, st = _sp(sc); j, jt = _sp(in1)
        self.S.op(eng, lambda h: h.scalar_tensor_tensor(out=o, in0=i, scalar=s, in1=j, op0=op0, op1=op1),
                  R=it + st + jt, W=ot)

    def copy(self, eng, out, in_):
        o, ot = _sp(out); i, it = _sp(in_)
        if eng == "act":
            self.S.op(eng, lambda h: h.activation(out=o, in_=i, func=AF.Copy), R=it, W=ot)
        else:
            self.S.op(eng, lambda h: h.tensor_copy(out=o, in_=i), R=it, W=ot)

    def memset(self, eng, out, val):
        o, ot = _sp(out)
        self.S.op(eng, lambda h: h.memset(o, val), W=ot)

    def dma(self, q, ch, out, in_):
        o, ot = _sp(out); i, it = _sp(in_)
        if isinstance(ch, str):
            ch = self.S.chan_for(ch)
        self.S.dma(q, ch, o, i, R=it, W=ot)

    def setup_psum(self):
        self.banks = []
        for i in range(8):
            h = self.es.enter_context(self.nc.psum_tensor("bank%d" % i, [128, 512], F32))
            self.banks.append((h[:, :], Tok(excl=True)))
        self._bk = {"a": [0, 0, 4], "b": [0, 4, 4]}

    def bank(self, pool="a"):
        st = self._bk[pool]
        i = st[1] + st[0] % st[2]
        st[0] += 1
        return self.banks[i]


def build(nc, dbg=None):
    stage = int(os.environ.get("KSTAGE", "99"))
    es = ExitStack()
    with es:
        k = K(nc, es, dbg)
        S = k.S
        k.setup_psum()

        def din(name, shape, dt=F32):
            return nc.dram_tensor(name, list(shape), dt, kind="ExternalInput").ap()

        def dout(name, shape, dt=F32):
            return nc.dram_tensor(name, list(shape), dt, kind="ExternalOutput").ap()

        x_in = din("x_in", [T, D])
        cond = din("cond", [128, KT])
        consts = din("consts", [128, 1024])
        pvec = din("pvec", [NL, 128, NPV])
        w_ada = din("w_ada", [NL, D, 6 * D])
        w_in = din("w_in", [NL, D, NIN])
        w_branch = din("w_branch", [NL, 3, 512, D])
        w_out = din("w_out", [NL, D, D])
        w_f1 = din("w_ffn_in", [NL, D, 2 * FH])
        w_f2 = din("w_ffn_out", [NL, FH, D])
        nab = din("nab", [NL, 4, 2, 10, 128, 512])
        aug_in = din("aug", [2, 32, T + 256])
        rope = din("rope", [2, 128, T])
        ext_k = din("ext_k", [NL, 2, 256, 512])
        ext_v = din("ext_v", [NL, 2, 256, 512])
        rw_w2 = din("rw_w2", [NL, 128, 512])
        rw_a2 = din("rw_a2", [NL, 128, 512])
        rw_g2 = din("rw_g2", [NL, 128, 512])
        s0_in = din("s0", [NL, 2, 4, 128, 128])
        keepf = din("keepf", [128, 1])
        st_out = dout("st", [NL, 2, 4, 4, 128, 128])
        y_out = dout("y", [T, D])
        kv_out = dout("kv", [NL, 4, T, 512])

        top = es
        P = lambda name, shape, dt, ntok=1, scope=None: Buf(S, scope or top, name, shape, dt, ntok)

        xT = P("xT", [128, KT, T], F32, ntok=KT)
        hT = P("hT", [128, KT, T], BF16, ntok=KT)
        cst = P("cst", [128, 1024], F32)
        pv = P("pv", [128, NPV], F32)
        mod = P("mod", [128, 96], F32)
        mod1p = P("mod1p", [128, 32], F32)
        lnp = P("lnp", [128, 64], F32)
        scond = P("scond", [128, KT], BF16)
        condt = P("condt", [128, KT], F32)
        onesb = P("onesb", [128, 128], BF16)
        NWB = 3
        wring = [P("wr%d" % i, [128, 4096], BF16) for i in range(NWB)]
        wch = [S.chan() for _ in range(NWB)]
        wcur = [0]
        ch_misc = S.chan()
        ch_pool = S.chan()
        ch_out = S.chan()

        ident = (cst[:, 0:128], cst.t[0])

        k.dma("sp", "cst", (cst[:, :], cst.t[0]), consts)
        k.dma("sp", "condt", (condt[:, :], condt.t[0]), cond)
        k.act((scond[:, :], scond.t[0]), (condt[:, :], condt.t[0]), AF.Silu)
        k.memset("dve", (onesb[:, :], onesb.t[0]), 1.0)

        with ExitStack() as sc:
            xin = [P("xin%d" % i, [128, D], F32, scope=sc) for i in range(2)]
            chx = [S.chan(), S.chan()]
            for tt in range(8):
                b = xin[tt % 2]
                k.dma("sp", chx[tt % 2], (b[:, :], b.t[0]), x_in[tt * 128:(tt + 1) * 128, :])
                for kq in range(4):
                    bk, bt = k.bank("a")
                    for j in range(4):
                        kt = kq * 4 + j
                        k.tr((bk[:, j * 128:(j + 1) * 128], bt), (b[:, kt * 128:(kt + 1) * 128], b.t[0]), ident)
                    for j in range(4):
                        kt = kq * 4 + j
                        eng = "dve" if j % 2 == 0 else "act"
                        if eng == "dve":
                            k.ts("dve", (xT[:, kt, tt * 128:(tt + 1) * 128], xT.t[kt]),
                                 (bk[:, j * 128:(j + 1) * 128], bt), ALPHA, None, ALU.mult)
                        else:
                            k.act((xT[:, kt, tt * 128:(tt + 1) * 128], xT.t[kt]),
                                  (bk[:, j * 128:(j + 1) * 128], bt), AF.Copy, scale=ALPHA)

        def wload(srcs):
            i = wcur[0] % NWB
            wcur[0] += 1
            b = wring[i]
            for dst, src in srcs:
                k.dma("pool", wch[i], (dst(b), b.t[0]), src)
            return b

        def wview(b, kt_n, ncols):
            return b.h[:, 0:kt_n * ncols].rearrange("p (k n) -> p k n", k=kt_n)

        def stream(jobs, depth=NWB - 1):
            bufs = {}
            n = len(jobs)
            for i in range(min(depth, n)):
                bufs[i] = jobs[i][0]()
            for i in range(n):
                jobs[i][1](bufs.pop(i))
                if i + depth < n:
                    bufs[i + depth] = jobs[i + depth][0]()

        def wsrc(w2d, c0, nc_, kt0=0, ktn=KT):
            return w2d[kt0 * 128:(kt0 + ktn) * 128, c0:c0 + nc_].rearrange("(k p) n -> p k n", p=128)

        def stats_norm(eps, emit):
            with ExitStack() as sc:
                MU = P("MU", [128, T], F32, scope=sc)
                RS = P("RS", [128, T], F32, scope=sc)
                cb = [P("cb%d" % i, [128, 512], BF16, scope=sc) for i in range(2)]
                sq = [P("sq%d" % i, [128, 512], BF16, scope=sc) for i in range(2)]
                tmp = [P("nt%d" % i, [128, T], F32, scope=sc) for i in range(2)]
                s1 = [k.bank("b") for _ in range(2)]
                s2 = [k.bank("b") for _ in range(2)]
                i = 0
                for kt in range(KT):
                    for hf in range(2):
                        sl = slice(hf * 512, (hf + 1) * 512)
                        c, q = cb[i % 2], sq[i % 2]
                        i += 1
                        k.copy("dve", (c[:, :], c.t[0]), (xT[:, kt, sl], xT.t[kt]))
                        k.act((q[:, :], q.t[0]), (xT[:, kt, sl], xT.t[kt]), AF.Square)
                        k.mm(s1[hf], (onesb[:, :], onesb.t[0]), (c[:, :], c.t[0]), start=(kt == 0), stop=(kt == KT - 1), inc=True)
                        k.mm(s2[hf], (onesb[:, :], onesb.t[0]), (q[:, :], q.t[0]), start=(kt == 0), stop=(kt == KT - 1), inc=True)
                for hf in range(2):
                    sl = slice(hf * 512, (hf + 1) * 512)
                    k.ts("dve", (MU[:, sl], MU.t[0]), s1[hf], 1.0 / D, None, ALU.mult)
                    k.tt("dve", (RS[:, sl], RS.t[0]), (MU[:, sl], MU.t[0]), (MU[:, sl], MU.t[0]), ALU.mult)
                    k.stt("dve", (RS[:, sl], RS.t[0]), s2[hf], 1.0 / D, (RS[:, sl], RS.t[0]), ALU.mult, ALU.subtract)
                    k.act((RS[:, sl], RS.t[0]), (RS[:, sl], RS.t[0]), AF.Sqrt, bias=eps)
                    k.S.op("dve", lambda h, sl=sl: h.reciprocal(out=RS[:, sl], in_=RS[:, sl]), W=[RS.t[0]])
                for kt in range(KT):
                    tm = tmp[kt % 2]
                    k.tt("dve", (tm[:, :], tm.t[0]), (xT[:, kt, :], xT.t[kt]), (MU[:, :], MU.t[0]), ALU.subtract)
                    k.tt("dve", (tm[:, :], tm.t[0]), (tm[:, :], tm.t[0]), (RS[:, :], RS.t[0]), ALU.mult)
                    emit(kt, tm)

        def modulate(which):
            sh0 = 0 if which == 0 else 48
            def emit(kt, tm):
                k.act((hT[:, kt, :], hT.t[kt]), (tm[:, :], tm.t[0]), AF.Identity,
                      bias=(mod[:, sh0 + kt:sh0 + kt + 1], mod.t[0]),
                      scale=(mod1p[:, which * 16 + kt:which * 16 + kt + 1], mod1p.t[0]))
            stats_norm(ALPHA * ALPHA * LN_EPS, emit)

        def layernorm(which, out_alpha):
            o = which * 32 if out_alpha != 1.0 else None
            def emit(kt, tm):
                if out_alpha != 1.0:
                    gs = (lnp[:, which * 32 + kt:which * 32 + kt + 1], lnp.t[0])
                    bs = (lnp[:, which * 32 + 16 + kt:which * 32 + 17 + kt], lnp.t[0])
                else:
                    g0 = PV["ln2_g"][0]; b0 = PV["ln2_b"][0]
                    gs = (pv[:, g0 + kt:g0 + kt + 1], pv.t[0])
                    bs = (pv[:, b0 + kt:b0 + kt + 1], pv.t[0])
                k.act((xT[:, kt, :], xT.t[kt]), (tm[:, :], tm.t[0]), AF.Identity, bias=bs, scale=gs)
            stats_norm(LN_EPS, emit)

        def adaln(l):
            bk, bt = k.bank("a")
            jobs = []
            for cgi in range(48):
                def ld(cgi=cgi):
                    return wload([(lambda b: wview(b, KT, 256), wsrc(w_ada[l], cgi * 256, 256))])
                def use(b, cgi=cgi):
                    wv = wview(b, KT, 256)
                    for j in range(2):
                        col = cgi * 2 + j
                        for kt in range(KT):
                            k.mm((bk[:, col:col + 1], bt), (wv[:, kt, j * 128:(j + 1) * 128], b.t[0]),
                                 (scond[:, kt:kt + 1], scond.t[0]), start=(kt == 0), stop=(kt == KT - 1),
                                 inc=(kt == KT - 1))
                jobs.append((ld, use))
            stream(jobs)
            b0 = PV["b_ada"][0]
            k.tt("dve", (mod[:, :], mod.t[0]), (bk[:, 0:96], bt), (pv[:, b0:b0 + 96], pv.t[0]), ALU.add)
            k.ts("dve", (mod1p[:, 0:16], mod1p.t[0]), (mod[:, 16:32], mod.t[0]), 1.0, None, ALU.add)
            k.ts("dve", (mod1p[:, 16:32], mod1p.t[0]), (mod[:, 64:80], mod.t[0]), 1.0, None, ALU.add)
            for i, nm in enumerate(("ln1_g", "ln1_b", "ln2_g", "ln2_b")):
                o0 = PV[nm][0]
                k.ts("dve", (lnp[:, i * 16:(i + 1) * 16], lnp.t[0]), (pv[:, o0:o0 + 16], pv.t[0]), ALPHA, None, ALU.mult)

        def proj_fm(wmat, c0, ncols, consume, kt0=0, ktn=KT, src=None, src_tok=None):
            jobs = []
            nchunk = ncols // 256
            for ci in range(nchunk):
                def ld(ci=ci):
                    return wload([(lambda b: wview(b, ktn, 256), wsrc(wmat, c0 + ci * 256, 256, kt0, ktn))])
                def use(b, ci=ci):
                    wv = wview(b, ktn, 256)
                    for j in range(2):
                        for hf in range(2):
                            bk = k.bank("a")
                            for kt in range(ktn):
                                rhs = src(kt, hf)
                                k.mm(bk, (wv[:, kt, j * 128:(j + 1) * 128], b.t[0]), rhs,
                                     start=(kt == 0), stop=(kt == ktn - 1))
                            consume(ci * 2 + j, hf, bk)
                jobs.append((ld, use))
            return jobs

        hsrc = lambda kt, hf: (hT[:, kt, hf * 512:(hf + 1) * 512], hT.t[kt])

        def proj_tm(l, c0, slot, post=None):
            with ExitStack() as sc:
                stg = [P("stg%d" % i, [128, 512], F32, scope=sc) for i in range(2)]
                bufs = [wload([(lambda b: wview(b, KT, 256), wsrc(w_in[l], c0 + ci * 256, 256))]) for ci in range(2)]
                for tt in range(8):
                    bk, bt = k.bank("a")
                    for ci in range(2):
                        wv = wview(bufs[ci], KT, 256)
                        for kt in range(KT):
                            k.mm((bk[:, ci * 256:(ci + 1) * 256], bt), (hT[:, kt, tt * 128:(tt + 1) * 128], hT.t[kt]),
                                 (wv[:, kt, :], bufs[ci].t[0]), start=(kt == 0), stop=(kt == KT - 1),
                                 inc=(ci == 1 and kt == KT - 1))
                    st = stg[tt % 2]
                    k.copy("act", (st[:, :], st.t[0]), (bk[:, :], bt))
                    if post is not None:
                        post(tt, (bk, bt))
                    k.dma("sp", "o_stg%d" % (tt % 2), kv_out[l, slot, tt * 128:(tt + 1) * 128, :], (st[:, :], st.t[0]))

        KMIX = os.environ.get("KMIX", "abc")
        onesf = (cst[:, 768:896], cst.t[0])
        rotT = (cst[:, 896:1024], cst.t[0])
        bch = [S.chan() for _ in range(4)]
        bcur = [0]

        def attention(sc, qT, kT, V, ncomp, bias_src, out_cb, aug=None, skip=None):
            kd = 128 // ncomp
            units = [(h, qh, c, j) for h in range(4) for qh in range(2) for c in range(ncomp) for j in range(10)
                     if not (skip is not None and skip(qh, j))]
            first, last = {}, {}
            for ui, (h, qh, c, j) in enumerate(units):
                first.setdefault((h, qh, c), ui)
                last[(h, qh, c)] = ui
            pt = [P("apt%d" % i, [128, 512], BF16, scope=sc) for i in range(3)]
            if aug is None:
                bring = [P("br%d" % i, [128, 512], F32, scope=sc) for i in range(4)]
                tmp = [P("atm%d" % i, [128, 512], F32, scope=sc) for i in range(2)]

                def bload(src):
                    i = bcur[0] % 4
                    bcur[0] += 1
                    k.dma("sp", bch[i], (bring[i][:, :], bring[i].t[0]), src)
                    return bring[i]
                srcs = [bias_src(h, qh, j) for (h, qh, c, j) in units]
                bufs = {}
                for i in range(min(3, len(units))):
                    bufs[i] = bload(srcs[i])
            n = 0

            def issue_s(u_):
                h_, qh_, c_, j_ = units[u_]
                rows_ = slice(c_ * kd, (c_ + 1) * kd)
                sb_ = k.bank("a")
                if aug is None:
                    k.mm(sb_, (kT[rows_, h_, j_ * 128:(j_ + 1) * 128], kT.t[h_]), (qT[rows_, h_, qh_ * 512:(qh_ + 1) * 512], qT.t[h_]))
                else:
                    ka, qa = aug
                    k.mm(sb_, (kT[rows_, h_, j_ * 128:(j_ + 1) * 128], kT.t[h_]), (qT[rows_, h_, qh_ * 512:(qh_ + 1) * 512], qT.t[h_]),
                         start=True, stop=False, inc=False)
                    k.mm(sb_, (ka[:, j_ * 128:(j_ + 1) * 128], ka.t[0]), (qa[:, qh_ * 512:(qh_ + 1) * 512], qa.t[0]),
                         start=False, stop=True)
                return sb_
            SPRE = 2
            sbs = {}
            for i in range(min(SPRE, len(units))):
                sbs[i] = issue_s(i)
            for ui, (h, qh, c, j) in enumerate(units):
                if first[(h, qh, 0)] == ui:
                    obanks = [k.bank("b") for _ in range(2 * ncomp)]
                sb = sbs.pop(ui)
                p_ = pt[n % 3]
                if aug is None:
                    bb = bufs.pop(ui)
                    tm = tmp[n % 2]
                    k.tt("dve", (tm[:, :], tm.t[0]), sb, (bb[:, :], bb.t[0]), ALU.add)
                    if ui + 3 < len(units):
                        bufs[ui + 3] = bload(srcs[ui + 3])
                    k.act((p_[:, :], p_.t[0]), (tm[:, :], tm.t[0]), AF.Exp)
                else:
                    k.act((p_[:, :], p_.t[0]), sb, AF.Exp)
                n += 1
                if ui + SPRE < len(units):
                    sbs[ui + SPRE] = issue_s(ui + SPRE)
                st_, sp_ = (first[(h, qh, c)] == ui), (last[(h, qh, c)] == ui)
                k.mm(obanks[2 * c], (V[:, j, h * 128:(h + 1) * 128], V.t[j]), (p_[:, :], p_.t[0]),
                     start=st_, stop=sp_, inc=True)
                k.mm(obanks[2 * c + 1], (onesb[:, :], onesb.t[0]), (p_[:, :], p_.t[0]),
                     start=st_, stop=sp_, inc=True)
                if last[(h, qh, ncomp - 1)] == ui:
                    out_cb(h, qh, obanks)

        def load_ext(sc, l, which, kT, V):
            k.dma("pool", "pool_extv", (V[:, 8:10, :], [V.t[8], V.t[9]]), ext_v[l, which].rearrange("(t p) n -> p t n", p=128))
            sc = ExitStack()
            ek = P("ek", [128, 2, 512], F32, scope=sc)
            k.dma("sp", "ek", (ek[:, :, :], ek.t[0]), ext_k[l, which].rearrange("(t p) n -> p t n", p=128))
            for t in range(2):
                bk, bt = k.bank("a")
                for h in range(4):
                    k.tr((bk[:, h * 128:(h + 1) * 128], bt), (ek[:, t, h * 128:(h + 1) * 128], ek.t[0]), ident)
                for h in range(4):
                    k.copy("dve" if h % 2 == 0 else "act", (kT[:, h, 1024 + t * 128:1024 + (t + 1) * 128], kT.t[h]),
                           (bk[:, h * 128:(h + 1) * 128], bt))
            sc.close()

        def mixer_phase(l):
            with ExitStack() as msc:
                oc_ = P("oT2", [128, 4, T], BF16, ntok=4, scope=msc)
                if "c" in KMIX:
                    rwkv_phase(l, oc_)
                else:
                    k.memset("dve", (oc_[:, :, :], oc_.t), 0.0)
                oT = [P("oT%d" % i, [128, 4, T], BF16, ntok=4, scope=msc) for i in range(2)] + [oc_]
                if "a" in KMIX:
                    with ExitStack() as sc:
                        qT = P("qT", [128, 4, T], BF16, ntok=4, scope=sc)
                        kT = P("kT", [128, 4, T + 256], BF16, ntok=4, scope=sc)
                        V = P("V", [128, 10, 512], BF16, ntok=10, scope=sc)
                        load_ext(sc, l, 0, kT, V)
                        def cq(ft, hf, bk):
                            k.act((qT[:, ft, hf * 512:(hf + 1) * 512], qT.t[ft]), bk, AF.Copy, scale=128 ** -0.5)
                        def ck(ft, hf, bk):
                            k.copy("dve", (kT[:, ft, hf * 512:(hf + 1) * 512], kT.t[ft]), bk)
                        stream(proj_fm(w_in[l], C_NAQ, 512, cq, src=hsrc) + proj_fm(w_in[l], C_NAK, 512, ck, src=hsrc))
                        proj_tm(l, C_NAK, 0)
                        proj_tm(l, C_NAV, 1, post=lambda tt, bk: k.copy("dve", (V[:, tt, :], V.t[tt]), bk))
                        def cb_a(h, qh, ob):
                            with ExitStack() as s2:
                                rc = P("rc", [128, 512], F32, scope=s2)
                                k.S.op("dve", lambda e: e.reciprocal(out=rc[:, :], in_=ob[1][0]), R=[ob[1][1]], W=[rc.t[0]])
                                k.tt("dve", (oT[0][:, h, qh * 512:(qh + 1) * 512], oT[0].t[h]), ob[0], (rc[:, :], rc.t[0]), ALU.mult)
                        attention(sc, qT, kT, V, 1, lambda h, qh, j: nab[l, h, qh, j], cb_a,
                                  skip=lambda qh, j: (qh == 0 and j in (6, 7)) or (qh == 1 and j in (0, 1)))
                else:
                    k.memset("dve", (oT[0][:, :, :], oT[0].t), 0.0)
                    proj_tm(l, C_NAK, 0); proj_tm(l, C_NAV, 1)
                if "b" in KMIX:
                    with ExitStack() as sc:
                        qT = P("dqT", [128, 4, T], BF16, ntok=4, scope=sc)
                        kT = P("dkT", [128, 4, T + 256], BF16, ntok=4, scope=sc)
                        V = P("dV", [128, 10, 512], BF16, ntok=10, scope=sc)
                        lam = P("lamt", [128, 8], F32, scope=sc)
                        sg = P("sgn", [128, 4], F32, scope=sc)
                        kaug = P("kaug", [32, T + 256], BF16, scope=sc)
                        qaug = P("qaug", [32, T], BF16, scope=sc)
                        rsc = ExitStack()
                        cs = P("cs", [128, 2, T], F32, scope=rsc)
                        xr = [P("xr%d" % i, [128, 512], F32, scope=rsc) for i in range(2)]
                        t1 = [P("t1%d" % i, [128, 512], F32, scope=rsc) for i in range(2)]
                        k.dma("sp", "cs", (cs[:, :, :], cs.t[0]), rope.rearrange("c p t -> p c t"))
                        augf = P("augf", [32, T + 256], F32, scope=rsc)
                        k.dma("sp", "augf", (augf[:, :], augf.t[0]), aug_in[0])
                        k.copy("dve", (kaug[:, :], kaug.t[0]), (augf[:, :], augf.t[0]))
                        k.dma("sp", "augf", (augf[:, 0:T], augf.t[0]), aug_in[1][:, 0:T])
                        k.copy("dve", (qaug[:, :], qaug.t[0]), (augf[:, 0:T], augf.t[0]))
                        load_ext(sc, l, 1, kT, V)
                        lo = PV["lam"][0]
                        k.tt("dve", (lam[:, 0:1], lam.t[0]), (pv[:, lo:lo + 1], pv.t[0]), (pv[:, lo + 1:lo + 2], pv.t[0]), ALU.mult)
                        k.tt("dve", (lam[:, 1:2], lam.t[0]), (pv[:, lo + 2:lo + 3], pv.t[0]), (pv[:, lo + 3:lo + 4], pv.t[0]), ALU.mult)
                        bk, bt = k.bank("a")
                        k.mm((bk[:, 0:2], bt), onesf, (lam[:, 0:2], lam.t[0]))
                        k.act((lam[:, 2:4], lam.t[0]), (bk[:, 0:2], bt), AF.Exp)
                        lam_init = 0.8 - 0.6 * float(np.exp(-0.3 * l))
                        k.tt("dve", (lam[:, 4:5], lam.t[0]), (lam[:, 3:4], lam.t[0]), (lam[:, 2:3], lam.t[0]), ALU.subtract)
                        k.ts("dve", (lam[:, 5:6], lam.t[0]), (lam[:, 4:5], lam.t[0]), -lam_init, None, ALU.add)
                        so = PV["subln"][0]
                        k.ts("dve", (lam[:, 6:7], lam.t[0]), (pv[:, so:so + 1], pv.t[0]), 0.0, None, ALU.add)
                        k.ts("dve", (sg[:, :], sg.t[0]), (pv[:, so:so + 4], pv.t[0]), 1.0 - lam_init, None, ALU.mult)
                        ri = [0]
                        def mk(dst, scale):
                            def c_(ft, hf, bk):
                                i = ri[0] % 2; ri[0] += 1
                                sl = slice(hf * 512, (hf + 1) * 512)
                                k.act((xr[i][:, :], xr[i].t[0]), bk, AF.Copy, scale=scale)
                                rb = k.bank("b")
                                k.mm(rb, rotT, (xr[i][:, :], xr[i].t[0]))
                                k.tt("dve", (t1[i][:, :], t1[i].t[0]), (xr[i][:, :], xr[i].t[0]), (cs[:, 0, sl], cs.t[0]), ALU.mult)
                                k.tt("dve", (xr[i][:, :], xr[i].t[0]), rb, (cs[:, 1, sl], cs.t[0]), ALU.mult)
                                k.tt("dve", (dst[:, ft, sl], dst.t[ft]), (t1[i][:, :], t1[i].t[0]), (xr[i][:, :], xr[i].t[0]), ALU.add)
                            return c_
                        stream(proj_fm(w_in[l], C_DFQ, 512, mk(qT, 64 ** -0.5), src=hsrc) + proj_fm(w_in[l], C_DFK, 512, mk(kT, 1.0), src=hsrc))
                        rsc.close()
                        proj_tm(l, C_DFK, 2)
                        proj_tm(l, C_DFV, 3, post=lambda tt, bk: k.copy("dve", (V[:, tt, :], V.t[tt]), bk))
                        def cb_b(h, qh, ob):
                            with ExitStack() as s2:
                                r0 = P("r0", [128, 512], F32, scope=s2)
                                r1 = P("r1", [128, 512], F32, scope=s2)
                                sq = P("sqb", [128, 512], BF16, scope=s2)
                                k.S.op("dve", lambda e: e.reciprocal(out=r0[:, :], in_=ob[1][0]), R=[ob[1][1]], W=[r0.t[0]])
                                k.S.op("dve", lambda e: e.reciprocal(out=r1[:, :], in_=ob[3][0]), R=[ob[3][1]], W=[r1.t[0]])
                                k.tt("dve", (r0[:, :], r0.t[0]), ob[0], (r0[:, :], r0.t[0]), ALU.mult)
                                k.tt("dve", (r1[:, :], r1.t[0]), ob[2], (r1[:, :], r1.t[0]), ALU.mult)
                                k.stt("dve", (r0[:, :], r0.t[0]), (r1[:, :], r1.t[0]), (lam[:, 5:6], lam.t[0]), (r0[:, :], r0.t[0]), ALU.mult, ALU.add)
                                k.act((sq[:, :], sq.t[0]), (r0[:, :], r0.t[0]), AF.Square)
                                mb = k.bank("a")
                                k.mm(mb, (onesb[:, :], onesb.t[0]), (sq[:, :], sq.t[0]))
                                k.act((r1[:, :], r1.t[0]), mb, AF.Sqrt, bias=LN_EPS, scale=1.0 / 128)
                                k.S.op("dve", lambda e: e.reciprocal(out=r1[:, :], in_=r1[:, :]), W=[r1.t[0]])
                                k.tt("dve", (r0[:, :], r0.t[0]), (r0[:, :], r0.t[0]), (r1[:, :], r1.t[0]), ALU.mult)
                                k.ts("dve", (oT[1][:, h, qh * 512:(qh + 1) * 512], oT[1].t[h]), (r0[:, :], r0.t[0]),
                                     (sg[:, h:h + 1], sg.t[0]), None, ALU.mult)
                        attention(sc, qT, kT, V, 2, None, cb_b, aug=(kaug, qaug))
                else:
                    k.memset("dve", (oT[1][:, :, :], oT[1].t), 0.0)
                    proj_tm(l, C_DFK, 2); proj_tm(l, C_DFV, 3)
                with ExitStack() as sc:
                    mT = P("mT", [128, KT, T], BF16, ntok=KT, scope=sc)
                    acc = P("acc", [128, 2, T], F32, ntok=2, scope=sc)
                    sg4 = [P("sg4%d" % i, [128, 512], F32, scope=sc) for i in range(4)]
                    jobs = []
                    for jj in range(8):
                        for i in range(3):
                            def ld(jj=jj, i=i):
                                return wload([(lambda b: wview(b, KT, 256), wsrc(w_in[l], C_GATE + i * D + jj * 256, 256))])
                            def ldb(jj=jj, i=i):
                                return wload([(lambda b: b.h[:, 0:1024].rearrange("p (k n) -> p k n", k=4),
                                               w_branch[l, i, :, jj * 256:(jj + 1) * 256].rearrange("(k p) n -> p k n", p=128))])
                            def useg(b, jj=jj, i=i):
                                wv = wview(b, KT, 256)
                                for u in range(4):
                                    j2, hf = u // 2, u % 2
                                    bk = k.bank("a")
                                    for kt in range(KT):
                                        k.mm(bk, (wv[:, kt, j2 * 128:(j2 + 1) * 128], b.t[0]), hsrc(kt, hf), start=(kt == 0), stop=(kt == KT - 1))
                                    k.act((sg4[u][:, :], sg4[u].t[0]), bk, AF.Sigmoid)
                            def useb(b, jj=jj, i=i):
                                wv = b.h[:, 0:1024].rearrange("p (k n) -> p k n", k=4)
                                for u in range(4):
                                    j2, hf = u // 2, u % 2
                                    sl = slice(hf * 512, (hf + 1) * 512)
                                    bk = k.bank("b")
                                    for k4 in range(4):
                                        k.mm(bk, (wv[:, k4, j2 * 128:(j2 + 1) * 128], b.t[0]), (oT[i][:, k4, sl], oT[i].t[k4]), start=(k4 == 0), stop=(k4 == 3))
                                    if i == 0:
                                        k.tt("dve", (acc[:, j2, sl], acc.t[j2]), (sg4[u][:, :], sg4[u].t[0]), bk, ALU.mult)
                                    else:
                                        t_ = sg4[u]
                                        k.tt("dve", (t_[:, :], t_.t[0]), (sg4[u][:, :], sg4[u].t[0]), bk, ALU.mult)
                                        if i == 1:
                                            k.tt("dve", (acc[:, j2, sl], acc.t[j2]), (acc[:, j2, sl], acc.t[j2]), (t_[:, :], t_.t[0]), ALU.add)
                                        else:
                                            k.tt("dve", (mT[:, jj * 2 + j2, sl], mT.t[jj * 2 + j2]), (acc[:, j2, sl], acc.t[j2]), (t_[:, :], t_.t[0]), ALU.add)
                            jobs.append((ld, useg))
                            jobs.append((ldb, useb))
                    stream(jobs)
                    def cres(ft, hf, bk):
                        sl = slice(hf * 512, (hf + 1) * 512)
                        k.stt("dve", (xT[:, ft, sl], xT.t[ft]), bk, (mod[:, g1c + ft:g1c + ft + 1], mod.t[0]),
                              (xT[:, ft, sl], xT.t[ft]), ALU.mult, ALU.add)
                    stream(proj_fm(w_out[l], 0, D, cres, src=lambda kt, hf: (mT[:, kt, hf * 512:(hf + 1) * 512], mT.t[kt])))

        CW = 0.6065306597126334
        bd = (cst[:, 128:256], cst.t[0])
        m_us = (cst[:, 256:384], cst.t[0]); m_ui = (cst[:, 384:512], cst.t[0])
        m_ls = (cst[:, 512:640], cst.t[0]); m_li = (cst[:, 640:768], cst.t[0])
        ch_st = S.chan()

        def rwkv_phase(l, ocT):
            pvc = lambda nm, j, n=1: (pv[:, PV[nm][0] + j:PV[nm][0] + j + n], pv.t[0])
            with ExitStack() as sc:
                tw = P("tw", [128, T], F32, scope=sc)
                adT = P("adT", [128, T], F32, scope=sc)
                sgd = P("sgd", [128, T], F32, scope=sc)
                c0 = P("c0", [128, 16], F32, scope=sc)
                kf = P("kf", [128, 1], F32, scope=sc)
                k.dma("sp", "kf", (kf[:, :], kf.t[0]), keepf)
                m0, m1 = PV["mix0"][0], PV["mix1"][0]
                k.tt("dve", (c0[:, 0:15], c0.t[0]), (pv[:, m0:m0 + 15], pv.t[0]), (pv[:, m1:m1 + 15], pv.t[0]), ALU.add)
                k.ts("dve", (c0[:, 0:15], c0.t[0]), (c0[:, 0:15], c0.t[0]), -1.0, 1.0, ALU.mult, ALU.add)

                def shifted(tiles_dst):
                    with ExitStack() as fs_:
                        Fr = P("Fr", [128, 4, 258], F32, scope=fs_)
                        k.memset("dve", (Fr[:, :, :], Fr.t[0]), 0.0)
                        _shifted(tiles_dst, Fr)

                def _shifted(tiles_dst, Fr):
                    for g0 in range(0, len(tiles_dst), 2):
                        grp = tiles_dst[g0:g0 + 2]
                        b = wload([(lambda b_, i=i: wview(b_, KT, 256)[:, :, i * 128:(i + 1) * 128],
                                    wsrc(w_in[l], C_RW + ft * 128, 128)) for i, (ft, _) in enumerate(grp)])
                        wv = wview(b, KT, 256)
                        for i, (ft, dst) in enumerate(grp):
                            for hf in range(2):
                                bk = k.bank("a")
                                for kt in range(KT):
                                    k.mm(bk, (wv[:, kt, i * 128:(i + 1) * 128], b.t[0]), hsrc(kt, hf), start=(kt == 0), stop=(kt == KT - 1))
                                k.S.op("act", lambda h, hf=hf, bk=bk: h.activation(
                                    out=Fr[:, 2 * hf:2 * hf + 2, 1:257], in_=bk[0].rearrange("p (s t) -> p s t", s=2), func=AF.Copy),
                                    R=[bk[1]], W=[Fr.t[0]])
                            k.ts("dve", (Fr[:, 1:4, 0:1], Fr.t[0]), (Fr[:, 0:3, 256:257], Fr.t[0]), (kf[:, 0:1], kf.t[0]), None, ALU.mult)
                            k.ts("dve", (Fr[:, 0:3, 257:258], Fr.t[0]), (Fr[:, 1:4, 1:2], Fr.t[0]), (kf[:, 0:1], kf.t[0]), None, ALU.mult)
                            d3 = dst[:, :].rearrange("p (s t) -> p s t", s=4)
                            k.ts("dve", (d3, dst.t[0]), (Fr[:, :, 1:257], Fr.t[0]), (c0[:, ft:ft + 1], c0.t[0]), None, ALU.mult)
                            k.stt("dve", (d3, dst.t[0]), (Fr[:, :, 0:256], Fr.t[0]), (pv[:, m0 + ft:m0 + ft + 1], pv.t[0]), (d3, dst.t[0]), ALU.mult, ALU.add)
                            k.stt("dve", (d3, dst.t[0]), (Fr[:, :, 2:258], Fr.t[0]), (pv[:, m1 + ft:m1 + ft + 1], pv.t[0]), (d3, dst.t[0]), ALU.mult, ALU.add)

                shifted([(12, tw), (13, adT)])
                shifted([(14, sgd)])
                k.act((tw[:, :], tw.t[0]), (tw[:, :], tw.t[0]), AF.Tanh)
                k.act((sgd[:, :], sgd.t[0]), (sgd[:, :], sgd.t[0]), AF.Sigmoid)

                for p in range(4):
                    with ExitStack() as ps_:
                        rT = P("rT", [128, T], F32, scope=ps_)
                        kT_ = P("kTr", [128, T], F32, scope=ps_)
                        vT = P("vT", [128, T], F32, scope=ps_)
                        kkT = P("kkT", [128, T], F32, scope=ps_)
                        yacc = P("yacc", [128, T], F32, scope=ps_)
                        Vtok = P("Vtok", [128, 8, 128], F32, ntok=8, scope=ps_)
                        w2t = P("w2t", [128, 128], F32, scope=ps_)
                        a2t = P("a2t", [128, 128], F32, scope=ps_)
                        g2t = P("g2t", [128, 128], F32, scope=ps_)
                        k.dma("sp", "w2t", (w2t[:, :], w2t.t[0]), rw_w2[l][:, p * 128:(p + 1) * 128])
                        k.dma("sp", "a2t", (a2t[:, :], a2t.t[0]), rw_a2[l][:, p * 128:(p + 1) * 128])
                        k.dma("sp", "g2t", (g2t[:, :], g2t.t[0]), rw_g2[l][:, p * 128:(p + 1) * 128])
                        shifted([(p, rT), (4 + p, kT_)])
                        shifted([(8 + p, vT)])
                        k.ts("dve", (kkT[:, :], kkT.t[0]), (kT_[:, :], kT_.t[0]), pvc("kk", p), None, ALU.mult)
                        k.act((yacc[:, :], yacc.t[0]), (kkT[:, :], kkT.t[0]), AF.Square)
                        for hf in range(2):
                            sl = slice(hf * 512, (hf + 1) * 512)
                            bk = k.bank("a")
                            k.mm(bk, bd, (yacc[:, sl], yacc.t[0]))
                            k.ts("dve", (yacc[:, sl], yacc.t[0]), bk, 1e-24, None, ALU.max)
                        k.act((yacc[:, :], yacc.t[0]), (yacc[:, :], yacc.t[0]), AF.Sqrt)
                        k.S.op("dve", lambda h: h.reciprocal(out=yacc[:, :], in_=yacc[:, :]), W=[yacc.t[0]])
                        k.tt("dve", (kkT[:, :], kkT.t[0]), (kkT[:, :], kkT.t[0]), (yacc[:, :], yacc.t[0]), ALU.mult)
                        for g in range(2):
                            bk, bt = k.bank("a")
                            for j in range(4):
                                c = g * 4 + j
                                k.tr((bk[:, j * 128:(j + 1) * 128], bt), (vT[:, c * 128:(c + 1) * 128], vT.t[0]), ident)
                            k.copy("act", (Vtok[:, g * 4:(g + 1) * 4, :], Vtok.t[g * 4:(g + 1) * 4]), (bk.rearrange("p (c f) -> p c f", c=4), bt))

                        def asig_half(d, hf, dst):
                            rows = slice(64 * d, 64 * d + 64)
                            bk = k.bank("a")
                            k.mm(bk, (a2t[rows, :], a2t.t[0]), (adT[rows, hf * 512:(hf + 1) * 512], adT.t[0]))
                            k.act(dst, bk, AF.Sigmoid, bias=pvc("a0", d * 4 + p))

                        for d in range(2):
                            with ExitStack() as ds_:
                                S0 = [P("S0%d" % i, [128, 128], F32, scope=ds_) for i in range(2)]
                                Up = [P("Up%d" % i, [128, 2, 128], F32, scope=ds_) for i in range(2)]
                                Vp = [P("Vp%d" % i, [128, 2, 128], F32, scope=ds_) for i in range(2)]
                                wful = P("wful", [128, 8], F32, scope=ds_)
                                tot = P("tot", [128, 4], F32, scope=ds_)
                                for i in range(2):
                                    k.memset("dve", (Up[i][:, :, :], Up[i].t[0]), 0.0)
                                    k.memset("dve", (Vp[i][:, :, :], Vp[i].t[0]), 0.0)
                                k.dma("sp", "S0", (S0[0][:, :], S0[0].t[0]), s0_in[l, d, p])
                                scur = 0
                                rows_d = slice(64 * d, 64 * d + 64)
                                for hf in ((0, 1) if d == 0 else (1, 0)):
                                    sl = slice(hf * 512, (hf + 1) * 512)
                                    with ExitStack() as hs_:
                                        HB = lambda nm: P(nm, [128, 512], F32, scope=hs_)
                                        btT, ktT, atT, rtT = HB("btT"), HB("ktT"), HB("atT"), HB("rtT")
                                        Tk = P("Tk", [128, 4, 3, 128], F32, ntok=4, scope=hs_)
                                        with ExitStack() as ts_:
                                            x1, x2 = P("x1", [128, 512], F32, scope=ts_), P("x2", [128, 512], F32, scope=ts_)
                                            A = lambda b_: (b_[:, :], b_.t[0])
                                            asig_half(d, hf, A(x1))
                                            k.tt("dve", A(btT), (kkT[:, sl], kkT.t[0]), A(x1), ALU.mult)
                                            k.ts("dve", A(x1), A(x1), pvc("ka", p), None, ALU.mult)
                                            k.S.op("dve", lambda h: h.tensor_scalar(out=x1[:, :], in0=x1[:, :], scalar1=pv[:, PV["ka"][0] + p:PV["ka"][0] + p + 1],
                                                                                   scalar2=1.0, op0=ALU.subtract, op1=ALU.add), R=[pv.t[0]], W=[x1.t[0]])
                                            k.tt("dve", A(ktT), (kT_[:, sl], kT_.t[0]), A(x1), ALU.mult)
                                            bk = k.bank("a")
                                            k.mm(bk, (w2t[rows_d, :], w2t.t[0]), (tw[rows_d, sl], tw.t[0]))
                                            k.act(A(x1), bk, AF.Sigmoid, bias=pvc("w0", d * 4 + p))
                                            for c in range(4):
                                                cs_ = slice(c * 128, (c + 1) * 128)
                                                k.S.op("dve", lambda h, cs_=cs_: h.tensor_tensor_scan(out=x2[:, cs_], data0=onesf[0][:, 0:128], data1=x1[:, cs_],
                                                                                                   initial=0.0, op0=ALU.mult, op1=ALU.add),
                                                       R=[x1.t[0], cst.t[0]], W=[x2.t[0]])
                                            if d == 0:
                                                k.tt("dve", A(x1), A(x2), A(x1), ALU.subtract)
                                            else:
                                                k.copy("dve", (tot[:, 0:4], tot.t[0]), (x2[:, :].rearrange("p (c t) -> p c t", c=4)[:, :, 127], x2.t[0]))
                                                for c in range(4):
                                                    cs_ = slice(c * 128, (c + 1) * 128)
                                                    k.S.op("dve", lambda h, cs_=cs_, c=c: h.tensor_scalar(out=x2[:, cs_], in0=x2[:, cs_], scalar1=tot[:, c:c + 1],
                                                                                                         scalar2=-1.0, op0=ALU.subtract, op1=ALU.mult), R=[tot.t[0]], W=[x2.t[0]])
                                                k.tt("dve", A(atT), A(x2), A(x1), ALU.add)
                                                k.copy("dve", A(x1), A(x2))
                                                k.copy("dve", A(x2), A(atT))
                                            k.act(A(atT), A(x1), AF.Exp, scale=-CW)
                                            k.stt("dve", A(atT), A(atT), -1.0, (kkT[:, sl], kkT.t[0]), ALU.mult, ALU.mult)
                                            k.act(A(rtT), A(x2), AF.Exp, scale=-CW)
                                            wcol = 127 if d == 0 else 0
                                            k.copy("dve", (wful[:, hf * 4:(hf + 1) * 4], wful.t[0]),
                                                   (rtT[:, :].rearrange("p (c t) -> p c t", c=4)[:, :, wcol], rtT.t[0]))
                                            k.tt("dve", A(rtT), A(rtT), (rT[:, sl], rT.t[0]), ALU.mult)
                                            k.act(A(x1), A(x2), AF.Exp, scale=CW)
                                            k.tt("dve", A(btT), A(btT), A(x1), ALU.mult)
                                            k.tt("dve", A(ktT), A(ktT), A(x1), ALU.mult)
                                        for c in range(4):
                                            cs_ = slice(c * 128, (c + 1) * 128)
                                            bk, bt = k.bank("a")
                                            for i, src_ in enumerate((btT, ktT, atT)):
                                                k.tr((bk[:, i * 128:(i + 1) * 128], bt), (src_[:, cs_], src_.t[0]), ident)
                                            k.copy("act", (Tk[:, c, :, :], Tk.t[c]), (bk[:, 0:384].rearrange("p (i f) -> p i f", i=3), bt))
                                        with ExitStack() as cs2:
                                            MhA = [[P("Mh%d%d" % (i, e), [128, 512], F32, scope=cs2) for e in range(2)] for i in range(2)]
                                            ZBA = [[P("ZB%d%d" % (i, e), [128, 384], F32, scope=cs2) for e in range(2)] for i in range(2)]
                                            AtT = P("AtT", [128, 128], F32, scope=cs2)
                                            tS = P("tS", [128, 128], F32, scope=cs2)
                                            mS, mI = (m_us, m_ui) if d == 0 else (m_ls, m_li)
                                            mN = m_ls if d == 0 else m_us
                                            corder = (0, 1, 2, 3) if d == 0 else (3, 2, 1, 0)
                                            for g in range(2):
                                                cpair = corder[2 * g:2 * g + 2]
                                                chains = [(ci, c, e) for ci, c in enumerate(cpair) for e in range(2)]
                                                for ci, c in enumerate(cpair):
                                                    gc = hf * 4 + c
                                                    vp = Vp[gc % 2]
                                                    k.copy("dve", (vp[:, 0, 0:64], vp.t[0]), (Vtok[:, gc, 0:64], Vtok.t[gc]))
                                                    k.copy("dve", (vp[:, 1, 64:128], vp.t[0]), (Vtok[:, gc, 64:128], Vtok.t[gc]))
                                                b1s = {}
                                                for (ci, c, e) in chains:
                                                    rw_ = slice(64 * e, 64 * e + 64)
                                                    cs_ = slice(c * 128, (c + 1) * 128)
                                                    b1 = k.bank("a")
                                                    b1s[(ci, e)] = b1
                                                    for i, (lh, rh) in enumerate(((btT, atT), (ktT, atT), (btT, rtT), (ktT, rtT))):
                                                        k.mm((b1[0][:, i * 128:(i + 1) * 128], b1[1]), (lh[rw_, cs_], lh.t[0]), (rh[rw_, cs_], rh.t[0]),
                                                             inc=(i == 3))
                                                for (ci, c, e) in chains:
                                                    M_ = MhA[ci][e]
                                                    b1 = b1s[(ci, e)]
                                                    k.tt("dve", (M_[:, 0:128], M_.t[0]), (b1[0][:, 0:128], b1[1]), mS, ALU.mult)
                                                    k.tt("dve", (M_[:, 128:256], M_.t[0]), (b1[0][:, 128:256], b1[1]), mS, ALU.mult)
                                                    k.tt("dve", (M_[:, 256:384], M_.t[0]), (b1[0][:, 256:384], b1[1]), mI, ALU.mult)
                                                    k.tt("dve", (M_[:, 384:512], M_.t[0]), (b1[0][:, 384:512], b1[1]), mI, ALU.mult)
                                                b2s = {}
                                                for (ci, c, e) in chains:
                                                    rw_ = slice(64 * e, 64 * e + 64)
                                                    cs_ = slice(c * 128, (c + 1) * 128)
                                                    gc = hf * 4 + c
                                                    M_ = MhA[ci][e]
                                                    b2 = k.bank("b")
                                                    b2s[(ci, e)] = b2
                                                    k.mm((b2[0][:, 0:128], b2[1]), (atT[rw_, cs_], atT.t[0]), (btT[rw_, cs_], btT.t[0]), inc=False)
                                                    k.mm((b2[0][:, 128:192], b2[1]), (M_[:, 128:256], M_.t[0]), (Vtok[:, gc, 64 * e:64 * e + 64], Vtok.t[gc]))
                                                for (ci, c, e) in chains:
                                                    M_, Z_ = MhA[ci][e], ZBA[ci][e]
                                                    b2 = b2s[(ci, e)]
                                                    k.tt("dve", (Z_[:, 128:256], Z_.t[0]), (b2[0][:, 0:128], b2[1]), mN, ALU.mult)
                                                    k.copy("act", (Z_[:, 64:128], Z_.t[0]), (b2[0][:, 128:192], b2[1]))
                                                    k.copy("act", (Z_[:, 0:64], Z_.t[0]), (Tk[:, c, 2, 64 * e:64 * e + 64], Tk.t[c]))
                                                    k.copy("act", (Z_[:, 256:384], Z_.t[0]), (M_[:, 0:128], M_.t[0]))
                                                for lv in range(7):
                                                    b3s = {}
                                                    for (ci, c, e) in chains:
                                                        Z_ = ZBA[ci][e]
                                                        b3 = k.bank("b")
                                                        b3s[(ci, e)] = b3
                                                        if lv < 6:
                                                            k.mm((b3[0][:, 0:256], b3[1]), (Z_[:, 256:384], Z_.t[0]), (Z_[:, 0:256], Z_.t[0]), inc=False)
                                                            k.mm((b3[0][:, 256:384], b3[1]), (Z_[:, 128:256], Z_.t[0]), (Z_[:, 256:384], Z_.t[0]))
                                                        else:
                                                            k.mm((b3[0][:, 0:128], b3[1]), (Z_[:, 256:384], Z_.t[0]), (Z_[:, 0:128], Z_.t[0]))
                                                    for (ci, c, e) in chains:
                                                        Z_ = ZBA[ci][e]
                                                        b3 = b3s[(ci, e)]
                                                        k.tt("dve", (Z_[:, 0:128], Z_.t[0]), (Z_[:, 0:128], Z_.t[0]), (b3[0][:, 0:128], b3[1]), ALU.add)
                                                        if lv < 6:
                                                            k.copy("act", (Z_[:, 128:384], Z_.t[0]), (b3[0][:, 128:384], b3[1]))
                                                for ci, c in enumerate(cpair):
                                                    gc = hf * 4 + c
                                                    cs_ = slice(c * 128, (c + 1) * 128)
                                                    vp = Vp[gc % 2]; up = Up[gc % 2]
                                                    Mh, ZB = MhA[ci], ZBA[ci]
                                                    for e in range(2):
                                                        k.copy("act", (tS[:, 64 * e:64 * e + 64], tS.t[0]), (ZB[e][:, 0:64], ZB[e].t[0]))
                                                    b4 = k.bank("a")
                                                    k.tr((b4[0][:, 0:128], b4[1]), (tS[:, :], tS.t[0]), ident)
                                                    k.copy("act", (AtT[:, :], AtT.t[0]), (b4[0][:, 0:128], b4[1]))
                                                    s0t = S0[scur % 2]; s1t = S0[(scur + 1) % 2]
                                                    b5 = k.bank("a")
                                                    k.mm((b5[0][:, 0:128], b5[1]), (AtT[:, :], AtT.t[0]), (s0t[:, :], s0t.t[0]))
                                                    for e in range(2):
                                                        k.tt("dve", (up[:, e, 64 * e:64 * e + 64], up.t[0]), (b5[0][:, 64 * e:64 * e + 64], b5[1]),
                                                             (ZB[e][:, 64:128], ZB[e].t[0]), ALU.add)
                                                    b6 = k.bank("b")
                                                    k.mm((b6[0][:, 0:128], b6[1]), (s0t[:, :], s0t.t[0]), (rtT[:, cs_], rtT.t[0]), start=True, stop=False, inc=False)
                                                    for e in range(2):
                                                        k.mm((b6[0][:, 0:128], b6[1]), (up[:, e, :], up.t[0]), (Mh[e][:, 256:384], Mh[e].t[0]), start=False, stop=False, inc=False)
                                                        k.mm((b6[0][:, 0:128], b6[1]), (vp[:, e, :], vp.t[0]), (Mh[e][:, 384:512], Mh[e].t[0]), start=False, stop=(e == 1), inc=(e == 1))
                                                    ys = (yacc[:, gc * 128:(gc + 1) * 128], yacc.t[0])
                                                    if d == 0:
                                                        k.copy("act", ys, (b6[0][:, 0:128], b6[1]))
                                                    else:
                                                        k.tt("dve", ys, ys, (b6[0][:, 0:128], b6[1]), ALU.add)
                                                    b7 = k.bank("a")
                                                    k.mm((b7[0][:, 0:128], b7[1]), (Tk[:, c, 0, :], Tk.t[c]), (up[:, 0, :], up.t[0]), start=True, stop=False, inc=False)
                                                    k.mm((b7[0][:, 0:128], b7[1]), (Tk[:, c, 0, :], Tk.t[c]), (up[:, 1, :], up.t[0]), start=False, stop=False, inc=False)
                                                    k.mm((b7[0][:, 0:128], b7[1]), (Tk[:, c, 1, :], Tk.t[c]), (Vtok[:, gc, :], Vtok.t[gc]), start=False, stop=True)
                                                    k.tt("dve", (tS[:, :], tS.t[0]), (b7[0][:, 0:128], b7[1]), (s0t[:, :], s0t.t[0]), ALU.add)
                                                    k.stt("dve", (s1t[:, :], s1t.t[0]), (tS[:, :], tS.t[0]), (wful[:, gc:gc + 1], wful.t[0]), bd, ALU.mult, ALU.mult)
                                                    scur += 1
                                                    seq_end = (gc % 2 == 1) if d == 0 else (gc % 2 == 0)
                                                    if seq_end:
                                                        k.dma("sp", "o_S0_%d" % ((scur) % 2), st_out[l, d, gc // 2, p], (s1t[:, :], s1t.t[0]))
                                                        k.ts("dve", (s1t[:, :], s1t.t[0]), (s1t[:, :], s1t.t[0]), (kf[:, 0:1], kf.t[0]), None, ALU.mult)
                        with ExitStack() as gs_:
                            HB = lambda nm: P(nm, [128, 512], F32, scope=gs_)
                            mu_, rs_, u_, ts_ = HB("gmu"), HB("grs"), HB("gu"), HB("gts")
                            A = lambda b_: (b_[:, :], b_.t[0])
                            for hf in range(2):
                                sl = slice(hf * 512, (hf + 1) * 512)
                                ya = (yacc[:, sl], yacc.t[0])
                                b1 = k.bank("a")
                                k.mm(b1, bd, ya)
                                k.act(A(u_), ya, AF.Square)
                                b2 = k.bank("a")
                                k.mm(b2, bd, A(u_))
                                k.ts("dve", A(mu_), b1, 1.0 / 64, None, ALU.mult)
                                k.tt("dve", A(rs_), A(mu_), A(mu_), ALU.mult)
                                k.stt("dve", A(rs_), b2, 1.0 / 64, A(rs_), ALU.mult, ALU.subtract)
                                k.act(A(rs_), A(rs_), AF.Sqrt, bias=GN_EPS)
                                k.S.op("dve", lambda h: h.reciprocal(out=rs_[:, :], in_=rs_[:, :]), W=[rs_.t[0]])
                                k.tt("dve", ya, ya, A(mu_), ALU.subtract)
                                k.tt("dve", ya, ya, A(rs_), ALU.mult)
                                k.S.op("dve", lambda h, sl=sl: h.tensor_scalar(out=yacc[:, sl], in0=yacc[:, sl], scalar1=pvc("lnx_g", p)[0], scalar2=pvc("lnx_b", p)[0],
                                                                            op0=ALU.mult, op1=ALU.add), R=[pv.t[0]], W=[yacc.t[0]])
                                asig_half(0, hf, A(ts_))
                                asig_half(1, hf, A(u_))
                                k.tt("dve", A(ts_), A(ts_), A(u_), ALU.add)
                                k.ts("dve", A(ts_), A(ts_), pvc("ka", p), None, ALU.mult)
                                k.S.op("dve", lambda h: h.tensor_scalar(out=ts_[:, :], in0=ts_[:, :], scalar1=pvc("ka", p)[0], scalar2=1.0, op0=ALU.subtract, op1=ALU.add),
                                       R=[pv.t[0]], W=[ts_.t[0]])
                                k.S.op("dve", lambda h: h.tensor_scalar(out=ts_[:, :], in0=ts_[:, :], scalar1=pvc("ka", p)[0], scalar2=1.0, op0=ALU.subtract, op1=ALU.add),
                                       R=[pv.t[0]], W=[ts_.t[0]])
                                k.tt("dve", A(u_), (rT[:, sl], rT.t[0]), (kT_[:, sl], kT_.t[0]), ALU.mult)
                                k.stt("dve", A(u_), A(u_), pvc("rk", p), A(ts_), ALU.mult, ALU.mult)
                                b3 = k.bank("a")
                                k.mm(b3, bd, A(u_))
                                k.tt("dve", A(u_), b3, (vT[:, sl], vT.t[0]), ALU.mult)
                                k.tt("dve", ya, ya, A(u_), ALU.add)
                                b4 = k.bank("a")
                                k.mm(b4, (g2t[:, :], g2t.t[0]), (sgd[:, sl], sgd.t[0]))
                                k.tt("dve", (ocT[:, p, sl], ocT.t[p]), ya, b4, ALU.mult)

        g1c, g2c = 32, 80
        try:
          chk(1)
          for l in range(NL):
            k.dma("sp", "pv", (pv[:, :], pv.t[0]), pvec[l])
            adaln(l)
            chk(2)
            modulate(0)
            chk(3)
            mixer_phase(l)
            chk(4)
            layernorm(0, ALPHA)
            modulate(1)
            chk(5)
            if stage >= 2:
                with ExitStack() as sc:
                    hid = [P("hid%d" % i, [128, 11, T], BF16, ntok=11, scope=sc) for i in range(2)]
                    sg = [P("sg%d" % i, [128, 512], F32, scope=sc) for i in range(2)]
                    sgi = [0]
                    for ps in range(4):
                        hb = hid[ps % 2]
                        jobs = []
                        for j in range(11):
                            ht = ps * 11 + j
                            def ld(ht=ht):
                                return wload([(lambda b: wview(b, KT, 256)[:, :, 0:128], wsrc(w_f1[l], ht * 128, 128)),
                                              (lambda b: wview(b, KT, 256)[:, :, 128:256], wsrc(w_f1[l], FH + ht * 128, 128))])
                            def use(b, j=j, hb=hb):
                                wv = wview(b, KT, 256)
                                for hf in range(2):
                                    bg = k.bank("a"); bu = k.bank("a")
                                    for kt in range(KT):
                                        k.mm(bg, (wv[:, kt, 0:128], b.t[0]), hsrc(kt, hf), start=(kt == 0), stop=(kt == KT - 1))
                                    for kt in range(KT):
                                        k.mm(bu, (wv[:, kt, 128:256], b.t[0]), hsrc(kt, hf), start=(kt == 0), stop=(kt == KT - 1))
                                    s_ = sg[sgi[0] % 2]; sgi[0] += 1
                                    k.act((s_[:, :], s_.t[0]), bg, AF.Silu)
                                    k.tt("dve", (hb[:, j, hf * 512:(hf + 1) * 512], hb.t[j]), (s_[:, :], s_.t[0]), bu, ALU.mult)
                            jobs.append((ld, use))
                        for cg in range(8):
                            def ld2(cg=cg, ps=ps):
                                return wload([(lambda b: wview(b, 11, 256), wsrc(w_f2[l], cg * 256, 256, ps * 11, 11))])
                            def use2(b, cg=cg, hb=hb):
                                wv = wview(b, 11, 256)
                                for jj in range(2):
                                    kt_o = cg * 2 + jj
                                    for hf in range(2):
                                        bk = k.bank("a")
                                        for j in range(11):
                                            k.mm(bk, (wv[:, j, jj * 128:(jj + 1) * 128], b.t[0]),
                                                 (hb[:, j, hf * 512:(hf + 1) * 512], hb.t[j]), start=(j == 0), stop=(j == 10))
                                        sl = slice(hf * 512, (hf + 1) * 512)
                                        k.stt("dve", (xT[:, kt_o, sl], xT.t[kt_o]), bk, (mod[:, g2c + kt_o:g2c + kt_o + 1], mod.t[0]),
                                              (xT[:, kt_o, sl], xT.t[kt_o]), ALU.mult, ALU.add)
                            jobs.append((ld2, use2))
                        stream(jobs)
            layernorm(1, ALPHA if l < NL - 1 else 1.0)
        except _Stop:
            pass

        with ExitStack() as sc:
            yst = [P("yst%d" % i, [128, D], F32, scope=sc) for i in range(2)]
            for tt in range(8):
                st = yst[tt % 2]
                for kq in range(4):
                    bk, bt = k.bank("a")
                    for j in range(4):
                        kt = kq * 4 + j
                        k.tr((bk[:, j * 128:(j + 1) * 128], bt), (xT[:, kt, tt * 128:(tt + 1) * 128], xT.t[kt]), ident)
                    k.copy("dve" if kq % 2 == 0 else "act", (st[:, kq * 512:(kq + 1) * 512], st.t[0]), (bk[:, :], bt))
                k.dma("sp", "o_yst%d" % (tt % 2), y_out[tt * 128:(tt + 1) * 128, :], (st[:, :], st.t[0]))
        for c_ in S.chans:
            if c_.val > 0:
                S.E["sp"].h.wait_ge(c_.sem, c_.val)
    return nc


def _fm(v):
    v = np.asarray(v, np.float32)
    n = v.shape[-1] // 128
    return np.ascontiguousarray(np.swapaxes(v.reshape(v.shape[:-1] + (n, 128)), -1, -2))


def _consts():
    c = np.zeros((128, 1024), np.float32)
    p = np.arange(128)[:, None]
    f = np.arange(128)[None, :]
    c[:, 0:128] = (p == f)
    c[:, 128:256] = ((p // 64) == (f // 64))
    c[:, 256:384] = (f > p)
    c[:, 384:512] = (f >= p)
    c[:, 512:640] = (f < p)
    c[:, 640:768] = (f <= p)
    c[:, 768:896] = 1.0
    for m in range(128):
        if m % 32 < 16:
            c[m + 16, 896 + m] = -1.0
        else:
            c[m - 16, 896 + m] = 1.0
    return c


def _pvec(inp):
    out = np.zeros((L, 128, NPV), np.float32)
    def put(nm, arr):
        o, w = PV[nm]
        out[:, :, o:o + w] = arr.reshape(L, 128, w)
    put("b_ada", _fm(inp["b_ada"].reshape(L, 6, D)).transpose(0, 2, 1, 3))
    for nm in ("ln1_g", "ln1_b", "ln2_g", "ln2_b"):
        put(nm, _fm(inp[nm]))
    put("subln", _fm(inp["diff_subln"]))
    put("mix0", _fm(inp["rwkv_mix"][:, 0]))
    put("mix1", _fm(inp["rwkv_mix"][:, 1]))
    put("a0", _fm(inp["rwkv_a0"]).transpose(0, 2, 1, 3))
    put("w0", _fm(inp["rwkv_w0"]).transpose(0, 2, 1, 3))
    put("kk", _fm(inp["rwkv_kk"]))
    put("ka", _fm(inp["rwkv_ka"]))
    put("rk", _fm(inp["rwkv_rk"].reshape(L, 512)))
    put("lnx_g", _fm(inp["rwkv_lnx_g"]))
    put("lnx_b", _fm(inp["rwkv_lnx_b"]))
    lam = np.zeros((L, 128, 4), np.float32)
    lam[:, 0:64, :] = np.swapaxes(np.asarray(inp["diff_lambda"], np.float32), 1, 2)
    put("lam", lam)
    return out


def _na_index():
    q = np.arange(T); rq, cq = q // 64, q % 64
    kk = np.arange(T); rk, ck = kk // 64, kk % 64
    row_start = np.clip(rq - 4, 0, 16 - 8)
    win_c0 = np.clip(cq - 8, 0, 64 - 16)
    inw = ((rk[None, :] >= row_start[:, None]) & (rk[None, :] < row_start[:, None] + 8) &
           (ck[None, :] >= win_c0[:, None]) & (ck[None, :] < win_c0[:, None] + 16))
    rr = np.clip(rk[None, :] - rq[:, None] + 7, 0, 14)
    rc = np.clip(ck[None, :] - cq[:, None], -15, 15) + 15
    return inw, rr, rc


def _tiles(bias_qk):
    b = np.swapaxes(bias_qk, -1, -2)
    sh = b.shape[:-2]
    b = b.reshape(sh + (10, 128, 2, 512))
    return np.ascontiguousarray(np.moveaxis(b, -2, -4))


def _ctx_mask():
    m = np.full((T, T + 256), NEG, np.float32)
    for s_ in range(4):
        m[s_ * 256:(s_ + 1) * 256, s_ * 256:(s_ + 1) * 256] = 0.0
    return m


def _rope_tables():
    t = np.arange(T)
    rows = (t // 64).astype(np.float32); cols = (t % 64).astype(np.float32)
    half = 32
    inv = (10000.0 ** (-np.arange(0, half, 2, dtype=np.float32) / half)).astype(np.float32)
    ang_r = rows[:, None] * inv; ang_c = cols[:, None] * inv
    ang = np.concatenate([ang_r, ang_r, ang_c, ang_c], -1)
    ang = np.concatenate([ang, ang], -1).T
    return np.stack([np.cos(ang), np.sin(ang)], 0).astype(np.float32)


_CACHE = {}


def kernel(**inp):
    inp = {k_: np.asarray(v) for k_, v in inp.items()}
    if "nc" not in _CACHE:
        nc = bass.Bass("TRN2", target_bir_lowering=False)
        build(nc)
        _CACHE["nc"] = nc
    nc = _CACHE["nc"]
    consts = _consts()
    pvec = _pvec(inp)
    shared = {"consts": consts, "pvec": pvec[:NL]}
    for nm in ("w_ada", "w_in", "w_branch", "w_out", "w_ffn_in", "w_ffn_out"):
        shared[nm] = np.ascontiguousarray(inp[nm][:NL], dtype=np.float32)
    inw, rr, rc = _na_index()
    rpb = np.asarray(inp["na_rpb"], np.float32)[:NL]
    lat_bias = np.zeros((NL, 4, T, T + 256), np.float32)
    lat_bias[:, :, :, :T] = np.where(inw[None, None], rpb[:, :, rr, rc], np.float32(NEG))
    nab_lat = _tiles(lat_bias)
    cm = _ctx_mask()
    nab_ctx = np.ascontiguousarray(np.broadcast_to(_tiles(cm)[None, None], (NL, 4, 2, 10, 128, 512)))
    BIG = 29952.0
    aug_lat = np.zeros((2, 32, T + 256), np.float32)
    aug_ctx = np.zeros((2, 32, T + 256), np.float32)
    for s_ in range(4):
        aug_ctx[0, s_, s_ * 256:(s_ + 1) * 256] = 1.0
        aug_ctx[1, s_, s_ * 256:(s_ + 1) * 256] = BIG
    aug_ctx[0, 4, :] = 1.0
    aug_ctx[1, 4, :T] = -BIG
    rope_lat = _rope_tables()
    rope_ctx = np.stack([np.ones((128, T), np.float32), np.zeros((128, T), np.float32)], 0)
    zkv = np.zeros((NL, 2, 256, 512), np.float32)
    shared["rw_w2"] = np.ascontiguousarray(inp["rwkv_w2"][:NL].reshape(NL, 128, 512), dtype=np.float32)
    shared["rw_a2"] = np.ascontiguousarray(inp["rwkv_a2"][:NL].reshape(NL, 128, 512), dtype=np.float32)
    shared["rw_g2"] = np.ascontiguousarray(inp["rwkv_g2"][:NL], dtype=np.float32)
    zs0 = np.zeros((NL, 2, 4, 128, 128), np.float32)
    in_maps = []
    for c in range(8):
        m = dict(shared)
        if c < 4:
            m["x_in"] = np.ascontiguousarray(inp["x_sample"][c])
            m["cond"] = _fm(inp["c"][c])
            m["nab"] = nab_lat; m["aug"] = aug_lat; m["rope"] = rope_lat
            m["ext_k"] = np.ascontiguousarray(np.stack([inp["cache_na_k"][c, :NL].reshape(NL, 256, 512),
                                                         inp["cache_diff_k"][c, :NL].reshape(NL, 256, 512)], 1))
            m["ext_v"] = np.ascontiguousarray(np.stack([inp["cache_na_v"][c, :NL].reshape(NL, 256, 512),
                                                         inp["cache_diff_v"][c, :NL].reshape(NL, 256, 512)], 1))
            s0 = np.zeros((NL, 2, 4, 128, 128), np.float32)
            for d_, nm in enumerate(("state_rwkv_fwd", "state_rwkv_bwd")):
                st = np.asarray(inp[nm][c, :NL], np.float32)
                for h in range(8):
                    e = h % 2
                    s0[:, d_, h // 2, 64 * e:64 * e + 64, 64 * e:64 * e + 64] = np.swapaxes(st[:, h], -1, -2)
            m["s0"] = s0
            m["keepf"] = np.ones((128, 1), np.float32)
        else:
            j = c - 4
            m["x_in"] = np.ascontiguousarray(inp["x_prompt"][4 * j:4 * j + 4].reshape(T, D))
            m["cond"] = _fm(inp["c_ctx"])
            m["nab"] = nab_ctx; m["aug"] = aug_ctx; m["rope"] = rope_ctx
            m["ext_k"] = zkv; m["ext_v"] = zkv
            m["s0"] = zs0
            m["keepf"] = np.zeros((128, 1), np.float32)
        in_maps.append(m)
    if os.environ.get("KSIM"):
        return nc, in_maps
    res = run_bass_kernel_spmd(nc, in_maps, core_ids=list(range(8)))
    R = res.results
    y_sample = np.stack([R[c]["y"] for c in range(4)], 0)
    y_prompt = np.concatenate([R[c]["y"].reshape(4, 256, D) for c in range(4, 8)], 0)
    kv = np.concatenate([R[c]["kv"].reshape(NL, 4, 4, 256, 512) for c in range(4, 8)], 2)
    kv = kv.transpose(1, 2, 0, 3, 4)
    new_na_k = np.ascontiguousarray(kv[0]).reshape(16, NL, 256, 4, 128)
    new_na_v = np.ascontiguousarray(kv[1]).reshape(16, NL, 256, 4, 128)
    new_diff_k = np.ascontiguousarray(kv[2]).reshape(16, NL, 256, 4, 2, 64)
    new_diff_v = np.ascontiguousarray(kv[3]).reshape(16, NL, 256, 4, 128)
    st = np.concatenate([R[c]["st"] for c in range(4, 8)], 2)
    fin = np.zeros((2, 16, NL, 8, 64, 64), np.float32)
    for h in range(8):
        e = h % 2
        blk = st[:, :, :, h // 2, 64 * e:64 * e + 64, 64 * e:64 * e + 64]
        fin[:, :, :, h] = np.transpose(blk, (1, 2, 0, 4, 3))
    return (y_prompt, y_sample, new_na_k, new_na_v, new_diff_k, new_diff_v, fin[0], fin[1])
```
